# Optimizing a Trainium2 kernel written in Bass

```python
import jax
import jax.numpy as jnp
from jax import lax
import numpy as np


D_MODEL = 1024
BATCH = 4
SEQ = 4096
DEPTH = 2

GRID_W = 64
CTX_LEN = 256
HEAD_DIM = 128
N_Q_HEADS = 8
N_KV_HEADS = 2
Q_PER_KV = N_Q_HEADS // N_KV_HEADS
ATTN_WIDTH = N_Q_HEADS * HEAD_DIM
KV_WIDTH = N_KV_HEADS * HEAD_DIM
WINDOW = 128
BLOCK = 128
BAND = BLOCK + 2 * WINDOW
N_FREQ = HEAD_DIM // 4
ROPE_BASE = 10000.0
D_RNN = D_MODEL
N_RNN_BLOCKS = 8
RNN_BLOCK_W = D_RNN // N_RNN_BLOCKS
CONV_W = 4
CONV_LEFT = 2
LRU_C = 8.0
N_BRANCH = 2
D_FF = ((8 * D_MODEL + 3 * 256 - 1) // (3 * 256)) * 256
IN_SIZES = (D_RNN, D_RNN, ATTN_WIDTH, KV_WIDTH, KV_WIDTH, N_BRANCH * D_MODEL)
IN_WIDTH = sum(IN_SIZES)
MOD_CHUNKS = 6
EPS = 1e-6
NEG_INF = -1e30

kernel_name = 'hybrid_rglru_swa_diffusion_block'


def _rms_norm(x, g):
    xf = x.astype(jnp.float32)
    y = xf * lax.rsqrt(jnp.mean(xf * xf, axis=-1, keepdims=True) + EPS)
    return y.astype(x.dtype) * g


def _split_in(p):
    offs = np.cumsum(IN_SIZES)[:-1].tolist()
    return jnp.split(p, offs, axis=-1)


def _rope_tables(seq_len):
    rows = seq_len // GRID_W
    row = jnp.repeat(jnp.arange(rows, dtype=jnp.int32), GRID_W)
    col = jnp.tile(jnp.arange(GRID_W, dtype=jnp.int32), rows)
    inv = ROPE_BASE ** (-jnp.arange(N_FREQ, dtype=jnp.float32) / N_FREQ)
    ang_r = row.astype(jnp.float32)[:, None] * inv[None, :]
    ang_c = col.astype(jnp.float32)[:, None] * inv[None, :]
    return (jnp.cos(ang_r), jnp.sin(ang_r), jnp.cos(ang_c), jnp.sin(ang_c))


def _rotate(x, cos, sin):
    x1, x2 = jnp.split(x, 2, axis=-1)
    cos = cos[None, :, None, :].astype(x.dtype)
    sin = sin[None, :, None, :].astype(x.dtype)
    return jnp.concatenate([x1 * cos - x2 * sin, x2 * cos + x1 * sin], axis=-1)


def _rope_2d(x, rope):
    cr, sr, cc, sc = rope
    xr, xc = jnp.split(x, 2, axis=-1)
    return jnp.concatenate([_rotate(xr, cr, sr), _rotate(xc, cc, sc)], axis=-1)


def _dwconv_centred(x, w, b):
    s = x.shape[1]
    xp = jnp.pad(x, ((0, 0), (CONV_LEFT, CONV_W - 1 - CONV_LEFT), (0, 0)))
    y = xp[:, 0:s] * w[0]
    for k in range(1, CONV_W):
        y = y + xp[:, k:k + s] * w[k]
    return y + b


def _block_diag(x, w, b):
    xb = x.reshape(x.shape[:-1] + (N_RNN_BLOCKS, RNN_BLOCK_W))
    y = jnp.einsum('bsni,nij->bsnj', xb, w)
    return y.reshape(x.shape) + b


def _lru_coeffs(x, wa, ba, wx, bx, lam):
    r = jax.nn.sigmoid(_block_diag(x, wa, ba)).astype(jnp.float32)
    i = jax.nn.sigmoid(_block_diag(x, wx, bx))
    log_a = LRU_C * r * jax.nn.log_sigmoid(lam.astype(jnp.float32))
    a = jnp.exp(log_a)
    b = jnp.sqrt(-jnp.expm1(2.0 * log_a)) * (i * x).astype(jnp.float32)
    return a, b


def _linear_scan(a, b, h0, reverse):
    def combine(l, r):
        return l[0] * r[0], r[0] * l[1] + r[1]
    a_cum, h = lax.associative_scan(combine, (a, b), axis=1, reverse=reverse)
    if h0 is None:
        return h
    return h + a_cum * h0[:, None, :]


def _rglru_branch(x_lat, x_ctx, conv_w, conv_b, wa, ba, wx, bx, lam, need_ctx):
    xl = _dwconv_centred(x_lat, conv_w, conv_b)
    xc = _dwconv_centred(x_ctx, conv_w, conv_b)
    ac_f, bc_f = _lru_coeffs(xc, wa[0], ba[0], wx[0], bx[0], lam[0])
    ac_b, bc_b = _lru_coeffs(xc, wa[1], ba[1], wx[1], bx[1], lam[1])
    hc_f = _linear_scan(ac_f, bc_f, None, False)
    hc_b = _linear_scan(ac_b, bc_b, None, True)
    al_f, bl_f = _lru_coeffs(xl, wa[0], ba[0], wx[0], bx[0], lam[0])
    al_b, bl_b = _lru_coeffs(xl, wa[1], ba[1], wx[1], bx[1], lam[1])
    hl = (_linear_scan(al_f, bl_f, hc_f[:, -1], False)
          + _linear_scan(al_b, bl_b, hc_b[:, 0], True))
    y_lat = hl.astype(x_lat.dtype)
    y_ctx = (hc_f + hc_b).astype(x_ctx.dtype) if need_ctx else None
    return y_lat, y_ctx


def _band(t, nb):
    n_side = WINDOW // BLOCK
    tp = jnp.pad(t, ((0, 0), (WINDOW, WINDOW), (0, 0), (0, 0)))
    tp = tp.reshape(t.shape[0], nb + 2 * n_side, BLOCK, t.shape[2], t.shape[3])
    return jnp.concatenate([tp[:, j:j + nb] for j in range(2 * n_side + 1)], axis=2)


def _latent_attention(q, k, v, kc, vc, sink):
    bsz, s = q.shape[0], q.shape[1]
    nb = s // BLOCK
    n_ctx = kc.shape[1]
    scale = HEAD_DIM ** -0.5
    qb = q.reshape(bsz, nb, BLOCK, N_KV_HEADS, Q_PER_KV, HEAD_DIM)
    kb = _band(k, nb)
    vb = _band(v, nb)
    s_band = jnp.einsum('bnqkgd,bnpkd->bnkgqp', qb, kb).astype(jnp.float32) * scale
    q_pos = jnp.arange(nb)[:, None] * BLOCK + jnp.arange(BLOCK)[None, :]
    k_pos = jnp.arange(nb)[:, None] * BLOCK - WINDOW + jnp.arange(BAND)[None, :]
    kp = k_pos[:, None, :]
    valid = (jnp.abs(kp - q_pos[:, :, None]) <= WINDOW) & (kp >= 0) & (kp < s)
    s_band = jnp.where(valid[None, :, None, None], s_band, NEG_INF)
    s_ctx = jnp.einsum('bnqkgd,bckd->bnkgqc', qb, kc).astype(jnp.float32) * scale
    s_sink = jnp.broadcast_to(
        sink.astype(jnp.float32).reshape(1, 1, N_KV_HEADS, Q_PER_KV, 1, 1),
        s_band.shape[:-1] + (1,))
    p = jax.nn.softmax(jnp.concatenate([s_band, s_ctx, s_sink], axis=-1), axis=-1)
    p_band = p[..., :BAND].astype(v.dtype)
    p_ctx = p[..., BAND:BAND + n_ctx].astype(v.dtype)
    o = (jnp.einsum('bnkgqp,bnpkd->bnqkgd', p_band, vb)
         + jnp.einsum('bnkgqc,bckd->bnqkgd', p_ctx, vc))
    return o.reshape(bsz, s, ATTN_WIDTH)


def _context_attention(qc, kc, vc, sink):
    bsz, n_ctx = qc.shape[0], qc.shape[1]
    scale = HEAD_DIM ** -0.5
    qg = qc.reshape(bsz, n_ctx, N_KV_HEADS, Q_PER_KV, HEAD_DIM)
    sc = jnp.einsum('bqkgd,bckd->bkgqc', qg, kc).astype(jnp.float32) * scale
    s_sink = jnp.broadcast_to(
        sink.astype(jnp.float32).reshape(1, N_KV_HEADS, Q_PER_KV, 1, 1),
        sc.shape[:-1] + (1,))
    p = jax.nn.softmax(jnp.concatenate([sc, s_sink], axis=-1), axis=-1)
    o = jnp.einsum('bkgqc,bckd->bqkgd', p[..., :n_ctx].astype(vc.dtype), vc)
    return o.reshape(bsz, n_ctx, ATTN_WIDTH)


def _merge(y_rnn, g_rnn, y_attn, gate_logits, w_o_rnn, w_o_attn, w_out):
    ya = (y_rnn * jax.nn.gelu(g_rnn)) @ w_o_rnn
    yb = y_attn @ w_o_attn
    ga, gb = jnp.split(jax.nn.sigmoid(gate_logits), N_BRANCH, axis=-1)
    return (ga * ya + gb * yb) @ w_out


def _mixer(h, hc, rope, w_in, conv_w, conv_b, wa, ba, wx, bx, lam, sink,
           w_o_rnn, w_o_attn, w_out, need_ctx):
    bsz, s = h.shape[0], h.shape[1]
    n_ctx = hc.shape[1]
    xr, gr, q, k, v, gl = _split_in(h @ w_in)
    xrc, grc, qc, kc, vc, glc = _split_in(hc @ w_in)
    y_rnn, y_rnn_c = _rglru_branch(xr, xrc, conv_w, conv_b, wa, ba, wx, bx, lam, need_ctx)
    q = _rope_2d(q.reshape(bsz, s, N_Q_HEADS, HEAD_DIM), rope)
    k = _rope_2d(k.reshape(bsz, s, N_KV_HEADS, HEAD_DIM), rope)
    v = v.reshape(bsz, s, N_KV_HEADS, HEAD_DIM)
    kc = kc.reshape(bsz, n_ctx, N_KV_HEADS, HEAD_DIM)
    vc = vc.reshape(bsz, n_ctx, N_KV_HEADS, HEAD_DIM)
    o = _latent_attention(q, k, v, kc, vc, sink)
    out = _merge(y_rnn, gr, o, gl, w_o_rnn, w_o_attn, w_out)
    out_c = None
    if need_ctx:
        oc = _context_attention(qc.reshape(bsz, n_ctx, N_Q_HEADS, HEAD_DIM), kc, vc, sink)
        out_c = _merge(y_rnn_c, grc, oc, glc, w_o_rnn, w_o_attn, w_out)
    return out, out_c


def _swiglu(h, w_ffn_in, w_ffn_out):
    gate, up = jnp.split(h @ w_ffn_in, 2, axis=-1)
    return (jax.nn.silu(gate) * up) @ w_ffn_out


def setup_inputs(seed: int = 0) -> dict:
    key = jax.random.key(seed)
    ks = jax.random.split(key, 24)
    f32 = jnp.float32

    def nrm(k, shape, scale):
        return jax.random.normal(k, shape, f32) * scale

    L = DEPTH
    a0 = jax.random.uniform(ks[15], (L, 2, D_RNN), f32, 0.9, 0.999)
    return {
        'x': nrm(ks[0], (BATCH, SEQ, D_MODEL), 1.0),
        'c': nrm(ks[1], (BATCH, D_MODEL), 1.0),
        'ctx': nrm(ks[2], (BATCH, CTX_LEN, D_MODEL), 1.0),
        'c_ctx': nrm(ks[3], (D_MODEL,), 1.0),
        'w_mod': nrm(ks[4], (L, D_MODEL, MOD_CHUNKS * D_MODEL), 0.5 * D_MODEL ** -0.5),
        'b_mod': nrm(ks[5], (L, MOD_CHUNKS * D_MODEL), 0.02),
        'g_mix_pre': 1.0 + nrm(ks[6], (L, D_MODEL), 0.02),
        'g_mix_post': 1.0 + nrm(ks[7], (L, D_MODEL), 0.02),
        'g_ffn_pre': 1.0 + nrm(ks[8], (L, D_MODEL), 0.02),
        'g_ffn_post': 1.0 + nrm(ks[9], (L, D_MODEL), 0.02),
        'w_in': nrm(ks[10], (L, D_MODEL, IN_WIDTH), D_MODEL ** -0.5),
        'conv_w': nrm(ks[11], (L, CONV_W, D_RNN), CONV_W ** -0.5),
        'conv_b': nrm(ks[12], (L, D_RNN), 0.02),
        'lru_wa': nrm(ks[13], (L, 2, N_RNN_BLOCKS, RNN_BLOCK_W, RNN_BLOCK_W), RNN_BLOCK_W ** -0.5),
        'lru_ba': nrm(ks[14], (L, 2, D_RNN), 0.02),
        'lru_wx': nrm(ks[16], (L, 2, N_RNN_BLOCKS, RNN_BLOCK_W, RNN_BLOCK_W), RNN_BLOCK_W ** -0.5),
        'lru_bx': nrm(ks[17], (L, 2, D_RNN), 0.02),
        'lru_lam': jnp.log(a0) - jnp.log1p(-a0),
        'attn_sink': nrm(ks[18], (L, N_Q_HEADS), 0.5),
        'w_o_rnn': nrm(ks[19], (L, D_RNN, D_MODEL), D_RNN ** -0.5),
        'w_o_attn': nrm(ks[20], (L, ATTN_WIDTH, D_MODEL), ATTN_WIDTH ** -0.5),
        'w_out': nrm(ks[21], (L, D_MODEL, D_MODEL), D_MODEL ** -0.5),
        'w_ffn_in': nrm(ks[22], (L, D_MODEL, 2 * D_FF), D_MODEL ** -0.5),
        'w_ffn_out': nrm(ks[23], (L, D_FF, D_MODEL), D_FF ** -0.5),
    }


def reference(x, c, ctx, c_ctx, w_mod, b_mod, g_mix_pre, g_mix_post, g_ffn_pre,
              g_ffn_post, w_in, conv_w, conv_b, lru_wa, lru_ba, lru_wx, lru_bx,
              lru_lam, attn_sink, w_o_rnn, w_o_attn, w_out, w_ffn_in, w_ffn_out):
    rope = _rope_tables(x.shape[1])
    for l in range(DEPTH):
        need_ctx = l < DEPTH - 1
        mod = jax.nn.silu(c) @ w_mod[l] + b_mod[l]
        sh1, sc1, ga1, sh2, sc2, ga2 = jnp.split(mod[:, None, :], MOD_CHUNKS, axis=-1)
        mod_c = jax.nn.silu(c_ctx) @ w_mod[l] + b_mod[l]
        sh1c, sc1c, ga1c, sh2c, sc2c, ga2c = jnp.split(mod_c, MOD_CHUNKS, axis=-1)

        h = _rms_norm(x, g_mix_pre[l]) * (1.0 + sc1) + sh1
        hc = _rms_norm(ctx, g_mix_pre[l]) * (1.0 + sc1c) + sh1c
        m, mc = _mixer(h, hc, rope, w_in[l], conv_w[l], conv_b[l], lru_wa[l], lru_ba[l],
                       lru_wx[l], lru_bx[l], lru_lam[l], attn_sink[l], w_o_rnn[l],
                       w_o_attn[l], w_out[l], need_ctx)
        x = x + ga1 * _rms_norm(m, g_mix_post[l])
        if need_ctx:
            ctx = ctx + ga1c * _rms_norm(mc, g_mix_post[l])

        h = _rms_norm(x, g_ffn_pre[l]) * (1.0 + sc2) + sh2
        x = x + ga2 * _rms_norm(_swiglu(h, w_ffn_in[l], w_ffn_out[l]), g_ffn_post[l])
        if need_ctx:
            hc = _rms_norm(ctx, g_ffn_pre[l]) * (1.0 + sc2c) + sh2c
            ctx = ctx + ga2c * _rms_norm(_swiglu(hc, w_ffn_in[l], w_ffn_out[l]), g_ffn_post[l])
    return x
```

```python
import numpy as np
from contextlib import ExitStack
import concourse.bass as bass
import concourse.mybir as mybir
from concourse.bass_utils import run_bass_kernel_spmd

F32 = mybir.dt.float32
BF16 = mybir.dt.bfloat16
AF = mybir.ActivationFunctionType
ALU = mybir.AluOpType

D = 1024
KC = 8
NCTX = 256
NLAT = 2048
NLAT_FULL = 4096
NT = NCTX + NLAT
DFF = 2816
FC = 22
INW = 5632
L = 2
EPS = 1e-6
NEG = -30000.0
TILES = [(0, 256)] + [(256 + 512 * i, 512) for i in range(4)]
NBLK = NT // 128
PW = NT + 8
LP0 = NCTX + 4


class Region:
    __slots__ = ("w", "rs", "excl")

    def __init__(self, excl=False):
        self.w = None
        self.rs = []
        self.excl = excl


class Sched:
    def __init__(self, nc, n_dma_sems=12):
        self.nc = nc
        self.engs = {}
        for name, h in [("pe", nc.tensor), ("act", nc.scalar), ("dve", nc.vector),
                        ("pool", nc.gpsimd), ("sp", nc.sync)]:
            sem = nc.alloc_semaphore(name="s_" + name)
            self.engs[name] = dict(h=h, sem=sem, cnt=0, known={}, key="e_" + name, name=name)
        self.dq = {}
        for q in ("sp", "pool"):
            self.dq[q] = dict(slots=[dict(sem=nc.alloc_semaphore(name=f"d_{q}{i}"), tot=0, key=f"d_{q}{i}")
                                     for i in range(n_dma_sems)], nxt=0)

    def _collect(self, reads, writes):
        deps = {}

        def add(tok):
            if tok is None:
                return
            key, sem, val = tok
            if key not in deps or deps[key][1] < val:
                deps[key] = (sem, val)
        for r in reads:
            add(r.w)
        for w in writes:
            add(w.w)
            for t in w.rs:
                add(t)
        return deps

    def _wait(self, E, deps):
        for key, (sem, val) in deps.items():
            if key == E["key"] and E["name"] == "pe":
                continue
            if E["known"].get(key, 0) >= val:
                continue
            E["h"].wait_ge(sem, val)
            E["known"][key] = val

    def _commit(self, tok, reads, writes):
        for r in reads:
            r.rs.append(tok)
            if len(r.rs) > 48:
                best = {}
                for t in r.rs:
                    if t[0] not in best or best[t[0]][2] < t[2]:
                        best[t[0]] = t
                r.rs = list(best.values())
        for w in writes:
            w.w = tok
            w.rs = []

    def op(self, eng, fn, reads=(), writes=()):
        if any(r.excl for r in reads):
            writes = list(writes) + [r for r in reads if r.excl]
            reads = [r for r in reads if not r.excl]
        E = self.engs[eng]
        self._wait(E, self._collect(reads, writes))
        inst = fn(E["h"])
        E["cnt"] += 1
        inst.then_inc(E["sem"], 1)
        self._commit((E["key"], E["sem"], E["cnt"]), reads, writes)

    def dma(self, q, out, in_, reads=(), writes=()):
        E = self.engs[q]
        Dq = self.dq[q]
        slot = Dq["slots"][Dq["nxt"]]
        Dq["nxt"] = (Dq["nxt"] + 1) % len(Dq["slots"])
        deps = self._collect(reads, writes)
        if slot["tot"] > 0:
            k = slot["key"]
            if k not in deps or deps[k][1] < slot["tot"]:
                deps[k] = (slot["sem"], slot["tot"])
        self._wait(E, deps)
        inst = E["h"].dma_start(out=out, in_=in_)
        slot["tot"] += 16
        inst.then_inc(slot["sem"], 16)
        self._commit((slot["key"], slot["sem"], slot["tot"]), reads, writes)

    def collective(self, in_ap, out_ap, groups, reads=(), writes=()):
        E = self.engs["pool"]
        if not hasattr(self, "cc"):
            self.cc = dict(sem=self.nc.alloc_semaphore(name="s_cc"), tot=0, key="cc")
        cc = self.cc
        deps = self._collect(reads, writes)
        if cc["tot"] > 0:
            deps["cc"] = (cc["sem"], cc["tot"])
        self._wait(E, deps)
        inst = E["h"].collective_compute("AllGather", ALU.bypass, replica_groups=groups,
                                         ins=[in_ap.opt()], outs=[out_ap.opt()])
        cc["tot"] += 1
        inst.then_inc(cc["sem"], 1)
        self._commit(("cc", cc["sem"], cc["tot"]), reads, writes)

    def barrier(self, engines=("pe", "act", "dve", "pool", "sp")):
        deps = {}
        for name, e in self.engs.items():
            if e["cnt"] > 0:
                deps[e["key"]] = (e["sem"], e["cnt"])
        for q, Dq in self.dq.items():
            for s in Dq["slots"]:
                if s["tot"] > 0:
                    deps[s["key"]] = (s["sem"], s["tot"])
        if hasattr(self, "cc") and self.cc["tot"] > 0:
            deps["cc"] = (self.cc["sem"], self.cc["tot"])
        for name in engines:
            E = self.engs[name]
            for key, (sem, val) in deps.items():
                if key == E["key"]:
                    continue
                if E["known"].get(key, 0) >= val:
                    continue
                E["h"].wait_ge(sem, val)
                E["known"][key] = val


class _Stop(Exception):
    pass


def build(debug=False, stop=None):
    holder = {}
    try:
        return _build(debug, stop, holder)
    except _Stop:
        return holder['nc']


def _build(debug, stop, holder):
    nc = bass.Bass("TRN2", target_bir_lowering=False)
    holder['nc'] = nc

    def din(name, shape, dt=F32):
        return nc.dram_tensor(name, list(shape), dt, kind="ExternalInput").ap()

    def dscr(name, shape, dt):
        dbg = debug and name in ("Ud", "Od", "Md", "XB")
        return nc.dram_tensor(name, list(shape), dt, kind="ExternalOutput" if dbg else "Internal").ap()

    xin = din("xin", [KC, 128, NT])
    cvec = din("cvec", [128, KC, 2])
    w_mod = din("w_mod", [L, D, 6 * D])
    bmod = din("bmod", [L, 128, 48])
    gvec = din("gvec", [L, 128, 4, KC])
    w_in = din("w_in", [L, D, INW])
    convw = din("convw", [L, 128, 5, KC])
    flags = din("flags", [128, 2])
    convb = din("convb", [L, 128, KC])
    lru_wa = din("lru_wa", [L, 2, KC, 128, 128])
    lru_wx = din("lru_wx", [L, 2, KC, 128, 128])
    lruv = din("lruv", [L, 128, 3, 2, KC])
    sinkbc = din("sinkbc", [L, 128, KC])
    w_o_rnn = din("w_o_rnn", [L, D, D])
    w_o_attn = din("w_o_attn", [L, D, D])
    w_out = din("w_out", [L, D, D])
    w_ffn_in = din("w_ffn_in", [L, D, 2 * DFF])
    w_ffn_out = din("w_ffn_out", [L, DFF, D])
    ropeC = din("ropeC", [128, NT])
    ropeS = din("ropeS", [128, NT])
    consts = din("consts", [128, 3, 128])
    maskin = din("maskin", [128, 3, 512])
    out = nc.dram_tensor("out", [KC, 128, NLAT], F32, kind="ExternalOutput").ap()

    XA = dscr("XA", [KC, 128, NT], F32)
    XB = dscr("XB", [KC, 128, NT], F32)
    XRd = dscr("XRd", [KC, 128, NT], F32)
    Gd = dscr("Gd", [KC, 128, NT], BF16)
    Qd = dscr("Qd", [KC, 128, NT], BF16)
    GABd = dscr("GABd", [16, 128, NT], BF16)
    Ud = dscr("Ud", [KC, 128, NT], BF16)
    Od = dscr("Od", [KC, 128, NT], BF16)
    Md = dscr("Md", [KC, 128, NT], F32)
    ACTd = dscr("ACTd", [FC, 128, NT], BF16)
    Yd = dscr("Yd", [KC, 128, NT], F32)
    XLd = dscr("XLd", [KC, 128, NT], F32)
    HFd = dscr("HFd", [KC, 128, NT], F32)

    def dint(name, shape, dt):
        return nc.dram_tensor(name, list(shape), dt, kind="Internal").ap()
    EXx = [dint(f"EXx{l}", [128, 16], F32) for l in range(L)]
    GXx = [dint(f"GXx{l}", [256, 16], F32) for l in range(L)]
    EXk = [dint(f"EXk{l}", [128, 256], BF16) for l in range(L)]
    GXk = [dint(f"GXk{l}", [256, 256], BF16) for l in range(L)]
    EXv = [dint(f"EXv{l}", [128, 256], BF16) for l in range(L)]
    GXv = [dint(f"GXv{l}", [256, 256], BF16) for l in range(L)]
    EXs = [dint(f"EXs{l}", [128, 8], F32) for l in range(L)]
    GXs = [dint(f"GXs{l}", [256, 8], F32) for l in range(L)]
    PAIRS = [[0, 4], [1, 5], [2, 6], [3, 7]]

    top = ExitStack()
    with top:
        S = Sched(nc)
        nc._S = S

        nc._marks = []

        def checkpoint(name):
            nc._marks.append((name, S.engs["pe"]["cnt"]))
            if stop == name:
                S.barrier()
                raise _Stop()

        ncnt = dict(i=0)

        def sbt(es, name, shape, dt):
            ncnt["i"] += 1
            return es.enter_context(nc.sbuf_tensor(f"{name}_{ncnt['i']}", list(shape), dt))

        psb = [top.enter_context(nc.psum_tensor(f"ps{i}", [128, 512], F32)) for i in range(8)]
        psr = [Region(excl=True) for _ in range(8)]
        pstate = dict(i=0)

        def getps():
            i = pstate["i"]
            pstate["i"] = (i + 1) % 8
            return psb[i], psr[i]

        ident_f = sbt(top, "ident_f", [128, 2, 128], F32)
        cbf = sbt(top, "cbf", [128, 2, 128], BF16)
        ones_bf = sbt(top, "ones_bf", [128, 128], BF16)
        mask_f = sbt(top, "mask_f", [128, 3, 512], F32)
        mask_bf = sbt(top, "mask_bf", [128, 3, 512], BF16)
        FL = sbt(top, "FL", [128, 2], F32)
        KH = sbt(top, "KH", [128, 2, 128], BF16)
        VH = sbt(top, "VH", [128, 2, 128], BF16)
        HAL = sbt(top, "HAL", [128, KC, 2], F32)
        HIN = sbt(top, "HIN", [128, KC], F32)
        FIN = sbt(top, "FIN", [128, KC], F32)
        R_halo = Region()
        R_hin = Region()
        R_fin = Region()
        e_xh = sbt(top, "e1_xh", [128, KC, 2], F32)
        e_gx = sbt(top, "e1_gx", [128, 2, 16], F32)
        e_gk = sbt(top, "e1_gk", [128, 2, 256], BF16)
        e_gv = sbt(top, "e1_gv", [128, 2, 256], BF16)
        e_tk = sbt(top, "e1_tk", [128, 256], F32)
        e_gs = sbt(top, "e2_gs", [128, 2, 8], F32)
        e_ts = sbt(top, "e2_ts", [128, 8], F32)
        R_xrd = Region()
        MOD = sbt(top, "MOD", [128, L, 48, 2], F32)
        GV = sbt(top, "GV", [128, L, 4, KC], F32)
        SC = sbt(top, "SC", [128, L, 6, KC, 2], F32)
        CW = sbt(top, "CW", [128, L, 5, KC], F32)
        CB = sbt(top, "CB", [128, L, KC], F32)
        LV = sbt(top, "LV", [128, L, 3, 2, KC], F32)
        LD = sbt(top, "LD", [128, L, 4, 2, KC], F32)
        ESK = sbt(top, "ESK", [128, L, KC], F32)
        KT = sbt(top, "KT", [128, 2, NT], BF16)
        VT = sbt(top, "VT", [128, NBLK, 2, 128], BF16)
        R_const = Region()
        R_sc = Region()
        R_kv = Region()

        S.dma("sp", ident_f[:], consts[:, 0:2, :], writes=[R_const])
        S.dma("sp", mask_f[:], maskin, writes=[R_const])
        S.dma("sp", FL[:], flags, writes=[R_const])
        S.op("dve", lambda e: e.tensor_copy(out=cbf[:], in_=ident_f[:]), reads=[R_const], writes=[R_const])
        S.op("dve", lambda e: e.tensor_copy(out=mask_bf[:], in_=mask_f[:]), reads=[R_const], writes=[R_const])
        S.op("dve", lambda e: e.memset(ones_bf[:], 1.0), writes=[R_const])
        S.dma("sp", GV[:], gvec.rearrange("l p a k -> p l a k"), writes=[R_sc])
        S.dma("sp", CW[:], convw.rearrange("l p a k -> p l a k"), writes=[R_sc])
        S.dma("sp", CB[:], convb.rearrange("l p k -> p l k"), writes=[R_sc])
        S.dma("sp", LV[:], lruv.rearrange("l p a d k -> p l a d k"), writes=[R_sc])
        S.dma("sp", ESK[:], sinkbc.rearrange("l p k -> p l k"), writes=[R_sc])
        BM = sbt(top, "BM", [128, L, 48], F32)
        S.dma("sp", BM[:], bmod.rearrange("l p n -> p l n"), writes=[R_sc])

        ident_bf = cbf[:, 0, :]
        perm_bf = cbf[:, 1, :]

        with ExitStack() as es:
            cv = sbt(es, "cv", [128, KC, 2], F32)
            scv = sbt(es, "scv", [128, KC, 2], F32)
            wst = [sbt(es, f"wmst{i}", [128, KC, 512], F32) for i in range(2)]
            wsr = [Region(), Region()]
            R_cv = Region()
            S.dma("sp", cv[:], cvec, writes=[R_cv])
            S.op("act", lambda e: e.activation(out=scv[:], in_=cv[:], func=AF.Silu), reads=[R_cv], writes=[R_cv])
            it = 0
            for l in range(L):
                ps, pr = getps()
                for grp in range(12):
                    b = it % 2
                    it += 1
                    S.dma("sp", wst[b][:],
                          w_mod[l, :, grp * 512:(grp + 1) * 512].rearrange("(k p) n -> p k n", p=128),
                          writes=[wsr[b]])
                    for j in range(4):
                        n = grp * 4 + j
                        for kc in range(KC):
                            S.op("pe", lambda e, n=n, kc=kc, b=b, j=j, ps=ps: e.matmul(
                                ps[:, 2 * n:2 * n + 2], lhsT=wst[b][:, kc, j * 128:(j + 1) * 128],
                                rhs=scv[:, kc, :], start=(kc == 0), stop=(kc == KC - 1)),
                                reads=[wsr[b], R_cv], writes=[pr])
                for v in range(2):
                    S.op("dve", lambda e, l=l, ps=ps, v=v: e.tensor_tensor(
                        out=MOD[:, l, :, v], in0=ps[:, 0:96].rearrange("p (n v) -> p n v", v=2)[:, :, v],
                        in1=BM[:, l, :], op=ALU.add), reads=[pr, R_sc], writes=[R_sc])
            for l in range(L):
                for half, (gi_pre, gi_post) in enumerate([(0, 1), (2, 3)]):
                    base = half * 24
                    for v in range(2):
                        S.op("dve", lambda e, l=l, half=half, v=v, base=base, gi_pre=gi_pre: e.scalar_tensor_tensor(
                            out=SC[:, l, 3 * half + 0, :, v], in0=MOD[:, l, base + 8:base + 16, v], scalar=1.0,
                            in1=GV[:, l, gi_pre, :], op0=ALU.add, op1=ALU.mult), reads=[R_sc], writes=[R_sc])
                        S.op("dve", lambda e, l=l, half=half, v=v, base=base: e.tensor_copy(
                            out=SC[:, l, 3 * half + 1, :, v], in_=MOD[:, l, base:base + 8, v]), reads=[R_sc], writes=[R_sc])
                        S.op("dve", lambda e, l=l, half=half, v=v, base=base, gi_post=gi_post: e.tensor_tensor(
                            out=SC[:, l, 3 * half + 2, :, v], in0=MOD[:, l, base + 16:base + 24, v],
                            in1=GV[:, l, gi_post, :], op=ALU.mult), reads=[R_sc], writes=[R_sc])
                S.op("dve", lambda e, l=l: e.tensor_scalar(out=LD[:, l, 0:2], in0=LV[:, l, 0:2], scalar1=0.5, scalar2=None,
                                                          op0=ALU.mult), reads=[R_sc], writes=[R_sc])
                S.op("act", lambda e, l=l: e.activation(out=LD[:, l, 2], in_=LV[:, l, 2], func=AF.Exp, scale=-1.0),
                     reads=[R_sc], writes=[R_sc])
                S.op("act", lambda e, l=l: e.activation(out=LD[:, l, 2], in_=LD[:, l, 2], func=AF.Ln, bias=1.0),
                     reads=[R_sc], writes=[R_sc])
                S.op("dve", lambda e, l=l: e.tensor_scalar(out=LD[:, l, 3], in0=LD[:, l, 2], scalar1=-4.0, scalar2=None,
                                                          op0=ALU.mult), reads=[R_sc], writes=[R_sc])
                S.op("dve", lambda e, l=l: e.tensor_scalar(out=LD[:, l, 2], in0=LD[:, l, 2], scalar1=-8.0, scalar2=None,
                                                          op0=ALU.mult), reads=[R_sc], writes=[R_sc])
                S.op("act", lambda e, l=l: e.activation(out=ESK[:, l], in_=ESK[:, l], func=AF.Exp),
                     reads=[R_sc], writes=[R_sc])
            S.barrier()

        def norm_tile(es_name, l, which, xt, xr_reg, t0, w, v, Hdst, Hreg, tmp, rsd, sq):
            S.op("act", lambda e: e.activation(out=sq[:, :, 0:w], in_=xt[:, :, 0:w], func=AF.Square),
                 reads=[xr_reg], writes=[tmp["sq"]])
            ps, pr = getps()
            for k in range(KC):
                S.op("pe", lambda e, k=k: e.matmul(ps[:, 0:w], lhsT=ones_bf[:], rhs=sq[:, k, 0:w],
                                                   start=(k == 0), stop=(k == KC - 1)),
                     reads=[tmp["sq"], R_const], writes=[pr])
            S.op("act", lambda e: e.activation(out=rsd[:, 0:w], in_=ps[:, 0:w], func=AF.Sqrt, bias=EPS, scale=1.0 / D),
                 reads=[pr], writes=[tmp["rs"]])
            S.op("dve", lambda e: e.reciprocal(out=rsd[:, 0:w], in_=rsd[:, 0:w]), reads=[tmp["rs"]], writes=[tmp["rs"]])
            return

        def weight_block(es, name, kc, cols, nbuf=2):
            st = [sbt(es, f"{name}_st{i}", [128, kc, cols], F32) for i in range(nbuf)]
            wb = [sbt(es, f"{name}_wb{i}", [128, kc, cols], BF16) for i in range(nbuf)]
            return dict(st=st, wb=wb, sr=[Region() for _ in range(nbuf)], wr=[Region() for _ in range(nbuf)], i=0,
                        nbuf=nbuf)

        class WStream:
            def __init__(self, WB, srcs):
                self.WB, self.srcs, self.q, self.n = WB, srcs, {}, 0

            def get(self, i):
                dist = self.WB["nbuf"] - 1
                while self.n < len(self.srcs) and self.n <= i + dist:
                    self.q[self.n] = load_weight(self.WB, self.srcs[self.n])
                    self.n += 1
                return self.q.pop(i)

        def load_weight(WB, src_ap):
            b = WB["i"] % WB["nbuf"]
            WB["i"] += 1
            srcv = src_ap.rearrange("(k p) n -> p k n", p=128)
            kct = WB["st"][b].shape[1]
            for k0 in range(0, kct, 8):
                k1 = min(kct, k0 + 8)
                S.dma("sp", WB["st"][b][:, k0:k1, :], srcv[:, k0:k1, :], writes=[WB["sr"][b]])
            S.op("pool", lambda e: e.tensor_copy(out=WB["wb"][b][:], in_=WB["st"][b][:]),
                 reads=[WB["sr"][b]], writes=[WB["wr"][b]])
            return WB["wb"][b], WB["wr"][b]

        def pre_norm_pass(l, which, Xsrc, Hbuf, R_H, tiles):
            with ExitStack() as es:
                xt = [sbt(es, f"pn_x{i}", [128, KC, 512], F32) for i in range(3)]
                xr = [Region(), Region(), Region()]
                sq = sbt(es, "pn_sq", [128, KC, 512], BF16)
                rsd = sbt(es, "pn_rs", [128, 512], F32)
                tm = sbt(es, "pn_tm", [128, 2, 512], F32)
                tmp = dict(sq=Region(), rs=Region(), tm=[Region(), Region()])
                def pn_load(ti):
                    t0, w = tiles[ti]
                    b = ti % 3
                    S.dma("sp", xt[b][:, :, 0:w], Xsrc[:, :, t0:t0 + w].rearrange("k p t -> p k t"), writes=[xr[b]])
                pn_load(0)
                if len(tiles) > 1:
                    pn_load(1)
                for ti, (t0, w) in enumerate(tiles):
                    b = ti % 3
                    v = 1 if t0 < NCTX else 0
                    if ti + 2 < len(tiles):
                        pn_load(ti + 2)
                    norm_tile("pn", l, which, xt[b], xr[b], t0, w, v, Hbuf, R_H, tmp, rsd, sq)
                    for k in range(KC):
                        j = k % 2
                        eng_ = "pool" if k % 2 == 1 else "dve"
                        S.op(eng_, lambda e, k=k, j=j, b=b, w=w: e.tensor_tensor(
                            out=tm[:, j, 0:w], in0=xt[b][:, k, 0:w], in1=rsd[:, 0:w], op=ALU.mult),
                            reads=[xr[b], tmp["rs"]], writes=[tmp["tm"][j]])
                        S.op("act", lambda e, k=k, j=j, w=w, t0=t0, v=v: e.activation(
                            out=Hbuf[:, k, t0:t0 + w], in_=tm[:, j, 0:w], func=AF.Identity,
                            scale=SC[:, l, 3 * which + 0, k, v:v + 1], bias=SC[:, l, 3 * which + 1, k, v:v + 1]),
                            reads=[tmp["tm"][j], R_sc], writes=[R_H])
                S.barrier()

        def post_norm_pass(l, which, Ysrc, Xsrc, Xdst, tiles, Hbuf=None, R_H=None, lnext=None, final=False):
            with ExitStack() as es:
                yt = [sbt(es, f"po_y{i}", [128, KC, 512], F32) for i in range(3)]
                xt = [sbt(es, f"po_x{i}", [128, KC, 512], F32) for i in range(3)]
                yr = [Region(), Region(), Region()]
                xr = [Region(), Region(), Region()]
                sq = sbt(es, "po_sq", [128, KC, 512], BF16)
                rsd = sbt(es, "po_rs", [128, 512], F32)
                rsd2 = sbt(es, "po_rs2", [128, 512], F32)
                tm = sbt(es, "po_tm", [128, 2, 512], F32)
                tm2 = sbt(es, "po_tm2", [128, 2, 512], F32)
                tm2r = [Region(), Region()]
                tmp = dict(sq=Region(), rs=Region(), tm=[Region(), Region()])
                tmp2 = dict(sq=tmp["sq"], rs=Region(), tm=tmp["tm"])
                def po_load(ti):
                    t0, w = tiles[ti]
                    b = ti % 3
                    S.dma("sp", yt[b][:, :, 0:w], Ysrc[:, :, t0:t0 + w].rearrange("k p t -> p k t"), writes=[yr[b]])
                    S.dma("sp", xt[b][:, :, 0:w], Xsrc[:, :, t0:t0 + w].rearrange("k p t -> p k t"), writes=[xr[b]])
                po_load(0)
                if len(tiles) > 1:
                    po_load(1)
                for ti, (t0, w) in enumerate(tiles):
                    b = ti % 3
                    v = 1 if t0 < NCTX else 0
                    if ti + 2 < len(tiles):
                        po_load(ti + 2)
                    norm_tile("po", l, which, yt[b], yr[b], t0, w, v, None, None, tmp, rsd, sq)
                    for k in range(KC):
                        j = k % 2
                        S.op("dve", lambda e, k=k, j=j, b=b, w=w: e.tensor_tensor(
                            out=tm[:, j, 0:w], in0=yt[b][:, k, 0:w], in1=rsd[:, 0:w], op=ALU.mult),
                            reads=[yr[b], tmp["rs"]], writes=[tmp["tm"][j]])
                        S.op("dve", lambda e, k=k, j=j, b=b, w=w, v=v: e.scalar_tensor_tensor(
                            out=xt[b][:, k, 0:w], in0=tm[:, j, 0:w], scalar=SC[:, l, 3 * which + 2, k, v:v + 1],
                            in1=xt[b][:, k, 0:w], op0=ALU.mult, op1=ALU.add),
                            reads=[tmp["tm"][j], R_sc, xr[b]], writes=[xr[b]])
                    if final:
                        S.dma("sp", Xdst[:, :, t0 - NCTX:t0 - NCTX + w].rearrange("k p t -> p k t"), xt[b][:, :, 0:w],
                              reads=[xr[b]])
                    else:
                        S.dma("sp", Xdst[:, :, t0:t0 + w].rearrange("k p t -> p k t"), xt[b][:, :, 0:w], reads=[xr[b]])
                    if Hbuf is not None:
                        wn = 1 - which
                        norm_tile("po2", lnext, wn, xt[b], xr[b], t0, w, v, Hbuf, R_H, tmp2, rsd2, sq)
                        for k in range(KC):
                            j = k % 2
                            S.op("pool", lambda e, k=k, j=j, b=b, w=w: e.tensor_tensor(
                                out=tm2[:, j, 0:w], in0=xt[b][:, k, 0:w], in1=rsd2[:, 0:w], op=ALU.mult),
                                reads=[xr[b], tmp2["rs"]], writes=[tm2r[j]])
                            S.op("act", lambda e, k=k, j=j, w=w, t0=t0, v=v, wn=wn: e.activation(
                                out=Hbuf[:, k, t0:t0 + w], in_=tm2[:, j, 0:w], func=AF.Identity,
                                scale=SC[:, lnext, 3 * wn + 0, k, v:v + 1], bias=SC[:, lnext, 3 * wn + 1, k, v:v + 1]),
                                reads=[tm2r[j], R_sc], writes=[R_H])
                S.barrier()

        Hes = ExitStack()
        H = sbt(Hes, "H", [128, KC, NT], BF16)
        R_H = Region()
        checkpoint("mod")
        pre_norm_pass(0, 0, xin, H, R_H, TILES)
        checkpoint("pn0")
        Xcur = xin
        for l in range(L):
            need_ctx = l < L - 1
            tiles_out = TILES if need_ctx else TILES[1:]
            XA_l, XB_l = XA, XB

            R_e, R_g, R_d = Region(), Region(), Region()

            def e1a():
                S.dma("sp", e_xh[:], XRd[:, :, NT - 2:NT].rearrange("c p t -> p c t"), reads=[R_xrd], writes=[R_e])
                S.dma("sp", EXx[l], e_xh[:].rearrange("p c t -> p (c t)"), reads=[R_e], writes=[R_d])
                S.dma("sp", EXk[l].rearrange("p (h t) -> p h t", h=2), KT[:, :, NT - 128:NT], reads=[R_kv], writes=[R_d])
                S.dma("sp", EXv[l].rearrange("p (h t) -> p h t", h=2), VT[:, NBLK - 1, :, :], reads=[R_kv], writes=[R_d])
                S.collective(EXx[l], GXx[l], PAIRS, reads=[R_d], writes=[R_g])
                S.collective(EXk[l], GXk[l], PAIRS, reads=[R_d], writes=[R_g])
                S.collective(EXv[l], GXv[l], PAIRS, reads=[R_d], writes=[R_g])

            def e1b():
                gx, gk, gv, tk = e_gx, e_gk, e_gv, e_tk
                S.dma("sp", gx[:], GXx[l].rearrange("(r p) n -> p r n", r=2), reads=[R_g], writes=[R_e])
                S.dma("sp", gk[:], GXk[l].rearrange("(r p) n -> p r n", r=2), reads=[R_g], writes=[R_e])
                S.dma("sp", gv[:], GXv[l].rearrange("(r p) n -> p r n", r=2), reads=[R_g], writes=[R_e])
                S.op("dve", lambda e: e.tensor_scalar(out=tk[:, 0:16], in0=gx[:, 0, :], scalar1=FL[:, 0:1], scalar2=None,
                                                      op0=ALU.mult), reads=[R_e, R_const], writes=[R_e])
                S.op("dve", lambda e: e.scalar_tensor_tensor(out=HAL[:].rearrange("p c t -> p (c t)"), in0=gx[:, 1, :],
                                                             scalar=FL[:, 1:2], in1=tk[:, 0:16], op0=ALU.mult, op1=ALU.add),
                     reads=[R_e, R_const], writes=[R_halo])
                S.op("dve", lambda e: e.tensor_scalar(out=tk[:], in0=gk[:, 0, :], scalar1=FL[:, 0:1], scalar2=None,
                                                      op0=ALU.mult), reads=[R_e, R_const, R_halo], writes=[R_e])
                S.op("dve", lambda e: e.scalar_tensor_tensor(out=KH[:].rearrange("p h t -> p (h t)"), in0=gk[:, 1, :],
                                                             scalar=FL[:, 1:2], in1=tk[:], op0=ALU.mult, op1=ALU.add),
                     reads=[R_e, R_const], writes=[R_halo])
                S.op("dve", lambda e: e.tensor_scalar(out=tk[:], in0=gv[:, 0, :], scalar1=FL[:, 0:1], scalar2=None,
                                                      op0=ALU.mult), reads=[R_e, R_const, R_halo], writes=[R_e])
                S.op("dve", lambda e: e.scalar_tensor_tensor(out=VH[:].rearrange("p h t -> p (h t)"), in0=gv[:, 1, :],
                                                             scalar=FL[:, 1:2], in1=tk[:], op0=ALU.mult, op1=ALU.add),
                     reads=[R_e, R_const], writes=[R_halo])

            with ExitStack() as es:
                WB = weight_block(es, "pin", KC, 256, nbuf=3)
                rows = [sbt(es, f"pin_row{i}", [128, NT], F32) for i in range(2)]
                rrs = [[Region() for _ in TILES] for _ in range(2)]
                rtb = sbt(es, "pin_rt", [128, 2, 2, 512], F32)
                rtr = [Region(), Region()]
                qbf = sbt(es, "pin_qbf", [128, 2, 512], BF16)
                qbr = [Region(), Region()]
                t12 = sbt(es, "pin_t12", [128, 2, 2, 512], F32)
                t12r = [Region(), Region()]
                rowi = 0
                qi = 0
                order = list(range(0, 8)) + list(range(24, 28)) + list(range(16, 24)) + list(range(8, 16)) + list(range(28, 44))
                import os as _os
                if _os.environ.get("PIN_ORDER"):
                    order = [int(v) for v in _os.environ["PIN_ORDER"].split(",")]
                blocks = [(order[i], order[i + 1]) for i in range(0, len(order), 2)]
                wsP = WStream(WB, [w_in[l, :, bb[0] * 128:(bb[0] + 2) * 128] for bb in blocks])
                for bi_, (n0, n1) in enumerate(blocks):
                    assert n1 == n0 + 1 and n0 % 2 == 0
                    wt, wr = wsP.get(bi_)
                    for jn, n in enumerate((n0, n1)):
                        rb = rowi % 2
                        rowi += 1
                        row = rows[rb]
                        row_bf = row[:].bitcast(BF16)
                        for ti, (t0, w) in enumerate(TILES):
                            ps, pr = getps()
                            for k in range(KC):
                                S.op("pe", lambda e, k=k, ps=ps, jn=jn, wt=wt, t0=t0, w=w: e.matmul(
                                    ps[:, 0:w], lhsT=wt[:, k, jn * 128:(jn + 1) * 128], rhs=H[:, k, t0:t0 + w],
                                    start=(k == 0), stop=(k == KC - 1)), reads=[wr, R_H], writes=[pr])
                            rr = rrs[rb][ti]
                            if n < 8:
                                S.op("act", lambda e, ps=ps, row=row, t0=t0, w=w: e.activation(
                                    out=row[:, t0:t0 + w], in_=ps[:, 0:w], func=AF.Copy), reads=[pr], writes=[rr])
                            elif n < 16:
                                S.op("act", lambda e, ps=ps, row_bf=row_bf, t0=t0, w=w: e.activation(
                                    out=row_bf[:, t0:t0 + w], in_=ps[:, 0:w], func=AF.Gelu_apprx_tanh),
                                    reads=[pr], writes=[rr])
                            elif n < 26:
                                qb = qi % 2
                                qi += 1
                                S.dma("sp", rtb[:, qb, 0, 0:w], ropeC[:, t0:t0 + w], writes=[rtr[qb]])
                                S.dma("sp", rtb[:, qb, 1, 0:w], ropeS[:, t0:t0 + w], writes=[rtr[qb]])
                                S.op("act", lambda e, ps=ps, qb=qb, w=w: e.activation(
                                    out=qbf[:, qb, 0:w], in_=ps[:, 0:w], func=AF.Copy), reads=[pr], writes=[qbr[qb]])
                                ps2, pr2 = getps()
                                S.op("pe", lambda e, ps2=ps2, qb=qb, w=w: e.matmul(
                                    ps2[:, 0:w], lhsT=perm_bf, rhs=qbf[:, qb, 0:w], start=True, stop=True),
                                    reads=[qbr[qb], R_const], writes=[pr2])
                                S.op("dve", lambda e, ps=ps, qb=qb, t0=t0, w=w: e.tensor_tensor(
                                    out=t12[:, qb, 0, 0:w], in0=ps[:, 0:w], in1=rtb[:, qb, 0, 0:w], op=ALU.mult),
                                    reads=[pr, rtr[qb]], writes=[t12r[qb]])
                                S.op("dve", lambda e, ps2=ps2, qb=qb, t0=t0, w=w: e.tensor_tensor(
                                    out=t12[:, qb, 1, 0:w], in0=ps2[:, 0:w], in1=rtb[:, qb, 1, 0:w], op=ALU.mult),
                                    reads=[pr2, rtr[qb], t12r[qb]], writes=[t12r[qb]])
                                if n < 24:
                                    S.op("dve", lambda e, qb=qb, row_bf=row_bf, t0=t0, w=w: e.tensor_tensor(
                                        out=row_bf[:, t0:t0 + w], in0=t12[:, qb, 0, 0:w], in1=t12[:, qb, 1, 0:w],
                                        op=ALU.add), reads=[t12r[qb]], writes=[rr])
                                else:
                                    S.op("dve", lambda e, qb=qb, n=n, t0=t0, w=w: e.tensor_tensor(
                                        out=KT[:, n - 24, t0:t0 + w], in0=t12[:, qb, 0, 0:w], in1=t12[:, qb, 1, 0:w],
                                        op=ALU.add), reads=[t12r[qb]], writes=[R_kv])
                            elif n < 28:
                                S.op("act", lambda e, ps=ps, row_bf=row_bf, t0=t0, w=w: e.activation(
                                    out=row_bf[:, t0:t0 + w], in_=ps[:, 0:w], func=AF.Copy), reads=[pr], writes=[rr])
                                for bi in range(w // 128):
                                    blk = (t0 + bi * 128) // 128
                                    pst, prt = getps()
                                    pst_bf = pst[:].bitcast(BF16)
                                    S.op("pe", lambda e, pst_bf=pst_bf, row_bf=row_bf, t0=t0, bi=bi: e.transpose(
                                        pst_bf[:, 0:128], row_bf[:, t0 + bi * 128:t0 + (bi + 1) * 128], ident_bf),
                                        reads=[rr, R_const], writes=[prt])
                                    S.op("dve", lambda e, pst_bf=pst_bf, blk=blk, n=n: e.tensor_copy(
                                        out=VT[:, blk, n - 26, :], in_=pst_bf[:, 0:128]), reads=[prt], writes=[R_kv])
                            else:
                                S.op("act", lambda e, ps=ps, row_bf=row_bf, t0=t0, w=w: e.activation(
                                    out=row_bf[:, t0:t0 + w], in_=ps[:, 0:w], func=AF.Sigmoid), reads=[pr], writes=[rr])
                        if n < 8:
                            S.dma("sp", XRd[n], row[:], reads=rrs[rb], writes=[R_xrd])
                        elif n < 16:
                            S.dma("sp", Gd[n - 8], row_bf[:, 0:NT], reads=rrs[rb])
                        elif n < 24:
                            S.dma("sp", Qd[n - 16], row_bf[:, 0:NT], reads=rrs[rb])
                        elif n >= 28:
                            S.dma("sp", GABd[n - 28], row_bf[:, 0:NT], reads=rrs[rb])
                    if bi_ == 5 and len(blocks) > 6:
                        e1a()
                if len(blocks) <= 6:
                    e1a()
                e1b()
                S.barrier()
            Hes.close()
            checkpoint(f"pin{l}")


            def rnn_sweep(d):
                with ExitStack() as es:
                    sets = []
                    for si in range(2):
                        st = dict(
                            P=sbt(es, "rn_P", [128, PW], F32), XL=sbt(es, "rn_XL", [128, NT], F32),
                            XLb=sbt(es, "rn_XLb", [128, NT], BF16), A=sbt(es, "rn_A", [128, NT], F32),
                            A2=sbt(es, "rn_A2", [128, NT], F32), TI=sbt(es, "rn_TI", [128, NT], F32),
                            HF=sbt(es, "rn_HF", [128, NT], F32), Gs=sbt(es, "rn_G", [128, NT], BF16),
                            Ub=sbt(es, "rn_U", [128, NT], BF16))
                        for nm in ("P", "XL", "XLb", "A", "A2", "TI", "HF", "Gs", "Ub"):
                            st["R_" + nm] = Region()
                        sets.append(st)
                    trt = sbt(es, "rn_tr", [128, 2, 512], F32)
                    trr = [Region(), Region()]
                    wst = sbt(es, "rn_wst", [128, 2, KC, 128], F32)
                    wbf = sbt(es, "rn_wbf", [128, 2, KC, 128], BF16)
                    R_w = Region()
                    S.dma("sp", wst[:, 0], lru_wa[l, d].rearrange("b i j -> i b j"), writes=[R_w])
                    S.dma("sp", wst[:, 1], lru_wx[l, d].rearrange("b i j -> i b j"), writes=[R_w])
                    S.op("pool", lambda e: e.tensor_copy(out=wbf[:], in_=wst[:]), reads=[R_w], writes=[R_w])
                    if d == 0:
                        for st in sets:
                            S.op("pool", lambda e, st=st: e.memset(st["P"][:], 0.0), writes=[st["R_P"]])
                    tstate = dict(i=0)

                    def stage1(c):
                        st = sets[c % 2]
                        P, XL, XLb, HF, Gs = st["P"], st["XL"], st["XLb"], st["HF"], st["Gs"]
                        if d == 0:
                            S.dma("sp", P[:, 2:2 + NCTX], XRd[c, :, 0:NCTX], writes=[st["R_P"]])
                            S.dma("sp", P[:, LP0 + 2:LP0 + 2 + NLAT], XRd[c, :, NCTX:NT], writes=[st["R_P"]])
                            S.op("dve", lambda e: e.tensor_copy(out=P[:, LP0 + 2 + NLAT:LP0 + 3 + NLAT],
                                                                in_=HAL[:, c, 1:2]), reads=[R_halo, st["R_P"]], writes=[st["R_P"]])
                            S.op("dve", lambda e: e.tensor_copy(out=P[:, LP0 + 3 + NLAT:LP0 + 4 + NLAT],
                                                                in_=HAL[:, c, 0:1]), reads=[R_halo, st["R_P"]], writes=[st["R_P"]])
                            for (o0, n_, p0) in ((0, NCTX, 0), (NCTX, NLAT, LP0)):
                                S.op("dve", lambda e, o0=o0, n_=n_, p0=p0: e.tensor_scalar(
                                    out=XL[:, o0:o0 + n_], in0=P[:, p0:p0 + n_], scalar1=CW[:, l, 0, c:c + 1],
                                    scalar2=CB[:, l, c:c + 1], op0=ALU.mult, op1=ALU.add),
                                    reads=[st["R_P"], R_sc], writes=[st["R_XL"]])
                                for kk in range(1, 5):
                                    S.op("dve", lambda e, o0=o0, n_=n_, p0=p0, kk=kk: e.scalar_tensor_tensor(
                                        out=XL[:, o0:o0 + n_], in0=P[:, p0 + kk:p0 + kk + n_], scalar=CW[:, l, kk, c:c + 1],
                                        in1=XL[:, o0:o0 + n_], op0=ALU.mult, op1=ALU.add),
                                        reads=[st["R_P"], R_sc, st["R_XL"]], writes=[st["R_XL"]])
                            S.dma("sp", XLd[c], XL[:], reads=[st["R_XL"]])
                        else:
                            S.dma("sp", XL[:], XLd[c], writes=[st["R_XL"]])
                            S.dma("sp", HF[:], HFd[c], writes=[st["R_HF"]])
                            S.dma("sp", Gs[:], Gd[c], writes=[st["R_Gs"]])
                        S.op("pool", lambda e: e.tensor_copy(out=XLb[:], in_=XL[:]), reads=[st["R_XL"]], writes=[st["R_XLb"]])

                    def stage2(c):
                        st = sets[c % 2]
                        XLb, A, A2, TI = st["XLb"], st["A"], st["A2"], st["TI"]
                        for (t0, w) in TILES:
                            psr_, prr_ = getps()
                            psi_, pri_ = getps()
                            S.op("pe", lambda e, ps=psr_, t0=t0, w=w: e.matmul(
                                ps[:, 0:w], lhsT=wbf[:, 0, c, :], rhs=XLb[:, t0:t0 + w], start=True, stop=True),
                                reads=[R_w, st["R_XLb"]], writes=[prr_])
                            S.op("pe", lambda e, ps=psi_, t0=t0, w=w: e.matmul(
                                ps[:, 0:w], lhsT=wbf[:, 1, c, :], rhs=XLb[:, t0:t0 + w], start=True, stop=True),
                                reads=[R_w, st["R_XLb"]], writes=[pri_])
                            tb = tstate["i"] % 2
                            tstate["i"] += 1
                            S.op("act", lambda e, ps=psr_, tb=tb, w=w: e.activation(
                                out=trt[:, tb, 0:w], in_=ps[:, 0:w], func=AF.Tanh, scale=0.5,
                                bias=LD[:, l, 0, d, c:c + 1]), reads=[prr_, R_sc], writes=[trr[tb]])
                            S.op("act", lambda e, ps=psi_, t0=t0, w=w: e.activation(
                                out=TI[:, t0:t0 + w], in_=ps[:, 0:w], func=AF.Tanh, scale=0.5,
                                bias=LD[:, l, 1, d, c:c + 1]), reads=[pri_, R_sc], writes=[st["R_TI"]])
                            S.op("act", lambda e, tb=tb, t0=t0, w=w: e.activation(
                                out=A[:, t0:t0 + w], in_=trt[:, tb, 0:w], func=AF.Exp,
                                scale=LD[:, l, 3, d, c:c + 1], bias=LD[:, l, 3, d, c:c + 1]),
                                reads=[trr[tb], R_sc], writes=[st["R_A"]])
                            S.op("act", lambda e, tb=tb, t0=t0, w=w: e.activation(
                                out=A2[:, t0:t0 + w], in_=trt[:, tb, 0:w], func=AF.Exp,
                                scale=LD[:, l, 2, d, c:c + 1], bias=LD[:, l, 2, d, c:c + 1]),
                                reads=[trr[tb], R_sc], writes=[st["R_A2"]])
                        S.op("act", lambda e: e.activation(out=A2[:], in_=A2[:], func=AF.Sqrt, scale=-1.0, bias=1.0),
                             reads=[st["R_A2"]], writes=[st["R_A2"]])

                    def stage3(c):
                        st = sets[c % 2]
                        P, XL, A, A2, TI, HF, Gs, Ub = (st[k] for k in ("P", "XL", "A", "A2", "TI", "HF", "Gs", "Ub"))
                        S.op("dve", lambda e: e.scalar_tensor_tensor(out=TI[:], in0=TI[:], scalar=1.0, in1=XL[:],
                                                                     op0=ALU.add, op1=ALU.mult),
                             reads=[st["R_TI"], st["R_XL"]], writes=[st["R_TI"]])
                        S.op("dve", lambda e: e.scalar_tensor_tensor(out=TI[:], in0=TI[:], scalar=0.5, in1=A2[:],
                                                                     op0=ALU.mult, op1=ALU.mult),
                             reads=[st["R_TI"], st["R_A2"]], writes=[st["R_TI"]])
                        if d == 0:
                            S.op("dve", lambda e: e.tensor_tensor_scan(out=HF[:], data0=A[:], data1=TI[:], initial=0.0,
                                                                       op0=ALU.mult, op1=ALU.add),
                                 reads=[st["R_A"], st["R_TI"]], writes=[st["R_HF"]])
                            S.op("dve", lambda e: e.tensor_copy(out=FIN[:, c:c + 1], in_=HF[:, NT - 1:NT]),
                                 reads=[st["R_HF"]], writes=[R_fin])
                            S.dma("sp", HFd[c], HF[:], reads=[st["R_HF"]])
                        else:
                            S.op("dve", lambda e: e.tensor_tensor_scan(
                                out=P[:, NCTX - 1::-1], data0=A[:, NCTX - 1::-1], data1=TI[:, NCTX - 1::-1],
                                initial=0.0, op0=ALU.mult, op1=ALU.add),
                                reads=[st["R_A"], st["R_TI"], st["R_P"]], writes=[st["R_P"]])
                            S.op("dve", lambda e: e.tensor_tensor_scan(
                                out=P[:, NT - 1:NCTX - 1:-1], data0=A[:, NT - 1:NCTX - 1:-1], data1=TI[:, NT - 1:NCTX - 1:-1],
                                initial=HIN[:, c:c + 1], op0=ALU.mult, op1=ALU.add),
                                reads=[st["R_A"], st["R_TI"], st["R_P"], R_hin], writes=[st["R_P"]])
                            S.op("pool", lambda e: e.tensor_tensor(out=HF[:], in0=HF[:], in1=P[:, 0:NT], op=ALU.add),
                                 reads=[st["R_HF"], st["R_P"]], writes=[st["R_HF"]])
                            S.op("pool", lambda e: e.tensor_tensor(out=Ub[:], in0=HF[:], in1=Gs[:], op=ALU.mult),
                                 reads=[st["R_HF"], st["R_Gs"]], writes=[st["R_Ub"]])
                            S.dma("sp", Ud[c], Ub[:], reads=[st["R_Ub"]])

                    stage1(0)
                    for c in range(KC):
                        if c + 1 < KC:
                            stage1(c + 1)
                        stage2(c)
                        stage3(c)
                    S.barrier()

            rnn_sweep(0)
            R_e2, R_g2, R_d2 = Region(), Region(), Region()
            S.dma("sp", EXs[l], FIN[:], reads=[R_fin], writes=[R_d2])
            S.collective(EXs[l], GXs[l], PAIRS, reads=[R_d2], writes=[R_g2])
            checkpoint(f"rnnA{l}")
            with ExitStack() as es:
                Qt = [sbt(es, f"at_q{i}", [128, 4, 512], BF16) for i in range(2)]
                Qr = [Region(), Region()]
                Pj = [sbt(es, f"at_p{i}", [128, 512], BF16) for i in range(6)]
                Pr = [Region() for _ in range(6)]
                Ot = [sbt(es, f"at_o{i}", [128, 4, 512], BF16) for i in range(2)]
                Or = [Region(), Region()]
                dn = sbt(es, "at_dn", [128, 2, 512], F32)
                dnr = [Region(), Region()]
                pji = 0
                qti = 0
                bi_ = 0
                qtiles = (TILES if need_ctx else TILES[1:])
                qlist = [(kv, t0, w) for kv in range(2) for (t0, w) in qtiles]

                def q_load(i):
                    kv_, t0_, w_ = qlist[i]
                    S.dma("sp", Qt[i % 2][:, :, 0:w_],
                          Qd[kv_ * 4:(kv_ + 1) * 4, :, t0_:t0_ + w_].rearrange("g p t -> p g t"), writes=[Qr[i % 2]])
                q_load(0)
                for qi_, (kv, t0, w) in enumerate(qlist):
                    if True:
                        qb = qi_ % 2
                        nh = w // 128
                        if qi_ + 1 < len(qlist):
                            q_load(qi_ + 1)
                        for qq in range(nh):
                            q0 = qq * 128
                            tq = t0 + q0
                            if tq < NCTX:
                                keys = [(0, None), (128, None)]
                            else:
                                lb = (tq - NCTX) // 128
                                keys = []
                                if lb > 0:
                                    keys.append((tq - 128, 0))
                                keys.append((tq, None))
                                if lb < NLAT // 128 - 1:
                                    keys.append((tq + 128, 1))
                                else:
                                    keys.append((-1, 2))
                                keys += [(0, None), (128, None)]
                            pjs = []
                            for (k0, mk) in keys:
                                ps, pr = getps()
                                kap = KH[:, kv, :] if k0 < 0 else KT[:, kv, k0:k0 + 128]
                                S.op("pe", lambda e, ps=ps, kap=kap, qb=qb, q0=q0, mk=mk: e.matmul(
                                    ps[:, 0:512], lhsT=kap, rhs=Qt[qb][:, :, q0:q0 + 128],
                                    start=True, stop=(mk is None)), reads=[R_kv, R_halo, Qr[qb]], writes=[pr])
                                if mk is not None:
                                    S.op("pe", lambda e, ps=ps, mk=mk: e.matmul(
                                        ps[:, 0:512], lhsT=ident_bf, rhs=mask_bf[:, mk, :], start=False, stop=True),
                                        reads=[R_const], writes=[pr])
                                pb = pji % 6
                                pji += 1
                                S.op("act", lambda e, ps=ps, pb=pb: e.activation(
                                    out=Pj[pb][:], in_=ps[:, 0:512], func=AF.Exp, scale=float(128 ** -0.5)),
                                    reads=[pr], writes=[Pr[pb]])
                                pjs.append((pb, k0))
                            pso, pro = getps()
                            psd, prd = getps()
                            for ji, (pb, k0) in enumerate(pjs):
                                vap = VH[:, kv, :] if k0 < 0 else VT[:, k0 // 128, kv, :]
                                S.op("pe", lambda e, pso=pso, pb=pb, vap=vap, ji=ji: e.matmul(
                                    pso[:, 0:512], lhsT=vap, rhs=Pj[pb][:],
                                    start=(ji == 0), stop=(ji == len(pjs) - 1)), reads=[R_kv, R_halo, Pr[pb]], writes=[pro])
                            for ji, (pb, k0) in enumerate(pjs):
                                S.op("pe", lambda e, psd=psd, pb=pb, ji=ji: e.matmul(
                                    psd[:, 0:512], lhsT=ones_bf[:], rhs=Pj[pb][:],
                                    start=(ji == 0), stop=(ji == len(pjs) - 1)), reads=[R_const, Pr[pb]], writes=[prd])
                            db = bi_ % 2
                            bi_ += 1
                            for g in range(4):
                                S.op("dve", lambda e, psd=psd, db=db, g=g: e.tensor_scalar(
                                    out=dn[:, db, g * 128:(g + 1) * 128], in0=psd[:, g * 128:(g + 1) * 128],
                                    scalar1=ESK[:, l, kv * 4 + g:kv * 4 + g + 1], scalar2=None, op0=ALU.add),
                                    reads=[prd, R_sc], writes=[dnr[db]])
                            S.op("dve", lambda e, db=db: e.reciprocal(out=dn[:, db, :], in_=dn[:, db, :]),
                                 reads=[dnr[db]], writes=[dnr[db]])
                            S.op("dve", lambda e, pso=pso, db=db, qb=qb, q0=q0: e.tensor_tensor(
                                out=Ot[qb][:, :, q0:q0 + 128], in0=pso[:, 0:512].rearrange("p (g m) -> p g m", g=4),
                                in1=dn[:, db, :].rearrange("p (g m) -> p g m", g=4), op=ALU.mult),
                                reads=[pro, dnr[db]], writes=[Or[qb]])
                        S.dma("sp", Od[kv * 4:(kv + 1) * 4, :, t0:t0 + w].rearrange("g p t -> p g t"), Ot[qb][:, :, 0:w],
                              reads=[Or[qb]])
                S.barrier()

            checkpoint(f"attn{l}")
            S.dma("sp", e_gs[:], GXs[l].rearrange("(r p) n -> p r n", r=2), reads=[R_g2], writes=[R_e2])
            S.op("dve", lambda e: e.tensor_scalar(out=e_ts[:], in0=e_gs[:, 0, :], scalar1=FL[:, 0:1], scalar2=None,
                                                  op0=ALU.mult), reads=[R_e2, R_const], writes=[R_e2])
            S.op("dve", lambda e: e.scalar_tensor_tensor(out=HIN[:], in0=e_gs[:, 1, :], scalar=FL[:, 1:2], in1=e_ts[:],
                                                         op0=ALU.mult, op1=ALU.add),
                 reads=[R_e2, R_const], writes=[R_hin])
            rnn_sweep(1)
            checkpoint(f"rnn{l}")

            halves = [tiles_out]
            for hv in halves:
                h0 = hv[0][0]
                hw = hv[-1][0] + hv[-1][1] - h0
                with ExitStack() as es:
                    Uh = sbt(es, "mg_U", [128, KC, 2304], BF16)
                    Oh = sbt(es, "mg_O", [128, KC, 2304], BF16)
                    Zh = sbt(es, "mg_Z", [128, KC, 2304], BF16)
                    R_U2, R_O2, R_Z = Region(), Region(), Region()
                    WBa = weight_block(es, "mga", KC, 128)
                    WBb = weight_block(es, "mgb", KC, 128)
                    gab = sbt(es, "mg_gab", [128, 2, 2, 512], BF16)
                    gabr = [Region(), Region()]
                    zt = sbt(es, "mg_zt", [128, 2, 512], F32)
                    ztr = [Region(), Region()]
                    mrow = [sbt(es, f"mg_row{i}", [128, 2304], F32) for i in range(2)]
                    mrr = [[Region() for _ in hv] for _ in range(2)]
                    S.dma("sp", Uh[:, :, 0:hw], Ud[:, :, h0:h0 + hw].rearrange("k p t -> p k t"), writes=[R_U2])
                    S.dma("sp", Oh[:, :, 0:hw], Od[:, :, h0:h0 + hw].rearrange("k p t -> p k t"), writes=[R_O2])
                    gi = 0
                    wqa = load_weight(WBa, w_o_rnn[l, :, 0:128])
                    wqb = load_weight(WBb, w_o_attn[l, :, 0:128])
                    for n0 in range(0, KC):
                        wa_t, wa_r = wqa
                        wb_t, wb_r = wqb
                        if n0 + 1 < KC:
                            wqa = load_weight(WBa, w_o_rnn[l, :, (n0 + 1) * 128:(n0 + 2) * 128])
                            wqb = load_weight(WBb, w_o_attn[l, :, (n0 + 1) * 128:(n0 + 2) * 128])
                        else:
                            wqa = load_weight(WBa, w_out[l, :, 0:128])
                        for jn in range(1):
                            n = n0 + jn
                            for (t0, w) in hv:
                                o = t0 - h0
                                gb_ = gi % 2
                                gi += 1
                                S.dma("sp", gab[:, gb_, 0, 0:w], GABd[n, :, t0:t0 + w], writes=[gabr[gb_]])
                                S.dma("sp", gab[:, gb_, 1, 0:w], GABd[8 + n, :, t0:t0 + w], writes=[gabr[gb_]])
                                psa, pra = getps()
                                psb_, prb = getps()
                                for k in range(KC):
                                    S.op("pe", lambda e, k=k, psa=psa, jn=jn, wa_t=wa_t, o=o, w=w: e.matmul(
                                        psa[:, 0:w], lhsT=wa_t[:, k, jn * 128:(jn + 1) * 128], rhs=Uh[:, k, o:o + w],
                                        start=(k == 0), stop=(k == KC - 1)), reads=[wa_r, R_U2], writes=[pra])
                                for k in range(KC):
                                    S.op("pe", lambda e, k=k, psb_=psb_, jn=jn, wb_t=wb_t, o=o, w=w: e.matmul(
                                        psb_[:, 0:w], lhsT=wb_t[:, k, jn * 128:(jn + 1) * 128], rhs=Oh[:, k, o:o + w],
                                        start=(k == 0), stop=(k == KC - 1)), reads=[wb_r, R_O2], writes=[prb])
                                S.op("dve", lambda e, psa=psa, gb_=gb_, w=w: e.tensor_tensor(
                                    out=zt[:, gb_, 0:w], in0=psa[:, 0:w], in1=gab[:, gb_, 0, 0:w], op=ALU.mult),
                                    reads=[pra, gabr[gb_]], writes=[ztr[gb_]])
                                S.op("dve", lambda e, psb_=psb_, gb_=gb_, w=w: e.tensor_tensor(
                                    out=gab[:, gb_, 1, 0:w], in0=psb_[:, 0:w], in1=gab[:, gb_, 1, 0:w], op=ALU.mult),
                                    reads=[prb, gabr[gb_]], writes=[gabr[gb_]])
                                S.op("pool", lambda e, gb_=gb_, n=n, o=o, w=w: e.tensor_tensor(
                                    out=Zh[:, n, o:o + w], in0=zt[:, gb_, 0:w], in1=gab[:, gb_, 1, 0:w], op=ALU.add),
                                    reads=[ztr[gb_], gabr[gb_]], writes=[R_Z])
                    ri = 0
                    for n0 in range(0, KC):
                        wo_t, wo_r = wqa
                        if n0 + 1 < KC:
                            wqa = load_weight(WBa, w_out[l, :, (n0 + 1) * 128:(n0 + 2) * 128])
                        for jn in range(1):
                            n = n0 + jn
                            rb = ri % 2
                            ri += 1
                            for ti, (t0, w) in enumerate(hv):
                                o = t0 - h0
                                ps, pr = getps()
                                for k in range(KC):
                                    S.op("pe", lambda e, k=k, ps=ps, jn=jn, o=o, w=w: e.matmul(
                                        ps[:, 0:w], lhsT=wo_t[:, k, jn * 128:(jn + 1) * 128], rhs=Zh[:, k, o:o + w],
                                        start=(k == 0), stop=(k == KC - 1)), reads=[wo_r, R_Z], writes=[pr])
                                S.op("act", lambda e, ps=ps, rb=rb, o=o, w=w: e.activation(
                                    out=mrow[rb][:, o:o + w], in_=ps[:, 0:w], func=AF.Copy), reads=[pr], writes=[mrr[rb][ti]])
                            S.dma("sp", Md[n, :, h0:h0 + hw], mrow[rb][:, 0:hw], reads=mrr[rb])
                    S.barrier()

            checkpoint(f"merge{l}")
            with ExitStack() as es2:
                H2 = sbt(es2, "H2", [128, KC, NT], BF16)
                R_H2 = Region()
                post_norm_pass(l, 0, Md, Xcur, XA_l, tiles_out, Hbuf=H2, R_H=R_H2, lnext=l)
                checkpoint(f"post1{l}")
                with ExitStack() as es:
                    WBg = weight_block(es, "f1g", KC, 128, nbuf=3)
                    WBu = weight_block(es, "f1u", KC, 128, nbuf=3)
                    sg = sbt(es, "f1_sg", [128, 2, 512], F32)
                    sgr = [Region(), Region()]
                    arow = [sbt(es, f"f1_row{i}", [128, NT], BF16) for i in range(2)]
                    arr = [[Region() for _ in TILES] for _ in range(2)]
                    si = 0
                    wsG = WStream(WBg, [w_ffn_in[l, :, j * 128:(j + 1) * 128] for j in range(FC)])
                    wsU = WStream(WBu, [w_ffn_in[l, :, DFF + j * 128:DFF + (j + 1) * 128] for j in range(FC)])
                    for j in range(FC):
                        wg_t, wg_r = wsG.get(j)
                        wu_t, wu_r = wsU.get(j)
                        rb = j % 2
                        for ti, (t0, w) in enumerate(TILES):
                            if (t0, w) not in tiles_out:
                                continue
                            psg, prg = getps()
                            psu, pru = getps()
                            for k in range(KC):
                                S.op("pe", lambda e, k=k, psg=psg, t0=t0, w=w: e.matmul(
                                    psg[:, 0:w], lhsT=wg_t[:, k, :], rhs=H2[:, k, t0:t0 + w],
                                    start=(k == 0), stop=(k == KC - 1)), reads=[wg_r, R_H2], writes=[prg])
                            for k in range(KC):
                                S.op("pe", lambda e, k=k, psu=psu, t0=t0, w=w: e.matmul(
                                    psu[:, 0:w], lhsT=wu_t[:, k, :], rhs=H2[:, k, t0:t0 + w],
                                    start=(k == 0), stop=(k == KC - 1)), reads=[wu_r, R_H2], writes=[pru])
                            sb_ = si % 2
                            si += 1
                            S.op("act", lambda e, psg=psg, sb_=sb_, w=w: e.activation(
                                out=sg[:, sb_, 0:w], in_=psg[:, 0:w], func=AF.Silu), reads=[prg], writes=[sgr[sb_]])
                            S.op("dve", lambda e, psu=psu, sb_=sb_, rb=rb, t0=t0, w=w: e.tensor_tensor(
                                out=arow[rb][:, t0:t0 + w], in0=psu[:, 0:w], in1=sg[:, sb_, 0:w], op=ALU.mult),
                                reads=[pru, sgr[sb_]], writes=[arr[rb][ti]])
                        tl0 = tiles_out[0][0]
                        S.dma("sp", ACTd[j, :, tl0:NT], arow[rb][:, tl0:NT], reads=arr[rb])
                    S.barrier()
            checkpoint(f"f1{l}")
            segs = [tiles_out]
            for sg_ in segs:
                s0 = sg_[0][0]
                sw = sg_[-1][0] + sg_[-1][1] - s0
                with ExitStack() as es:
                    As = sbt(es, "f2_A", [128, FC, NT], BF16)
                    R_As = Region()
                    WB2 = weight_block(es, "f2w", FC, 128)
                    yrow = [sbt(es, f"f2_row{i}", [128, NT], F32) for i in range(2)]
                    yrr = [[Region() for _ in sg_] for _ in range(2)]
                    for k in range(FC):
                        S.dma("sp", As[:, k, 0:sw], ACTd[k, :, s0:s0 + sw], writes=[R_As])
                    wq2 = load_weight(WB2, w_ffn_out[l, :, 0:128])
                    for n in range(KC):
                        w2_t, w2_r = wq2
                        if n + 1 < KC:
                            wq2 = load_weight(WB2, w_ffn_out[l, :, (n + 1) * 128:(n + 2) * 128])
                        rb = n % 2
                        for ti, (t0, w) in enumerate(sg_):
                            o = t0 - s0
                            ps, pr = getps()
                            for k in range(FC):
                                S.op("pe", lambda e, k=k, ps=ps, o=o, w=w: e.matmul(
                                    ps[:, 0:w], lhsT=w2_t[:, k, :], rhs=As[:, k, o:o + w],
                                    start=(k == 0), stop=(k == FC - 1)), reads=[w2_r, R_As], writes=[pr])
                            S.op("act", lambda e, ps=ps, rb=rb, o=o, w=w: e.activation(
                                out=yrow[rb][:, o:o + w], in_=ps[:, 0:w], func=AF.Copy), reads=[pr], writes=[yrr[rb][ti]])
                        S.dma("sp", Yd[n, :, s0:s0 + sw], yrow[rb][:, 0:sw], reads=yrr[rb])
                    S.barrier()
            checkpoint(f"f2{l}")
            if l < L - 1:
                Hes = ExitStack()
                H = sbt(Hes, "H", [128, KC, NT], BF16)
                R_H = Region()
                post_norm_pass(l, 1, Yd, XA_l, XB_l, tiles_out, Hbuf=H, R_H=R_H, lnext=l + 1)
                Xcur = XB_l
            else:
                post_norm_pass(l, 1, Yd, XA_l, out, tiles_out, final=True)
        S.barrier()
    return nc


def _host_tables(half):
    inv = (10000.0 ** (-np.arange(32, dtype=np.float32) / 32.0)).astype(np.float32)
    t = np.arange(NLAT)
    pos = t if half == 0 else (NLAT_FULL - 1 - t)
    row = (pos // 64).astype(np.float32)
    col = (pos % 64).astype(np.float32)
    C = np.ones((128, NT), np.float32)
    Sg = np.zeros((128, NT), np.float32)
    perm = np.zeros((128, 128), np.float32)
    for p in range(128):
        hf = p // 64
        pp = p % 64
        fi = pp % 32
        ps_ = row if hf == 0 else col
        ang = (ps_ * inv[fi]).astype(np.float32)
        C[p, NCTX:] = np.cos(ang)
        if pp < 32:
            partner, sign = p + 32, -1.0
        else:
            partner, sign = p - 32, 1.0
        Sg[p, NCTX:] = sign * np.sin(ang)
        perm[partner, p] = 1.0
    ident = np.eye(128, dtype=np.float32)
    consts = np.stack([ident, perm, np.zeros_like(ident)], axis=1)
    i = np.arange(128)[:, None]
    m = np.arange(128)[None, :]
    lo = np.where(m <= i, 0.0, NEG).astype(np.float32)
    hi = np.where(i <= m, 0.0, NEG).astype(np.float32)
    halo = np.where((127 - i) <= m, 0.0, NEG).astype(np.float32)
    mask = np.stack([np.tile(lo, (1, 4)), np.tile(hi, (1, 4)), np.tile(halo, (1, 4))], axis=1)
    return C, Sg, np.ascontiguousarray(consts), np.ascontiguousarray(mask)


def _pc(v, nchunk):
    v = np.asarray(v, np.float32)
    lead = v.shape[:-1]
    v = v.reshape(lead + (nchunk, 128))
    return np.ascontiguousarray(np.moveaxis(v, -1, 0))


def make_in_maps(x, c, ctx, c_ctx, w_mod, b_mod, g_mix_pre, g_mix_post, g_ffn_pre, g_ffn_post, w_in, conv_w,
                 conv_b, lru_wa, lru_ba, lru_wx, lru_bx, lru_lam, attn_sink, w_o_rnn, w_o_attn, w_out,
                 w_ffn_in, w_ffn_out, n_cores=8):
    f = lambda a: np.ascontiguousarray(np.asarray(a, np.float32))
    bmod = np.stack([_pc(b_mod[l], 48) for l in range(L)])
    gvec = np.stack([np.stack([_pc(g[l], KC) for g in (g_mix_pre, g_mix_post, g_ffn_pre, g_ffn_post)], axis=1)
                     for l in range(L)])
    convb = np.stack([_pc(conv_b[l], KC) for l in range(L)])
    sinkbc = np.ascontiguousarray(np.broadcast_to(np.asarray(attn_sink, np.float32)[:, None, :], (L, 128, KC)))
    cw = np.asarray(conv_w, np.float32)
    zero = np.zeros_like(cw[:, :1])
    shared = dict(w_mod=f(w_mod), bmod=f(bmod), gvec=f(gvec), w_in=f(w_in), convb=f(convb),
                  sinkbc=f(sinkbc), w_o_rnn=f(w_o_rnn), w_o_attn=f(w_o_attn), w_out=f(w_out),
                  w_ffn_in=f(w_ffn_in), w_ffn_out=f(w_ffn_out))
    per_half = []
    for half in range(2):
        C, Sg, consts, mask = _host_tables(half)
        if half == 0:
            taps = np.concatenate([cw, zero], axis=1)
            dsel = [0, 1]
        else:
            taps = np.concatenate([zero, cw[:, ::-1]], axis=1)
            dsel = [1, 0]
        convw = np.stack([_pc(taps[l], KC) for l in range(L)])
        lruv = np.stack([np.stack([_pc(np.asarray(a, np.float32)[l][dsel], KC) for a in (lru_ba, lru_bx, lru_lam)],
                                  axis=1) for l in range(L)])
        wa = f(np.asarray(lru_wa, np.float32)[:, dsel])
        wx = f(np.asarray(lru_wx, np.float32)[:, dsel])
        flags = np.zeros((128, 2), np.float32)
        flags[:, 1 - half] = 1.0
        per_half.append(dict(ropeC=C, ropeS=Sg, consts=consts, maskin=mask, convw=f(convw), lruv=f(lruv),
                             lru_wa=wa, lru_wx=wx, flags=flags))
    maps = []
    for core in range(n_cores):
        b = core % 4
        half = core // 4
        xl = np.asarray(x[b], np.float32)[half * NLAT:(half + 1) * NLAT]
        cx = np.asarray(ctx[b], np.float32)
        if half == 1:
            xl = xl[::-1]
            cx = cx[::-1]
        xcat = np.concatenate([cx, xl], axis=0)
        xin = np.ascontiguousarray(xcat.T).reshape(KC, 128, NT)
        cv = np.stack([np.asarray(c[b], np.float32), np.asarray(c_ctx, np.float32)], axis=-1)
        cvec = np.ascontiguousarray(cv.reshape(KC, 128, 2).transpose(1, 0, 2))
        m = dict(shared)
        m.update(per_half[half])
        m["xin"] = xin
        m["cvec"] = cvec
        maps.append(m)
    return maps


_NC_CACHE = {}


def kernel(**inputs):
    if "nc" not in _NC_CACHE:
        _NC_CACHE["nc"] = build(debug=False)
    nc = _NC_CACHE["nc"]
    maps = make_in_maps(**inputs)
    res = run_bass_kernel_spmd(nc, maps, core_ids=list(range(8)))
    out = np.empty((4, NLAT_FULL, D), np.float32)
    for core in range(8):
        b = core % 4
        half = core // 4
        o = np.asarray(res.results[core]["out"], np.float32).reshape(D, NLAT).T
        if half == 1:
            o = o[::-1]
        out[b, half * NLAT:(half + 1) * NLAT] = o
    return out
```

```python
import numpy as np
from contextlib import ExitStack
import concourse.bass as bass
import concourse.mybir as mybir
from concourse.bass_utils import run_bass_kernel_spmd

F32 = mybir.dt.float32
BF16 = mybir.dt.bfloat16
AF = mybir.ActivationFunctionType
ALU = mybir.AluOpType

D = 1024
KC = 8
NCTX = 256
NLAT = 2048
NLAT_FULL = 4096
NT = NCTX + NLAT
DFF = 2816
FC = 22
INW = 5632
L = 2
EPS = 1e-6
NEG = -30000.0
TILES = [(0, 256)] + [(256 + 512 * i, 512) for i in range(4)]
NBLK = NT // 128
PW = NT + 8
LP0 = NCTX + 4


class Region:
    __slots__ = ("w", "rs", "excl")

    def __init__(self, excl=False):
        self.w = None
        self.rs = []
        self.excl = excl


class Sched:
    def __init__(self, nc, n_dma_sems=12):
        self.nc = nc
        self.engs = {}
        for name, h in [("pe", nc.tensor), ("act", nc.scalar), ("dve", nc.vector),
                        ("pool", nc.gpsimd), ("sp", nc.sync)]:
            sem = nc.alloc_semaphore(name="s_" + name)
            self.engs[name] = dict(h=h, sem=sem, cnt=0, known={}, key="e_" + name, name=name)
        self.dq = {}
        for q in ("sp", "pool"):
            self.dq[q] = dict(slots=[dict(sem=nc.alloc_semaphore(name=f"d_{q}{i}"), tot=0, key=f"d_{q}{i}")
                                     for i in range(n_dma_sems)], nxt=0)

    def _collect(self, reads, writes):
        deps = {}

        def add(tok):
            if tok is None:
                return
            key, sem, val = tok
            if key not in deps or deps[key][1] < val:
                deps[key] = (sem, val)
        for r in reads:
            add(r.w)
        for w in writes:
            add(w.w)
            for t in w.rs:
                add(t)
        return deps

    def _wait(self, E, deps):
        for key, (sem, val) in deps.items():
            if key == E["key"] and E["name"] == "pe":
                continue
            if E["known"].get(key, 0) >= val:
                continue
            E["h"].wait_ge(sem, val)
            E["known"][key] = val

    def _commit(self, tok, reads, writes):
        for r in reads:
            r.rs.append(tok)
            if len(r.rs) > 48:
                best = {}
                for t in r.rs:
                    if t[0] not in best or best[t[0]][2] < t[2]:
                        best[t[0]] = t
                r.rs = list(best.values())
        for w in writes:
            w.w = tok
            w.rs = []

    def op(self, eng, fn, reads=(), writes=()):
        if any(r.excl for r in reads):
            writes = list(writes) + [r for r in reads if r.excl]
            reads = [r for r in reads if not r.excl]
        E = self.engs[eng]
        self._wait(E, self._collect(reads, writes))
        inst = fn(E["h"])
        E["cnt"] += 1
        inst.then_inc(E["sem"], 1)
        self._commit((E["key"], E["sem"], E["cnt"]), reads, writes)

    def dma(self, q, out, in_, reads=(), writes=()):
        E = self.engs[q]
        Dq = self.dq[q]
        slot = Dq["slots"][Dq["nxt"]]
        Dq["nxt"] = (Dq["nxt"] + 1) % len(Dq["slots"])
        deps = self._collect(reads, writes)
        if slot["tot"] > 0:
            k = slot["key"]
            if k not in deps or deps[k][1] < slot["tot"]:
                deps[k] = (slot["sem"], slot["tot"])
        self._wait(E, deps)
        inst = E["h"].dma_start(out=out, in_=in_)
        slot["tot"] += 16
        inst.then_inc(slot["sem"], 16)
        self._commit((slot["key"], slot["sem"], slot["tot"]), reads, writes)

    def collective(self, in_ap, out_ap, groups, reads=(), writes=()):
        E = self.engs["pool"]
        if not hasattr(self, "cc"):
            self.cc = dict(sem=self.nc.alloc_semaphore(name="s_cc"), tot=0, key="cc")
        cc = self.cc
        deps = self._collect(reads, writes)
        if cc["tot"] > 0:
            deps["cc"] = (cc["sem"], cc["tot"])
        self._wait(E, deps)
        inst = E["h"].collective_compute("AllGather", ALU.bypass, replica_groups=groups,
                                         ins=[in_ap.opt()], outs=[out_ap.opt()])
        cc["tot"] += 1
        inst.then_inc(cc["sem"], 1)
        self._commit(("cc", cc["sem"], cc["tot"]), reads, writes)

    def barrier(self, engines=("pe", "act", "dve", "pool", "sp")):
        if getattr(self, "report", False):
            print("barrier: sbuf_bytes_remaining", self.nc.sbuf_bytes_remaining)
        deps = {}
        for name, e in self.engs.items():
            if e["cnt"] > 0:
                deps[e["key"]] = (e["sem"], e["cnt"])
        for q, Dq in self.dq.items():
            for s in Dq["slots"]:
                if s["tot"] > 0:
                    deps[s["key"]] = (s["sem"], s["tot"])
        if hasattr(self, "cc") and self.cc["tot"] > 0:
            deps["cc"] = (self.cc["sem"], self.cc["tot"])
        for name in engines:
            E = self.engs[name]
            for key, (sem, val) in deps.items():
                if key == E["key"]:
                    continue
                if E["known"].get(key, 0) >= val:
                    continue
                E["h"].wait_ge(sem, val)
                E["known"][key] = val


class _Stop(Exception):
    pass


def build(debug=False, stop=None):
    holder = {}
    try:
        return _build(debug, stop, holder)
    except _Stop:
        return holder['nc']


def _build(debug, stop, holder):
    nc = bass.Bass("TRN2", target_bir_lowering=False)
    holder['nc'] = nc

    def din(name, shape, dt=F32):
        return nc.dram_tensor(name, list(shape), dt, kind="ExternalInput").ap()

    def dscr(name, shape, dt):
        dbg = debug and name in ("Ud", "Od", "Md", "XB")
        return nc.dram_tensor(name, list(shape), dt, kind="ExternalOutput" if dbg else "Internal").ap()

    xin = din("xin", [KC, 128, NT])
    cvec = din("cvec", [128, KC, 2])
    w_mod = din("w_mod", [L, D, 6 * D])
    bmod = din("bmod", [L, 128, 48])
    gvec = din("gvec", [L, 128, 4, KC])
    w_in = din("w_in", [L, D, INW])
    convw = din("convw", [L, 128, 5, KC])
    flags = din("flags", [128, 2])
    convb = din("convb", [L, 128, KC])
    lru_wa = din("lru_wa", [L, 2, KC, 128, 128])
    lru_wx = din("lru_wx", [L, 2, KC, 128, 128])
    lruv = din("lruv", [L, 128, 3, 2, KC])
    sinkbc = din("sinkbc", [L, 128, KC])
    w_o_rnn = din("w_o_rnn", [L, D, D])
    w_o_attn = din("w_o_attn", [L, D, D])
    w_out = din("w_out", [L, D, D])
    w_ffn_in = din("w_ffn_in", [L, D, 2 * DFF])
    w_ffn_out = din("w_ffn_out", [L, DFF, D])
    ropeC = din("ropeC", [128, NT])
    ropeS = din("ropeS", [128, NT])
    consts = din("consts", [128, 3, 128])
    maskin = din("maskin", [128, 3, 512])
    out = nc.dram_tensor("out", [KC, 128, NLAT], F32, kind="ExternalOutput").ap()

    XA = dscr("XA", [KC, 128, NT], F32)
    XB = dscr("XB", [KC, 128, NT], F32)
    XRd = dscr("XRd", [KC, 128, NT], F32)
    Gd = dscr("Gd", [KC, 128, NT], BF16)
    Qd = dscr("Qd", [KC, 128, NT], BF16)
    GABd = dscr("GABd", [16, 128, NT], BF16)
    Ud = dscr("Ud", [KC, 128, NT], BF16)
    Od = dscr("Od", [KC, 128, NT], BF16)
    Md = dscr("Md", [KC, 128, NT], F32)
    ACTd = dscr("ACTd", [FC, 128, NT], BF16)
    Yd = dscr("Yd", [KC, 128, NT], F32)
    XLd = dscr("XLd", [KC, 128, NT], F32)
    HFd = dscr("HFd", [KC, 128, NT], F32)

    def dint(name, shape, dt):
        return nc.dram_tensor(name, list(shape), dt, kind="Internal").ap()
    EXx = [dint(f"EXx{l}", [128, 16], F32) for l in range(L)]
    GXx = [dint(f"GXx{l}", [256, 16], F32) for l in range(L)]
    EXk = [dint(f"EXk{l}", [128, 256], BF16) for l in range(L)]
    GXk = [dint(f"GXk{l}", [256, 256], BF16) for l in range(L)]
    EXv = [dint(f"EXv{l}", [128, 256], BF16) for l in range(L)]
    GXv = [dint(f"GXv{l}", [256, 256], BF16) for l in range(L)]
    EXs = [dint(f"EXs{l}", [128, 8], F32) for l in range(L)]
    GXs = [dint(f"GXs{l}", [256, 8], F32) for l in range(L)]
    PAIRS = [[0, 4], [1, 5], [2, 6], [3, 7]]

    top = ExitStack()
    with top:
        S = Sched(nc)
        nc._S = S

        nc._marks = []

        def checkpoint(name):
            nc._marks.append((name, S.engs["pe"]["cnt"]))
            if stop == name:
                S.barrier()
                raise _Stop()

        ncnt = dict(i=0)

        def sbt(es, name, shape, dt):
            ncnt["i"] += 1
            return es.enter_context(nc.sbuf_tensor(f"{name}_{ncnt['i']}", list(shape), dt))

        psb = [top.enter_context(nc.psum_tensor(f"ps{i}", [128, 512], F32)) for i in range(8)]
        psr = [Region(excl=True) for _ in range(8)]
        pstate = dict(i=0)

        def getps():
            i = pstate["i"]
            pstate["i"] = (i + 1) % 8
            return psb[i], psr[i]

        ident_f = sbt(top, "ident_f", [128, 2, 128], F32)
        cbf = sbt(top, "cbf", [128, 2, 128], BF16)
        ones_bf = sbt(top, "ones_bf", [128, 128], BF16)
        mask_f = sbt(top, "mask_f", [128, 3, 512], F32)
        mask_bf = sbt(top, "mask_bf", [128, 3, 512], BF16)
        FL = sbt(top, "FL", [128, 2], F32)
        KH = sbt(top, "KH", [128, 2, 128], BF16)
        VH = sbt(top, "VH", [128, 2, 128], BF16)
        HAL = sbt(top, "HAL", [128, KC, 2], F32)
        HIN = sbt(top, "HIN", [128, KC], F32)
        FIN = sbt(top, "FIN", [128, KC], F32)
        R_halo = Region()
        R_hin = Region()
        R_fin = Region()
        e_xh = sbt(top, "e1_xh", [128, KC, 2], F32)
        e_gx = sbt(top, "e1_gx", [128, 2, 16], F32)
        e_gk = sbt(top, "e1_gk", [128, 2, 256], BF16)
        e_gv = sbt(top, "e1_gv", [128, 2, 256], BF16)
        e_tk = sbt(top, "e1_tk", [128, 256], F32)
        e_gs = sbt(top, "e2_gs", [128, 2, 8], F32)
        e_ts = sbt(top, "e2_ts", [128, 8], F32)
        R_xrd = Region()
        MOD = sbt(top, "MOD", [128, L, 48, 2], F32)
        GV = sbt(top, "GV", [128, L, 4, KC], F32)
        SC = sbt(top, "SC", [128, L, 6, KC, 2], F32)
        CW = sbt(top, "CW", [128, L, 5, KC], F32)
        CB = sbt(top, "CB", [128, L, KC], F32)
        LV = sbt(top, "LV", [128, L, 3, 2, KC], F32)
        LD = sbt(top, "LD", [128, L, 4, 2, KC], F32)
        ESK = sbt(top, "ESK", [128, L, KC], F32)
        KT = sbt(top, "KT", [128, 2, NT], BF16)
        VT = sbt(top, "VT", [128, NBLK, 2, 128], BF16)
        R_const = Region()
        R_sc = Region()
        R_kv = Region()

        S.dma("sp", ident_f[:], consts[:, 0:2, :], writes=[R_const])
        S.dma("sp", mask_f[:], maskin, writes=[R_const])
        S.dma("sp", FL[:], flags, writes=[R_const])
        S.op("dve", lambda e: e.tensor_copy(out=cbf[:], in_=ident_f[:]), reads=[R_const], writes=[R_const])
        S.op("dve", lambda e: e.tensor_copy(out=mask_bf[:], in_=mask_f[:]), reads=[R_const], writes=[R_const])
        S.op("dve", lambda e: e.memset(ones_bf[:], 1.0), writes=[R_const])
        S.dma("sp", GV[:], gvec.rearrange("l p a k -> p l a k"), writes=[R_sc])
        S.dma("sp", CW[:], convw.rearrange("l p a k -> p l a k"), writes=[R_sc])
        S.dma("sp", CB[:], convb.rearrange("l p k -> p l k"), writes=[R_sc])
        S.dma("sp", LV[:], lruv.rearrange("l p a d k -> p l a d k"), writes=[R_sc])
        S.dma("sp", ESK[:], sinkbc.rearrange("l p k -> p l k"), writes=[R_sc])
        BM = sbt(top, "BM", [128, L, 48], F32)
        S.dma("sp", BM[:], bmod.rearrange("l p n -> p l n"), writes=[R_sc])

        ident_bf = cbf[:, 0, :]
        perm_bf = cbf[:, 1, :]

        with ExitStack() as es:
            cv = sbt(es, "cv", [128, KC, 2], F32)
            scv = sbt(es, "scv", [128, KC, 2], F32)
            wst = [sbt(es, f"wmst{i}", [128, KC, 512], F32) for i in range(2)]
            wsr = [Region(), Region()]
            R_cv = Region()
            S.dma("sp", cv[:], cvec, writes=[R_cv])
            S.op("act", lambda e: e.activation(out=scv[:], in_=cv[:], func=AF.Silu), reads=[R_cv], writes=[R_cv])
            it = 0
            for l in range(L):
                ps, pr = getps()
                for grp in range(12):
                    b = it % 2
                    it += 1
                    S.dma("sp", wst[b][:],
                          w_mod[l, :, grp * 512:(grp + 1) * 512].rearrange("(k p) n -> p k n", p=128),
                          writes=[wsr[b]])
                    for j in range(4):
                        n = grp * 4 + j
                        for kc in range(KC):
                            S.op("pe", lambda e, n=n, kc=kc, b=b, j=j, ps=ps: e.matmul(
                                ps[:, 2 * n:2 * n + 2], lhsT=wst[b][:, kc, j * 128:(j + 1) * 128],
                                rhs=scv[:, kc, :], start=(kc == 0), stop=(kc == KC - 1)),
                                reads=[wsr[b], R_cv], writes=[pr])
                for v in range(2):
                    S.op("dve", lambda e, l=l, ps=ps, v=v: e.tensor_tensor(
                        out=MOD[:, l, :, v], in0=ps[:, 0:96].rearrange("p (n v) -> p n v", v=2)[:, :, v],
                        in1=BM[:, l, :], op=ALU.add), reads=[pr, R_sc], writes=[R_sc])
            for l in range(L):
                for half, (gi_pre, gi_post) in enumerate([(0, 1), (2, 3)]):
                    base = half * 24
                    for v in range(2):
                        S.op("dve", lambda e, l=l, half=half, v=v, base=base, gi_pre=gi_pre: e.scalar_tensor_tensor(
                            out=SC[:, l, 3 * half + 0, :, v], in0=MOD[:, l, base + 8:base + 16, v], scalar=1.0,
                            in1=GV[:, l, gi_pre, :], op0=ALU.add, op1=ALU.mult), reads=[R_sc], writes=[R_sc])
                        S.op("dve", lambda e, l=l, half=half, v=v, base=base: e.tensor_copy(
                            out=SC[:, l, 3 * half + 1, :, v], in_=MOD[:, l, base:base + 8, v]), reads=[R_sc], writes=[R_sc])
                        S.op("dve", lambda e, l=l, half=half, v=v, base=base, gi_post=gi_post: e.tensor_tensor(
                            out=SC[:, l, 3 * half + 2, :, v], in0=MOD[:, l, base + 16:base + 24, v],
                            in1=GV[:, l, gi_post, :], op=ALU.mult), reads=[R_sc], writes=[R_sc])
                S.op("dve", lambda e, l=l: e.tensor_scalar(out=LD[:, l, 0:2], in0=LV[:, l, 0:2], scalar1=0.5, scalar2=None,
                                                          op0=ALU.mult), reads=[R_sc], writes=[R_sc])
                S.op("act", lambda e, l=l: e.activation(out=LD[:, l, 2], in_=LV[:, l, 2], func=AF.Exp, scale=-1.0),
                     reads=[R_sc], writes=[R_sc])
                S.op("act", lambda e, l=l: e.activation(out=LD[:, l, 2], in_=LD[:, l, 2], func=AF.Ln, bias=1.0),
                     reads=[R_sc], writes=[R_sc])
                S.op("dve", lambda e, l=l: e.tensor_scalar(out=LD[:, l, 3], in0=LD[:, l, 2], scalar1=-4.0, scalar2=None,
                                                          op0=ALU.mult), reads=[R_sc], writes=[R_sc])
                S.op("dve", lambda e, l=l: e.tensor_scalar(out=LD[:, l, 2], in0=LD[:, l, 2], scalar1=-8.0, scalar2=None,
                                                          op0=ALU.mult), reads=[R_sc], writes=[R_sc])
                S.op("act", lambda e, l=l: e.activation(out=ESK[:, l], in_=ESK[:, l], func=AF.Exp),
                     reads=[R_sc], writes=[R_sc])
            S.barrier()

        def norm_tile(es_name, l, which, xt, xr_reg, t0, w, v, Hdst, Hreg, tmp, rsd, sq):
            S.op("act", lambda e: e.activation(out=sq[:, :, 0:w], in_=xt[:, :, 0:w], func=AF.Square),
                 reads=[xr_reg], writes=[tmp["sq"]])
            ps, pr = getps()
            for k in range(KC):
                S.op("pe", lambda e, k=k: e.matmul(ps[:, 0:w], lhsT=ones_bf[:], rhs=sq[:, k, 0:w],
                                                   start=(k == 0), stop=(k == KC - 1)),
                     reads=[tmp["sq"], R_const], writes=[pr])
            S.op("act", lambda e: e.activation(out=rsd[:, 0:w], in_=ps[:, 0:w], func=AF.Sqrt, bias=EPS, scale=1.0 / D),
                 reads=[pr], writes=[tmp["rs"]])
            S.op("dve", lambda e: e.reciprocal(out=rsd[:, 0:w], in_=rsd[:, 0:w]), reads=[tmp["rs"]], writes=[tmp["rs"]])
            return

        def weight_block(es, name, kc, cols, nbuf=2):
            st = [sbt(es, f"{name}_st{i}", [128, kc, cols], F32) for i in range(nbuf)]
            wb = [sbt(es, f"{name}_wb{i}", [128, kc, cols], BF16) for i in range(nbuf)]
            return dict(st=st, wb=wb, sr=[Region() for _ in range(nbuf)], wr=[Region() for _ in range(nbuf)], i=0,
                        nbuf=nbuf)

        class WStream:
            def __init__(self, WB, srcs):
                self.WB, self.srcs, self.q, self.n = WB, srcs, {}, 0

            def get(self, i):
                dist = self.WB["nbuf"] - 1
                while self.n < len(self.srcs) and self.n <= i + dist:
                    self.q[self.n] = load_weight(self.WB, self.srcs[self.n])
                    self.n += 1
                return self.q.pop(i)

        def load_weight(WB, src_ap):
            b = WB["i"] % WB["nbuf"]
            WB["i"] += 1
            srcv = src_ap.rearrange("(k p) n -> p k n", p=128)
            kct = WB["st"][b].shape[1]
            for k0 in range(0, kct, 8):
                k1 = min(kct, k0 + 8)
                S.dma("sp", WB["st"][b][:, k0:k1, :], srcv[:, k0:k1, :], writes=[WB["sr"][b]])
            S.op("pool", lambda e: e.tensor_copy(out=WB["wb"][b][:], in_=WB["st"][b][:]),
                 reads=[WB["sr"][b]], writes=[WB["wr"][b]])
            return WB["wb"][b], WB["wr"][b]

        def pre_norm_pass(l, which, Xsrc, Hbuf, R_H, tiles):
            with ExitStack() as es:
                xt = [sbt(es, f"pn_x{i}", [128, KC, 512], F32) for i in range(3)]
                xr = [Region(), Region(), Region()]
                sq = sbt(es, "pn_sq", [128, KC, 512], BF16)
                rsd = sbt(es, "pn_rs", [128, 512], F32)
                tm = sbt(es, "pn_tm", [128, 2, 512], F32)
                tmp = dict(sq=Region(), rs=Region(), tm=[Region(), Region()])
                def pn_load(ti):
                    t0, w = tiles[ti]
                    b = ti % 3
                    S.dma("sp", xt[b][:, :, 0:w], Xsrc[:, :, t0:t0 + w].rearrange("k p t -> p k t"), writes=[xr[b]])
                pn_load(0)
                if len(tiles) > 1:
                    pn_load(1)
                for ti, (t0, w) in enumerate(tiles):
                    b = ti % 3
                    v = 1 if t0 < NCTX else 0
                    if ti + 2 < len(tiles):
                        pn_load(ti + 2)
                    norm_tile("pn", l, which, xt[b], xr[b], t0, w, v, Hbuf, R_H, tmp, rsd, sq)
                    for k in range(KC):
                        j = k % 2
                        S.op("dve", lambda e, k=k, j=j, b=b, w=w: e.tensor_tensor(
                            out=tm[:, j, 0:w], in0=xt[b][:, k, 0:w], in1=rsd[:, 0:w], op=ALU.mult),
                            reads=[xr[b], tmp["rs"]], writes=[tmp["tm"][j]])
                        S.op("act", lambda e, k=k, j=j, w=w, t0=t0, v=v: e.activation(
                            out=Hbuf[:, k, t0:t0 + w], in_=tm[:, j, 0:w], func=AF.Identity,
                            scale=SC[:, l, 3 * which + 0, k, v:v + 1], bias=SC[:, l, 3 * which + 1, k, v:v + 1]),
                            reads=[tmp["tm"][j], R_sc], writes=[R_H])
                S.barrier()

        def post_norm_pass(l, which, Ysrc, Xsrc, Xdst, tiles, Hbuf=None, R_H=None, lnext=None, final=False):
            with ExitStack() as es:
                yt = [sbt(es, f"po_y{i}", [128, KC, 512], F32) for i in range(3)]
                xt = [sbt(es, f"po_x{i}", [128, KC, 512], F32) for i in range(3)]
                yr = [Region(), Region(), Region()]
                xr = [Region(), Region(), Region()]
                sq = sbt(es, "po_sq", [128, KC, 512], BF16)
                rsd = sbt(es, "po_rs", [128, 512], F32)
                rsd2 = sbt(es, "po_rs2", [128, 512], F32)
                tm = sbt(es, "po_tm", [128, 2, 512], F32)
                tm2 = sbt(es, "po_tm2", [128, 2, 512], F32)
                tm2r = [Region(), Region()]
                tmp = dict(sq=Region(), rs=Region(), tm=[Region(), Region()])
                tmp2 = dict(sq=tmp["sq"], rs=Region(), tm=tmp["tm"])
                def po_load(ti):
                    t0, w = tiles[ti]
                    b = ti % 3
                    S.dma("sp", yt[b][:, :, 0:w], Ysrc[:, :, t0:t0 + w].rearrange("k p t -> p k t"), writes=[yr[b]])
                    S.dma("sp", xt[b][:, :, 0:w], Xsrc[:, :, t0:t0 + w].rearrange("k p t -> p k t"), writes=[xr[b]])
                po_load(0)
                if len(tiles) > 1:
                    po_load(1)
                for ti, (t0, w) in enumerate(tiles):
                    b = ti % 3
                    v = 1 if t0 < NCTX else 0
                    if ti + 2 < len(tiles):
                        po_load(ti + 2)
                    norm_tile("po", l, which, yt[b], yr[b], t0, w, v, None, None, tmp, rsd, sq)
                    for k in range(KC):
                        j = k % 2
                        S.op("dve", lambda e, k=k, j=j, b=b, w=w: e.tensor_tensor(
                            out=tm[:, j, 0:w], in0=yt[b][:, k, 0:w], in1=rsd[:, 0:w], op=ALU.mult),
                            reads=[yr[b], tmp["rs"]], writes=[tmp["tm"][j]])
                        S.op("dve", lambda e, k=k, j=j, b=b, w=w, v=v: e.scalar_tensor_tensor(
                            out=xt[b][:, k, 0:w], in0=tm[:, j, 0:w], scalar=SC[:, l, 3 * which + 2, k, v:v + 1],
                            in1=xt[b][:, k, 0:w], op0=ALU.mult, op1=ALU.add),
                            reads=[tmp["tm"][j], R_sc, xr[b]], writes=[xr[b]])
                    if final:
                        S.dma("sp", Xdst[:, :, t0 - NCTX:t0 - NCTX + w].rearrange("k p t -> p k t"), xt[b][:, :, 0:w],
                              reads=[xr[b]])
                    else:
                        S.dma("sp", Xdst[:, :, t0:t0 + w].rearrange("k p t -> p k t"), xt[b][:, :, 0:w], reads=[xr[b]])
                    if Hbuf is not None:
                        wn = 1 - which
                        norm_tile("po2", lnext, wn, xt[b], xr[b], t0, w, v, Hbuf, R_H, tmp2, rsd2, sq)
                        for k in range(KC):
                            j = k % 2
                            S.op("dve", lambda e, k=k, j=j, b=b, w=w: e.tensor_tensor(
                                out=tm2[:, j, 0:w], in0=xt[b][:, k, 0:w], in1=rsd2[:, 0:w], op=ALU.mult),
                                reads=[xr[b], tmp2["rs"]], writes=[tm2r[j]])
                            S.op("act", lambda e, k=k, j=j, w=w, t0=t0, v=v, wn=wn: e.activation(
                                out=Hbuf[:, k, t0:t0 + w], in_=tm2[:, j, 0:w], func=AF.Identity,
                                scale=SC[:, lnext, 3 * wn + 0, k, v:v + 1], bias=SC[:, lnext, 3 * wn + 1, k, v:v + 1]),
                                reads=[tm2r[j], R_sc], writes=[R_H])
                S.barrier()

        Hes = ExitStack()
        H = sbt(Hes, "H", [128, KC, NT], BF16)
        R_H = Region()
        checkpoint("mod")
        pre_norm_pass(0, 0, xin, H, R_H, TILES)
        checkpoint("pn0")
        Xcur = xin
        for l in range(L):
            need_ctx = l < L - 1
            tiles_out = TILES if need_ctx else TILES[1:]
            XA_l, XB_l = XA, XB

            R_e, R_g, R_d = Region(), Region(), Region()

            def e1a():
                S.dma("sp", e_xh[:], XRd[:, :, NT - 2:NT].rearrange("c p t -> p c t"), reads=[R_xrd], writes=[R_e])
                S.dma("sp", EXx[l], e_xh[:].rearrange("p c t -> p (c t)"), reads=[R_e], writes=[R_d])
                S.dma("sp", EXk[l].rearrange("p (h t) -> p h t", h=2), KT[:, :, NT - 128:NT], reads=[R_kv], writes=[R_d])
                S.dma("sp", EXv[l].rearrange("p (h t) -> p h t", h=2), VT[:, NBLK - 1, :, :], reads=[R_kv], writes=[R_d])
                S.collective(EXx[l], GXx[l], PAIRS, reads=[R_d], writes=[R_g])
                S.collective(EXk[l], GXk[l], PAIRS, reads=[R_d], writes=[R_g])
                S.collective(EXv[l], GXv[l], PAIRS, reads=[R_d], writes=[R_g])

            def e1b():
                gx, gk, gv, tk = e_gx, e_gk, e_gv, e_tk
                S.dma("sp", gx[:], GXx[l].rearrange("(r p) n -> p r n", r=2), reads=[R_g], writes=[R_e])
                S.dma("sp", gk[:], GXk[l].rearrange("(r p) n -> p r n", r=2), reads=[R_g], writes=[R_e])
                S.dma("sp", gv[:], GXv[l].rearrange("(r p) n -> p r n", r=2), reads=[R_g], writes=[R_e])
                S.op("dve", lambda e: e.tensor_scalar(out=tk[:, 0:16], in0=gx[:, 0, :], scalar1=FL[:, 0:1], scalar2=None,
                                                      op0=ALU.mult), reads=[R_e, R_const], writes=[R_e])
                S.op("dve", lambda e: e.scalar_tensor_tensor(out=HAL[:].rearrange("p c t -> p (c t)"), in0=gx[:, 1, :],
                                                             scalar=FL[:, 1:2], in1=tk[:, 0:16], op0=ALU.mult, op1=ALU.add),
                     reads=[R_e, R_const], writes=[R_halo])
                S.op("dve", lambda e: e.tensor_scalar(out=tk[:], in0=gk[:, 0, :], scalar1=FL[:, 0:1], scalar2=None,
                                                      op0=ALU.mult), reads=[R_e, R_const, R_halo], writes=[R_e])
                S.op("dve", lambda e: e.scalar_tensor_tensor(out=KH[:].rearrange("p h t -> p (h t)"), in0=gk[:, 1, :],
                                                             scalar=FL[:, 1:2], in1=tk[:], op0=ALU.mult, op1=ALU.add),
                     reads=[R_e, R_const], writes=[R_halo])
                S.op("dve", lambda e: e.tensor_scalar(out=tk[:], in0=gv[:, 0, :], scalar1=FL[:, 0:1], scalar2=None,
                                                      op0=ALU.mult), reads=[R_e, R_const, R_halo], writes=[R_e])
                S.op("dve", lambda e: e.scalar_tensor_tensor(out=VH[:].rearrange("p h t -> p (h t)"), in0=gv[:, 1, :],
                                                             scalar=FL[:, 1:2], in1=tk[:], op0=ALU.mult, op1=ALU.add),
                     reads=[R_e, R_const], writes=[R_halo])

            with ExitStack() as es:
                WB = weight_block(es, "pin", KC, 256, nbuf=3)
                rows = [sbt(es, f"pin_row{i}", [128, NT], F32) for i in range(2)]
                rrs = [[Region() for _ in TILES] for _ in range(2)]
                rtb = sbt(es, "pin_rt", [128, 2, NT], F32)
                R_rt = Region()
                S.dma("sp", rtb[:, 0, :], ropeC, writes=[R_rt])
                S.dma("sp", rtb[:, 1, :], ropeS, writes=[R_rt])
                qbf = sbt(es, "pin_qbf", [128, 2, 512], BF16)
                qbr = [Region(), Region()]
                t12 = sbt(es, "pin_t12", [128, 2, 2, 512], F32)
                t12r = [Region(), Region()]
                rowi = 0
                qi = 0
                order = list(range(0, 8)) + list(range(24, 28)) + list(range(16, 24)) + list(range(8, 16)) + list(range(28, 44))
                import os as _os
                if _os.environ.get("PIN_ORDER"):
                    order = [int(v) for v in _os.environ["PIN_ORDER"].split(",")]
                blocks = [(order[i], order[i + 1]) for i in range(0, len(order), 2)]
                wsP = WStream(WB, [w_in[l, :, bb[0] * 128:(bb[0] + 2) * 128] for bb in blocks])
                for bi_, (n0, n1) in enumerate(blocks):
                    assert n1 == n0 + 1 and n0 % 2 == 0
                    wt, wr = wsP.get(bi_)
                    for jn, n in enumerate((n0, n1)):
                        rb = rowi % 2
                        rowi += 1
                        row = rows[rb]
                        row_bf = row[:].bitcast(BF16)
                        for ti, (t0, w) in enumerate(TILES):
                            ps, pr = getps()
                            for k in range(KC):
                                S.op("pe", lambda e, k=k, ps=ps, jn=jn, wt=wt, t0=t0, w=w: e.matmul(
                                    ps[:, 0:w], lhsT=wt[:, k, jn * 128:(jn + 1) * 128], rhs=H[:, k, t0:t0 + w],
                                    start=(k == 0), stop=(k == KC - 1)), reads=[wr, R_H], writes=[pr])
                            rr = rrs[rb][ti]
                            if n < 8:
                                S.op("act", lambda e, ps=ps, row=row, t0=t0, w=w: e.activation(
                                    out=row[:, t0:t0 + w], in_=ps[:, 0:w], func=AF.Copy), reads=[pr], writes=[rr])
                            elif n < 16:
                                S.op("act", lambda e, ps=ps, row_bf=row_bf, t0=t0, w=w: e.activation(
                                    out=row_bf[:, t0:t0 + w], in_=ps[:, 0:w], func=AF.Gelu_apprx_tanh),
                                    reads=[pr], writes=[rr])
                            elif n < 26:
                                qb = qi % 2
                                qi += 1
                                S.op("act", lambda e, ps=ps, qb=qb, w=w: e.activation(
                                    out=qbf[:, qb, 0:w], in_=ps[:, 0:w], func=AF.Copy), reads=[pr], writes=[qbr[qb]])
                                ps2, pr2 = getps()
                                S.op("pe", lambda e, ps2=ps2, qb=qb, w=w: e.matmul(
                                    ps2[:, 0:w], lhsT=perm_bf, rhs=qbf[:, qb, 0:w], start=True, stop=True),
                                    reads=[qbr[qb], R_const], writes=[pr2])
                                S.op("dve", lambda e, ps=ps, qb=qb, t0=t0, w=w: e.tensor_tensor(
                                    out=t12[:, qb, 0, 0:w], in0=ps[:, 0:w], in1=rtb[:, 0, t0:t0 + w], op=ALU.mult),
                                    reads=[pr, R_rt], writes=[t12r[qb]])
                                S.op("dve", lambda e, ps2=ps2, qb=qb, t0=t0, w=w: e.tensor_tensor(
                                    out=t12[:, qb, 1, 0:w], in0=ps2[:, 0:w], in1=rtb[:, 1, t0:t0 + w], op=ALU.mult),
                                    reads=[pr2, R_rt, t12r[qb]], writes=[t12r[qb]])
                                if n < 24:
                                    S.op("dve", lambda e, qb=qb, row_bf=row_bf, t0=t0, w=w: e.tensor_tensor(
                                        out=row_bf[:, t0:t0 + w], in0=t12[:, qb, 0, 0:w], in1=t12[:, qb, 1, 0:w],
                                        op=ALU.add), reads=[t12r[qb]], writes=[rr])
                                else:
                                    S.op("dve", lambda e, qb=qb, n=n, t0=t0, w=w: e.tensor_tensor(
                                        out=KT[:, n - 24, t0:t0 + w], in0=t12[:, qb, 0, 0:w], in1=t12[:, qb, 1, 0:w],
                                        op=ALU.add), reads=[t12r[qb]], writes=[R_kv])
                            elif n < 28:
                                S.op("act", lambda e, ps=ps, row_bf=row_bf, t0=t0, w=w: e.activation(
                                    out=row_bf[:, t0:t0 + w], in_=ps[:, 0:w], func=AF.Copy), reads=[pr], writes=[rr])
                                for bi in range(w // 128):
                                    blk = (t0 + bi * 128) // 128
                                    pst, prt = getps()
                                    pst_bf = pst[:].bitcast(BF16)
                                    S.op("pe", lambda e, pst_bf=pst_bf, row_bf=row_bf, t0=t0, bi=bi: e.transpose(
                                        pst_bf[:, 0:128], row_bf[:, t0 + bi * 128:t0 + (bi + 1) * 128], ident_bf),
                                        reads=[rr, R_const], writes=[prt])
                                    S.op("dve", lambda e, pst_bf=pst_bf, blk=blk, n=n: e.tensor_copy(
                                        out=VT[:, blk, n - 26, :], in_=pst_bf[:, 0:128]), reads=[prt], writes=[R_kv])
                            else:
                                S.op("act", lambda e, ps=ps, row_bf=row_bf, t0=t0, w=w: e.activation(
                                    out=row_bf[:, t0:t0 + w], in_=ps[:, 0:w], func=AF.Sigmoid), reads=[pr], writes=[rr])
                        if n < 8:
                            S.dma("sp", XRd[n], row[:], reads=rrs[rb], writes=[R_xrd])
                        elif n < 16:
                            S.dma("sp", Gd[n - 8], row_bf[:, 0:NT], reads=rrs[rb])
                        elif n < 24:
                            S.dma("sp", Qd[n - 16], row_bf[:, 0:NT], reads=rrs[rb])
                        elif n >= 28:
                            S.dma("sp", GABd[n - 28], row_bf[:, 0:NT], reads=rrs[rb])
                    if bi_ == 5 and len(blocks) > 6:
                        e1a()
                if len(blocks) <= 6:
                    e1a()
                e1b()
                S.barrier()
            Hes.close()
            checkpoint(f"pin{l}")


            def rnn_sweep(d):
                with ExitStack() as es:
                    sets = []
                    for si in range(2):
                        st = dict(
                            P=sbt(es, "rn_P", [128, PW], F32), XL=sbt(es, "rn_XL", [128, NT], F32),
                            XLb=sbt(es, "rn_XLb", [128, NT], BF16), A=sbt(es, "rn_A", [128, NT], F32),
                            A2=sbt(es, "rn_A2", [128, NT], F32), TI=sbt(es, "rn_TI", [128, NT], F32),
                            HF=sbt(es, "rn_HF", [128, NT], F32), Gs=sbt(es, "rn_G", [128, NT], BF16),
                            Ub=sbt(es, "rn_U", [128, NT], BF16))
                        for nm in ("P", "XL", "XLb", "A", "A2", "TI", "HF", "Gs", "Ub"):
                            st["R_" + nm] = Region()
                        sets.append(st)
                    trt = sbt(es, "rn_tr", [128, 2, 512], F32)
                    trr = [Region(), Region()]
                    wst = sbt(es, "rn_wst", [128, 2, KC, 128], F32)
                    wbf = sbt(es, "rn_wbf", [128, 2, KC, 128], BF16)
                    R_w = Region()
                    S.dma("sp", wst[:, 0], lru_wa[l, d].rearrange("b i j -> i b j"), writes=[R_w])
                    S.dma("sp", wst[:, 1], lru_wx[l, d].rearrange("b i j -> i b j"), writes=[R_w])
                    S.op("pool", lambda e: e.tensor_copy(out=wbf[:], in_=wst[:]), reads=[R_w], writes=[R_w])
                    if d == 0:
                        for st in sets:
                            S.op("pool", lambda e, st=st: e.memset(st["P"][:], 0.0), writes=[st["R_P"]])
                    tstate = dict(i=0)

                    def stage1(c):
                        st = sets[c % 2]
                        P, XL, XLb, HF, Gs = st["P"], st["XL"], st["XLb"], st["HF"], st["Gs"]
                        if d == 0:
                            S.dma("sp", P[:, 2:2 + NCTX], XRd[c, :, 0:NCTX], writes=[st["R_P"]])
                            S.dma("sp", P[:, LP0 + 2:LP0 + 2 + NLAT], XRd[c, :, NCTX:NT], writes=[st["R_P"]])
                            S.op("dve", lambda e: e.tensor_copy(out=P[:, LP0 + 2 + NLAT:LP0 + 3 + NLAT],
                                                                in_=HAL[:, c, 1:2]), reads=[R_halo, st["R_P"]], writes=[st["R_P"]])
                            S.op("dve", lambda e: e.tensor_copy(out=P[:, LP0 + 3 + NLAT:LP0 + 4 + NLAT],
                                                                in_=HAL[:, c, 0:1]), reads=[R_halo, st["R_P"]], writes=[st["R_P"]])
                            for (o0, n_, p0) in ((0, NCTX, 0), (NCTX, NLAT, LP0)):
                                S.op("dve", lambda e, o0=o0, n_=n_, p0=p0: e.tensor_scalar(
                                    out=XL[:, o0:o0 + n_], in0=P[:, p0:p0 + n_], scalar1=CW[:, l, 0, c:c + 1],
                                    scalar2=CB[:, l, c:c + 1], op0=ALU.mult, op1=ALU.add),
                                    reads=[st["R_P"], R_sc], writes=[st["R_XL"]])
                                for kk in range(1, 5):
                                    S.op("dve", lambda e, o0=o0, n_=n_, p0=p0, kk=kk: e.scalar_tensor_tensor(
                                        out=XL[:, o0:o0 + n_], in0=P[:, p0 + kk:p0 + kk + n_], scalar=CW[:, l, kk, c:c + 1],
                                        in1=XL[:, o0:o0 + n_], op0=ALU.mult, op1=ALU.add),
                                        reads=[st["R_P"], R_sc, st["R_XL"]], writes=[st["R_XL"]])
                            S.dma("sp", XLd[c], XL[:], reads=[st["R_XL"]])
                        else:
                            S.dma("sp", XL[:], XLd[c], writes=[st["R_XL"]])
                            S.dma("sp", HF[:], HFd[c], writes=[st["R_HF"]])
                            S.dma("sp", Gs[:], Gd[c], writes=[st["R_Gs"]])
                        S.op("pool", lambda e: e.tensor_copy(out=XLb[:], in_=XL[:]), reads=[st["R_XL"]], writes=[st["R_XLb"]])

                    def stage2(c):
                        st = sets[c % 2]
                        XLb, A, A2, TI = st["XLb"], st["A"], st["A2"], st["TI"]
                        for (t0, w) in TILES:
                            psr_, prr_ = getps()
                            psi_, pri_ = getps()
                            S.op("pe", lambda e, ps=psr_, t0=t0, w=w: e.matmul(
                                ps[:, 0:w], lhsT=wbf[:, 0, c, :], rhs=XLb[:, t0:t0 + w], start=True, stop=True),
                                reads=[R_w, st["R_XLb"]], writes=[prr_])
                            S.op("pe", lambda e, ps=psi_, t0=t0, w=w: e.matmul(
                                ps[:, 0:w], lhsT=wbf[:, 1, c, :], rhs=XLb[:, t0:t0 + w], start=True, stop=True),
                                reads=[R_w, st["R_XLb"]], writes=[pri_])
                            tb = tstate["i"] % 2
                            tstate["i"] += 1
                            S.op("act", lambda e, ps=psr_, tb=tb, w=w: e.activation(
                                out=trt[:, tb, 0:w], in_=ps[:, 0:w], func=AF.Tanh, scale=0.5,
                                bias=LD[:, l, 0, d, c:c + 1]), reads=[prr_, R_sc], writes=[trr[tb]])
                            S.op("act", lambda e, ps=psi_, t0=t0, w=w: e.activation(
                                out=TI[:, t0:t0 + w], in_=ps[:, 0:w], func=AF.Tanh, scale=0.5,
                                bias=LD[:, l, 1, d, c:c + 1]), reads=[pri_, R_sc], writes=[st["R_TI"]])
                            S.op("act", lambda e, tb=tb, t0=t0, w=w: e.activation(
                                out=A[:, t0:t0 + w], in_=trt[:, tb, 0:w], func=AF.Exp,
                                scale=LD[:, l, 3, d, c:c + 1], bias=LD[:, l, 3, d, c:c + 1]),
                                reads=[trr[tb], R_sc], writes=[st["R_A"]])
                            S.op("act", lambda e, tb=tb, t0=t0, w=w: e.activation(
                                out=A2[:, t0:t0 + w], in_=trt[:, tb, 0:w], func=AF.Exp,
                                scale=LD[:, l, 2, d, c:c + 1], bias=LD[:, l, 2, d, c:c + 1]),
                                reads=[trr[tb], R_sc], writes=[st["R_A2"]])
                        S.op("act", lambda e: e.activation(out=A2[:], in_=A2[:], func=AF.Sqrt, scale=-1.0, bias=1.0),
                             reads=[st["R_A2"]], writes=[st["R_A2"]])

                    def stage3(c):
                        st = sets[c % 2]
                        P, XL, A, A2, TI, HF, Gs, Ub = (st[k] for k in ("P", "XL", "A", "A2", "TI", "HF", "Gs", "Ub"))
                        S.op("dve", lambda e: e.scalar_tensor_tensor(out=TI[:], in0=TI[:], scalar=1.0, in1=XL[:],
                                                                     op0=ALU.add, op1=ALU.mult),
                             reads=[st["R_TI"], st["R_XL"]], writes=[st["R_TI"]])
                        S.op("dve", lambda e: e.scalar_tensor_tensor(out=TI[:], in0=TI[:], scalar=0.5, in1=A2[:],
                                                                     op0=ALU.mult, op1=ALU.mult),
                             reads=[st["R_TI"], st["R_A2"]], writes=[st["R_TI"]])
                        if d == 0:
                            S.op("dve", lambda e: e.tensor_tensor_scan(out=HF[:], data0=A[:], data1=TI[:], initial=0.0,
                                                                       op0=ALU.mult, op1=ALU.add),
                                 reads=[st["R_A"], st["R_TI"]], writes=[st["R_HF"]])
                            S.op("dve", lambda e: e.tensor_copy(out=FIN[:, c:c + 1], in_=HF[:, NT - 1:NT]),
                                 reads=[st["R_HF"]], writes=[R_fin])
                            S.dma("sp", HFd[c], HF[:], reads=[st["R_HF"]])
                        else:
                            S.op("dve", lambda e: e.tensor_tensor_scan(
                                out=P[:, NCTX - 1::-1], data0=A[:, NCTX - 1::-1], data1=TI[:, NCTX - 1::-1],
                                initial=0.0, op0=ALU.mult, op1=ALU.add),
                                reads=[st["R_A"], st["R_TI"], st["R_P"]], writes=[st["R_P"]])
                            S.op("dve", lambda e: e.tensor_tensor_scan(
                                out=P[:, NT - 1:NCTX - 1:-1], data0=A[:, NT - 1:NCTX - 1:-1], data1=TI[:, NT - 1:NCTX - 1:-1],
                                initial=HIN[:, c:c + 1], op0=ALU.mult, op1=ALU.add),
                                reads=[st["R_A"], st["R_TI"], st["R_P"], R_hin], writes=[st["R_P"]])
                            S.op("dve", lambda e: e.tensor_tensor(out=HF[:], in0=HF[:], in1=P[:, 0:NT], op=ALU.add),
                                 reads=[st["R_HF"], st["R_P"]], writes=[st["R_HF"]])
                            S.op("dve", lambda e: e.tensor_tensor(out=Ub[:], in0=HF[:], in1=Gs[:], op=ALU.mult),
                                 reads=[st["R_HF"], st["R_Gs"]], writes=[st["R_Ub"]])
                            S.dma("sp", Ud[c], Ub[:], reads=[st["R_Ub"]])

                    stage1(0)
                    for c in range(KC):
                        if c + 1 < KC:
                            stage1(c + 1)
                        stage2(c)
                        stage3(c)
                    S.barrier()

            rnn_sweep(0)
            R_e2, R_g2, R_d2 = Region(), Region(), Region()
            S.dma("sp", EXs[l], FIN[:], reads=[R_fin], writes=[R_d2])
            S.collective(EXs[l], GXs[l], PAIRS, reads=[R_d2], writes=[R_g2])
            checkpoint(f"rnnA{l}")
            with ExitStack() as es:
                Qt = [sbt(es, f"at_q{i}", [128, 4, 512], BF16) for i in range(2)]
                Qr = [Region(), Region()]
                Pj = [sbt(es, f"at_p{i}", [128, 512], BF16) for i in range(6)]
                Pr = [Region() for _ in range(6)]
                Ot = [sbt(es, f"at_o{i}", [128, 4, 512], BF16) for i in range(2)]
                Or = [Region(), Region()]
                dn = sbt(es, "at_dn", [128, 2, 512], F32)
                dnr = [Region(), Region()]
                pji = 0
                qti = 0
                bi_ = 0
                qtiles = (TILES if need_ctx else TILES[1:])
                abank = dict(s=0, o=0)
                qlist = [(kv, t0, w) for kv in range(2) for (t0, w) in qtiles]

                def q_load(i):
                    kv_, t0_, w_ = qlist[i]
                    S.dma("sp", Qt[i % 2][:, :, 0:w_],
                          Qd[kv_ * 4:(kv_ + 1) * 4, :, t0_:t0_ + w_].rearrange("g p t -> p g t"), writes=[Qr[i % 2]])
                q_load(0)
                for qi_, (kv, t0, w) in enumerate(qlist):
                    if True:
                        qb = qi_ % 2
                        nh = w // 128
                        if qi_ + 1 < len(qlist):
                            q_load(qi_ + 1)
                        for qq in range(nh):
                            q0 = qq * 128
                            tq = t0 + q0
                            if tq < NCTX:
                                keys = [(0, None), (128, None)]
                            else:
                                lb = (tq - NCTX) // 128
                                keys = []
                                if lb > 0:
                                    keys.append((tq - 128, 0))
                                keys.append((tq, None))
                                if lb < NLAT // 128 - 1:
                                    keys.append((tq + 128, 1))
                                else:
                                    keys.append((-1, 2))
                                keys += [(0, None), (128, None)]
                            pjs = []
                            for (k0, mk) in keys:
                                sbank = abank["s"] % 4
                                abank["s"] += 1
                                ps, pr = psb[sbank], psr[sbank]
                                kap = KH[:, kv, :] if k0 < 0 else KT[:, kv, k0:k0 + 128]
                                S.op("pe", lambda e, ps=ps, kap=kap, qb=qb, q0=q0, mk=mk: e.matmul(
                                    ps[:, 0:512], lhsT=kap, rhs=Qt[qb][:, :, q0:q0 + 128],
                                    start=True, stop=(mk is None)), reads=[R_kv, R_halo, Qr[qb]], writes=[pr])
                                if mk is not None:
                                    S.op("pe", lambda e, ps=ps, mk=mk: e.matmul(
                                        ps[:, 0:512], lhsT=ident_bf, rhs=mask_bf[:, mk, :], start=False, stop=True),
                                        reads=[R_const], writes=[pr])
                                pb = pji % 6
                                pji += 1
                                S.op("act", lambda e, ps=ps, pb=pb: e.activation(
                                    out=Pj[pb][:], in_=ps[:, 0:512], func=AF.Exp, scale=float(128 ** -0.5)),
                                    reads=[pr], writes=[Pr[pb]])
                                pjs.append((pb, k0))
                            ob = abank["o"] % 2
                            abank["o"] += 1
                            pso, pro = psb[4 + ob], psr[4 + ob]
                            psd, prd = psb[6 + ob], psr[6 + ob]
                            for ji, (pb, k0) in enumerate(pjs):
                                vap = VH[:, kv, :] if k0 < 0 else VT[:, k0 // 128, kv, :]
                                S.op("pe", lambda e, pso=pso, pb=pb, vap=vap, ji=ji: e.matmul(
                                    pso[:, 0:512], lhsT=vap, rhs=Pj[pb][:],
                                    start=(ji == 0), stop=(ji == len(pjs) - 1)), reads=[R_kv, R_halo, Pr[pb]], writes=[pro])
                            for ji, (pb, k0) in enumerate(pjs):
                                S.op("pe", lambda e, psd=psd, pb=pb, ji=ji: e.matmul(
                                    psd[:, 0:512], lhsT=ones_bf[:], rhs=Pj[pb][:],
                                    start=(ji == 0), stop=(ji == len(pjs) - 1)), reads=[R_const, Pr[pb]], writes=[prd])
                            db = bi_ % 2
                            bi_ += 1
                            for g in range(4):
                                S.op("dve", lambda e, psd=psd, db=db, g=g: e.tensor_scalar(
                                    out=dn[:, db, g * 128:(g + 1) * 128], in0=psd[:, g * 128:(g + 1) * 128],
                                    scalar1=ESK[:, l, kv * 4 + g:kv * 4 + g + 1], scalar2=None, op0=ALU.add),
                                    reads=[prd, R_sc], writes=[dnr[db]])
                            S.op("dve", lambda e, db=db: e.reciprocal(out=dn[:, db, :], in_=dn[:, db, :]),
                                 reads=[dnr[db]], writes=[dnr[db]])
                            S.op("dve", lambda e, pso=pso, db=db, qb=qb, q0=q0: e.tensor_tensor(
                                out=Ot[qb][:, :, q0:q0 + 128], in0=pso[:, 0:512].rearrange("p (g m) -> p g m", g=4),
                                in1=dn[:, db, :].rearrange("p (g m) -> p g m", g=4), op=ALU.mult),
                                reads=[pro, dnr[db]], writes=[Or[qb]])
                        S.dma("sp", Od[kv * 4:(kv + 1) * 4, :, t0:t0 + w].rearrange("g p t -> p g t"), Ot[qb][:, :, 0:w],
                              reads=[Or[qb]])
                S.barrier()

            checkpoint(f"attn{l}")
            S.dma("sp", e_gs[:], GXs[l].rearrange("(r p) n -> p r n", r=2), reads=[R_g2], writes=[R_e2])
            S.op("dve", lambda e: e.tensor_scalar(out=e_ts[:], in0=e_gs[:, 0, :], scalar1=FL[:, 0:1], scalar2=None,
                                                  op0=ALU.mult), reads=[R_e2, R_const], writes=[R_e2])
            S.op("dve", lambda e: e.scalar_tensor_tensor(out=HIN[:], in0=e_gs[:, 1, :], scalar=FL[:, 1:2], in1=e_ts[:],
                                                         op0=ALU.mult, op1=ALU.add),
                 reads=[R_e2, R_const], writes=[R_hin])
            rnn_sweep(1)
            checkpoint(f"rnn{l}")

            halves = [tiles_out]
            for hv in halves:
                h0 = hv[0][0]
                hw = hv[-1][0] + hv[-1][1] - h0
                with ExitStack() as es:
                    Uh = sbt(es, "mg_U", [128, KC, 2304], BF16)
                    Oh = sbt(es, "mg_O", [128, KC, 2304], BF16)
                    Zh = sbt(es, "mg_Z", [128, KC, 2304], BF16)
                    R_U2, R_O2, R_Z = Region(), Region(), Region()
                    WBa = weight_block(es, "mga", KC, 128)
                    WBb = weight_block(es, "mgb", KC, 128)
                    gab = sbt(es, "mg_gab", [128, 2, 2, 512], BF16)
                    gabr = [Region(), Region()]
                    zt = sbt(es, "mg_zt", [128, 2, 512], F32)
                    ztr = [Region(), Region()]
                    mrow = [sbt(es, f"mg_row{i}", [128, 2304], F32) for i in range(2)]
                    mrr = [[Region() for _ in hv] for _ in range(2)]
                    R_Ut = {}
                    R_Ot = {}
                    for (t0, w) in hv:
                        o = t0 - h0
                        R_Ut[t0] = Region()
                        R_Ot[t0] = Region()
                        S.dma("sp", Uh[:, :, o:o + w], Ud[:, :, t0:t0 + w].rearrange("k p t -> p k t"), writes=[R_Ut[t0]])
                        S.dma("sp", Oh[:, :, o:o + w], Od[:, :, t0:t0 + w].rearrange("k p t -> p k t"), writes=[R_Ot[t0]])
                    gi = 0
                    wqa = load_weight(WBa, w_o_rnn[l, :, 0:128])
                    wqb = load_weight(WBb, w_o_attn[l, :, 0:128])
                    for n0 in range(0, KC):
                        wa_t, wa_r = wqa
                        wb_t, wb_r = wqb
                        if n0 + 1 < KC:
                            wqa = load_weight(WBa, w_o_rnn[l, :, (n0 + 1) * 128:(n0 + 2) * 128])
                            wqb = load_weight(WBb, w_o_attn[l, :, (n0 + 1) * 128:(n0 + 2) * 128])
                        else:
                            wqa = load_weight(WBa, w_out[l, :, 0:128])
                        for jn in range(1):
                            n = n0 + jn
                            for (t0, w) in hv:
                                o = t0 - h0
                                gb_ = gi % 2
                                gi += 1
                                S.dma("sp", gab[:, gb_, 0, 0:w], GABd[n, :, t0:t0 + w], writes=[gabr[gb_]])
                                S.dma("sp", gab[:, gb_, 1, 0:w], GABd[8 + n, :, t0:t0 + w], writes=[gabr[gb_]])
                                psa, pra = getps()
                                psb_, prb = getps()
                                for k in range(KC):
                                    S.op("pe", lambda e, k=k, psa=psa, jn=jn, wa_t=wa_t, o=o, w=w: e.matmul(
                                        psa[:, 0:w], lhsT=wa_t[:, k, jn * 128:(jn + 1) * 128], rhs=Uh[:, k, o:o + w],
                                        start=(k == 0), stop=(k == KC - 1)), reads=[wa_r, R_Ut[t0]], writes=[pra])
                                for k in range(KC):
                                    S.op("pe", lambda e, k=k, psb_=psb_, jn=jn, wb_t=wb_t, o=o, w=w: e.matmul(
                                        psb_[:, 0:w], lhsT=wb_t[:, k, jn * 128:(jn + 1) * 128], rhs=Oh[:, k, o:o + w],
                                        start=(k == 0), stop=(k == KC - 1)), reads=[wb_r, R_Ot[t0]], writes=[prb])
                                S.op("dve", lambda e, psa=psa, gb_=gb_, w=w: e.tensor_tensor(
                                    out=zt[:, gb_, 0:w], in0=psa[:, 0:w], in1=gab[:, gb_, 0, 0:w], op=ALU.mult),
                                    reads=[pra, gabr[gb_]], writes=[ztr[gb_]])
                                S.op("dve", lambda e, psb_=psb_, gb_=gb_, w=w: e.tensor_tensor(
                                    out=gab[:, gb_, 1, 0:w], in0=psb_[:, 0:w], in1=gab[:, gb_, 1, 0:w], op=ALU.mult),
                                    reads=[prb, gabr[gb_]], writes=[gabr[gb_]])
                                S.op("pool", lambda e, gb_=gb_, n=n, o=o, w=w: e.tensor_tensor(
                                    out=Zh[:, n, o:o + w], in0=zt[:, gb_, 0:w], in1=gab[:, gb_, 1, 0:w], op=ALU.add),
                                    reads=[ztr[gb_], gabr[gb_]], writes=[R_Z])
                    ri = 0
                    for n0 in range(0, KC):
                        wo_t, wo_r = wqa
                        if n0 + 1 < KC:
                            wqa = load_weight(WBa, w_out[l, :, (n0 + 1) * 128:(n0 + 2) * 128])
                        for jn in range(1):
                            n = n0 + jn
                            rb = ri % 2
                            ri += 1
                            for ti, (t0, w) in enumerate(hv):
                                o = t0 - h0
                                ps, pr = getps()
                                for k in range(KC):
                                    S.op("pe", lambda e, k=k, ps=ps, jn=jn, o=o, w=w: e.matmul(
                                        ps[:, 0:w], lhsT=wo_t[:, k, jn * 128:(jn + 1) * 128], rhs=Zh[:, k, o:o + w],
                                        start=(k == 0), stop=(k == KC - 1)), reads=[wo_r, R_Z], writes=[pr])
                                S.op("act", lambda e, ps=ps, rb=rb, o=o, w=w: e.activation(
                                    out=mrow[rb][:, o:o + w], in_=ps[:, 0:w], func=AF.Copy), reads=[pr], writes=[mrr[rb][ti]])
                            S.dma("sp", Md[n, :, h0:h0 + hw], mrow[rb][:, 0:hw], reads=mrr[rb])
                    S.barrier()

            checkpoint(f"merge{l}")
            with ExitStack() as es2:
                H2 = sbt(es2, "H2", [128, KC, NT], BF16)
                R_H2 = Region()
                post_norm_pass(l, 0, Md, Xcur, XA_l, tiles_out, Hbuf=H2, R_H=R_H2, lnext=l)
                checkpoint(f"post1{l}")
                with ExitStack() as es:
                    WBg = weight_block(es, "f1g", KC, 128, nbuf=3)
                    WBu = weight_block(es, "f1u", KC, 128, nbuf=3)
                    sg = sbt(es, "f1_sg", [128, 2, 512], F32)
                    sgr = [Region(), Region()]
                    arow = [sbt(es, f"f1_row{i}", [128, NT], BF16) for i in range(2)]
                    arr = [[Region() for _ in TILES] for _ in range(2)]
                    si = 0
                    wsG = WStream(WBg, [w_ffn_in[l, :, j * 128:(j + 1) * 128] for j in range(FC)])
                    wsU = WStream(WBu, [w_ffn_in[l, :, DFF + j * 128:DFF + (j + 1) * 128] for j in range(FC)])
                    for j in range(FC):
                        wg_t, wg_r = wsG.get(j)
                        wu_t, wu_r = wsU.get(j)
                        rb = j % 2
                        for ti, (t0, w) in enumerate(TILES):
                            if (t0, w) not in tiles_out:
                                continue
                            psg, prg = getps()
                            psu, pru = getps()
                            for k in range(KC):
                                S.op("pe", lambda e, k=k, psg=psg, t0=t0, w=w: e.matmul(
                                    psg[:, 0:w], lhsT=wg_t[:, k, :], rhs=H2[:, k, t0:t0 + w],
                                    start=(k == 0), stop=(k == KC - 1)), reads=[wg_r, R_H2], writes=[prg])
                            for k in range(KC):
                                S.op("pe", lambda e, k=k, psu=psu, t0=t0, w=w: e.matmul(
                                    psu[:, 0:w], lhsT=wu_t[:, k, :], rhs=H2[:, k, t0:t0 + w],
                                    start=(k == 0), stop=(k == KC - 1)), reads=[wu_r, R_H2], writes=[pru])
                            sb_ = si % 2
                            si += 1
                            S.op("act", lambda e, psg=psg, sb_=sb_, w=w: e.activation(
                                out=sg[:, sb_, 0:w], in_=psg[:, 0:w], func=AF.Silu), reads=[prg], writes=[sgr[sb_]])
                            S.op("dve", lambda e, psu=psu, sb_=sb_, rb=rb, t0=t0, w=w: e.tensor_tensor(
                                out=arow[rb][:, t0:t0 + w], in0=psu[:, 0:w], in1=sg[:, sb_, 0:w], op=ALU.mult),
                                reads=[pru, sgr[sb_]], writes=[arr[rb][ti]])
                        tl0 = tiles_out[0][0]
                        S.dma("sp", ACTd[j, :, tl0:NT], arow[rb][:, tl0:NT], reads=arr[rb])
                    S.barrier()
            checkpoint(f"f1{l}")
            segs = [tiles_out]
            for sg_ in segs:
                s0 = sg_[0][0]
                sw = sg_[-1][0] + sg_[-1][1] - s0
                with ExitStack() as es:
                    As = sbt(es, "f2_A", [128, FC, NT], BF16)
                    R_As = Region()
                    WB2 = weight_block(es, "f2w", FC, 128)
                    yrow = [sbt(es, f"f2_row{i}", [128, NT], F32) for i in range(2)]
                    yrr = [[Region() for _ in sg_] for _ in range(2)]
                    R_Ast = [Region() for _ in sg_]
                    for ti, (t0, w) in enumerate(sg_):
                        o = t0 - s0
                        for k0 in range(0, FC, 8):
                            k1 = min(FC, k0 + 8)
                            S.dma("sp", As[:, k0:k1, o:o + w], ACTd[k0:k1, :, t0:t0 + w].rearrange("k p t -> p k t"),
                                  writes=[R_Ast[ti]])
                    wq2 = load_weight(WB2, w_ffn_out[l, :, 0:128])
                    for n in range(KC):
                        w2_t, w2_r = wq2
                        if n + 1 < KC:
                            wq2 = load_weight(WB2, w_ffn_out[l, :, (n + 1) * 128:(n + 2) * 128])
                        rb = n % 2
                        for ti, (t0, w) in enumerate(sg_):
                            o = t0 - s0
                            ps, pr = getps()
                            for k in range(FC):
                                S.op("pe", lambda e, k=k, ps=ps, o=o, w=w: e.matmul(
                                    ps[:, 0:w], lhsT=w2_t[:, k, :], rhs=As[:, k, o:o + w],
                                    start=(k == 0), stop=(k == FC - 1)), reads=[w2_r, R_Ast[ti]], writes=[pr])
                            S.op("act", lambda e, ps=ps, rb=rb, o=o, w=w: e.activation(
                                out=yrow[rb][:, o:o + w], in_=ps[:, 0:w], func=AF.Copy), reads=[pr], writes=[yrr[rb][ti]])
                        S.dma("sp", Yd[n, :, s0:s0 + sw], yrow[rb][:, 0:sw], reads=yrr[rb])
                    S.barrier()
            checkpoint(f"f2{l}")
            if l < L - 1:
                Hes = ExitStack()
                H = sbt(Hes, "H", [128, KC, NT], BF16)
                R_H = Region()
                post_norm_pass(l, 1, Yd, XA_l, XB_l, tiles_out, Hbuf=H, R_H=R_H, lnext=l + 1)
                Xcur = XB_l
            else:
                post_norm_pass(l, 1, Yd, XA_l, out, tiles_out, final=True)
        S.barrier()
    return nc


def _host_tables(half):
    inv = (10000.0 ** (-np.arange(32, dtype=np.float32) / 32.0)).astype(np.float32)
    t = np.arange(NLAT)
    pos = t if half == 0 else (NLAT_FULL - 1 - t)
    row = (pos // 64).astype(np.float32)
    col = (pos % 64).astype(np.float32)
    C = np.ones((128, NT), np.float32)
    Sg = np.zeros((128, NT), np.float32)
    perm = np.zeros((128, 128), np.float32)
    for p in range(128):
        hf = p // 64
        pp = p % 64
        fi = pp % 32
        ps_ = row if hf == 0 else col
        ang = (ps_ * inv[fi]).astype(np.float32)
        C[p, NCTX:] = np.cos(ang)
        if pp < 32:
            partner, sign = p + 32, -1.0
        else:
            partner, sign = p - 32, 1.0
        Sg[p, NCTX:] = sign * np.sin(ang)
        perm[partner, p] = 1.0
    ident = np.eye(128, dtype=np.float32)
    consts = np.stack([ident, perm, np.zeros_like(ident)], axis=1)
    i = np.arange(128)[:, None]
    m = np.arange(128)[None, :]
    lo = np.where(m <= i, 0.0, NEG).astype(np.float32)
    hi = np.where(i <= m, 0.0, NEG).astype(np.float32)
    halo = np.where((127 - i) <= m, 0.0, NEG).astype(np.float32)
    mask = np.stack([np.tile(lo, (1, 4)), np.tile(hi, (1, 4)), np.tile(halo, (1, 4))], axis=1)
    return C, Sg, np.ascontiguousarray(consts), np.ascontiguousarray(mask)


def _pc(v, nchunk):
    v = np.asarray(v, np.float32)
    lead = v.shape[:-1]
    v = v.reshape(lead + (nchunk, 128))
    return np.ascontiguousarray(np.moveaxis(v, -1, 0))


def make_in_maps(x, c, ctx, c_ctx, w_mod, b_mod, g_mix_pre, g_mix_post, g_ffn_pre, g_ffn_post, w_in, conv_w,
                 conv_b, lru_wa, lru_ba, lru_wx, lru_bx, lru_lam, attn_sink, w_o_rnn, w_o_attn, w_out,
                 w_ffn_in, w_ffn_out, n_cores=8):
    f = lambda a: np.ascontiguousarray(np.asarray(a, np.float32))
    bmod = np.stack([_pc(b_mod[l], 48) for l in range(L)])
    gvec = np.stack([np.stack([_pc(g[l], KC) for g in (g_mix_pre, g_mix_post, g_ffn_pre, g_ffn_post)], axis=1)
                     for l in range(L)])
    convb = np.stack([_pc(conv_b[l], KC) for l in range(L)])
    sinkbc = np.ascontiguousarray(np.broadcast_to(np.asarray(attn_sink, np.float32)[:, None, :], (L, 128, KC)))
    cw = np.asarray(conv_w, np.float32)
    zero = np.zeros_like(cw[:, :1])
    shared = dict(w_mod=f(w_mod), bmod=f(bmod), gvec=f(gvec), w_in=f(w_in), convb=f(convb),
                  sinkbc=f(sinkbc), w_o_rnn=f(w_o_rnn), w_o_attn=f(w_o_attn), w_out=f(w_out),
                  w_ffn_in=f(w_ffn_in), w_ffn_out=f(w_ffn_out))
    per_half = []
    for half in range(2):
        C, Sg, consts, mask = _host_tables(half)
        if half == 0:
            taps = np.concatenate([cw, zero], axis=1)
            dsel = [0, 1]
        else:
            taps = np.concatenate([zero, cw[:, ::-1]], axis=1)
            dsel = [1, 0]
        convw = np.stack([_pc(taps[l], KC) for l in range(L)])
        lruv = np.stack([np.stack([_pc(np.asarray(a, np.float32)[l][dsel], KC) for a in (lru_ba, lru_bx, lru_lam)],
                                  axis=1) for l in range(L)])
        wa = f(np.asarray(lru_wa, np.float32)[:, dsel])
        wx = f(np.asarray(lru_wx, np.float32)[:, dsel])
        flags = np.zeros((128, 2), np.float32)
        flags[:, 1 - half] = 1.0
        per_half.append(dict(ropeC=C, ropeS=Sg, consts=consts, maskin=mask, convw=f(convw), lruv=f(lruv),
                             lru_wa=wa, lru_wx=wx, flags=flags))
    maps = []
    for core in range(n_cores):
        b = core % 4
        half = core // 4
        xl = np.asarray(x[b], np.float32)[half * NLAT:(half + 1) * NLAT]
        cx = np.asarray(ctx[b], np.float32)
        if half == 1:
            xl = xl[::-1]
            cx = cx[::-1]
        xcat = np.concatenate([cx, xl], axis=0)
        xin = np.ascontiguousarray(xcat.T).reshape(KC, 128, NT)
        cv = np.stack([np.asarray(c[b], np.float32), np.asarray(c_ctx, np.float32)], axis=-1)
        cvec = np.ascontiguousarray(cv.reshape(KC, 128, 2).transpose(1, 0, 2))
        m = dict(shared)
        m.update(per_half[half])
        m["xin"] = xin
        m["cvec"] = cvec
        maps.append(m)
    return maps


_NC_CACHE = {}


def kernel(**inputs):
    if "nc" not in _NC_CACHE:
        _NC_CACHE["nc"] = build(debug=False)
    nc = _NC_CACHE["nc"]
    maps = make_in_maps(**inputs)
    res = run_bass_kernel_spmd(nc, maps, core_ids=list(range(8)))
    out = np.empty((4, NLAT_FULL, D), np.float32)
    for core in range(8):
        b = core % 4
        half = core // 4
        o = np.asarray(res.results[core]["out"], np.float32).reshape(D, NLAT).T
        if half == 1:
            o = o[::-1]
        out[b, half * NLAT:(half + 1) * NLAT] = o
    return out
```

```python
import numpy as np
from contextlib import ExitStack
import concourse.bass as bass
import concourse.mybir as mybir
from concourse.bass_utils import run_bass_kernel_spmd

F32 = mybir.dt.float32
BF16 = mybir.dt.bfloat16
AF = mybir.ActivationFunctionType
ALU = mybir.AluOpType

D = 1024
KC = 8
NCTX = 256
NLAT = 2048
NLAT_FULL = 4096
NT = NCTX + NLAT
DFF = 2816
FC = 22
INW = 5632
L = 2
EPS = 1e-6
NEG = -30000.0
TILES = [(0, 256)] + [(256 + 512 * i, 512) for i in range(4)]
NBLK = NT // 128
PW = NT + 8
LP0 = NCTX + 4


class Region:
    __slots__ = ("w", "rs", "excl")

    def __init__(self, excl=False):
        self.w = None
        self.rs = []
        self.excl = excl


class Sched:
    def __init__(self, nc, n_dma_sems=12):
        self.nc = nc
        self.engs = {}
        for name, h in [("pe", nc.tensor), ("act", nc.scalar), ("dve", nc.vector),
                        ("pool", nc.gpsimd), ("sp", nc.sync)]:
            sem = nc.alloc_semaphore(name="s_" + name)
            self.engs[name] = dict(h=h, sem=sem, cnt=0, known={}, key="e_" + name, name=name)
        self.dq = {}
        for q in ("sp", "pool"):
            self.dq[q] = dict(slots=[dict(sem=nc.alloc_semaphore(name=f"d_{q}{i}"), tot=0, key=f"d_{q}{i}")
                                     for i in range(n_dma_sems)], nxt=0)

    def _collect(self, reads, writes):
        deps = {}

        def add(tok):
            if tok is None:
                return
            key, sem, val = tok
            if key not in deps or deps[key][1] < val:
                deps[key] = (sem, val)
        for r in reads:
            add(r.w)
        for w in writes:
            add(w.w)
            for t in w.rs:
                add(t)
        return deps

    def _wait(self, E, deps):
        for key, (sem, val) in deps.items():
            if key == E["key"] and E["name"] == "pe":
                continue
            if E["known"].get(key, 0) >= val:
                continue
            E["h"].wait_ge(sem, val)
            E["known"][key] = val

    def _commit(self, tok, reads, writes):
        for r in reads:
            r.rs.append(tok)
            if len(r.rs) > 48:
                best = {}
                for t in r.rs:
                    if t[0] not in best or best[t[0]][2] < t[2]:
                        best[t[0]] = t
                r.rs = list(best.values())
        for w in writes:
            w.w = tok
            w.rs = []

    def op(self, eng, fn, reads=(), writes=()):
        if any(r.excl for r in reads):
            writes = list(writes) + [r for r in reads if r.excl]
            reads = [r for r in reads if not r.excl]
        E = self.engs[eng]
        self._wait(E, self._collect(reads, writes))
        inst = fn(E["h"])
        E["cnt"] += 1
        inst.then_inc(E["sem"], 1)
        self._commit((E["key"], E["sem"], E["cnt"]), reads, writes)

    def dma(self, q, out, in_, reads=(), writes=()):
        E = self.engs[q]
        Dq = self.dq[q]
        slot = Dq["slots"][Dq["nxt"]]
        Dq["nxt"] = (Dq["nxt"] + 1) % len(Dq["slots"])
        deps = self._collect(reads, writes)
        if slot["tot"] > 0:
            k = slot["key"]
            if k not in deps or deps[k][1] < slot["tot"]:
                deps[k] = (slot["sem"], slot["tot"])
        self._wait(E, deps)
        inst = E["h"].dma_start(out=out, in_=in_)
        slot["tot"] += 16
        inst.then_inc(slot["sem"], 16)
        self._commit((slot["key"], slot["sem"], slot["tot"]), reads, writes)

    def collective(self, in_ap, out_ap, groups, reads=(), writes=()):
        E = self.engs["pool"]
        if not hasattr(self, "cc"):
            self.cc = dict(sem=self.nc.alloc_semaphore(name="s_cc"), tot=0, key="cc")
        cc = self.cc
        deps = self._collect(reads, writes)
        if cc["tot"] > 0:
            deps["cc"] = (cc["sem"], cc["tot"])
        self._wait(E, deps)
        inst = E["h"].collective_compute("AllGather", ALU.bypass, replica_groups=groups,
                                         ins=[in_ap.opt()], outs=[out_ap.opt()])
        cc["tot"] += 1
        inst.then_inc(cc["sem"], 1)
        self._commit(("cc", cc["sem"], cc["tot"]), reads, writes)

    def barrier(self, engines=("pe", "act", "dve", "pool", "sp")):
        if getattr(self, "report", False):
            print("barrier: sbuf_bytes_remaining", self.nc.sbuf_bytes_remaining)
        deps = {}
        for name, e in self.engs.items():
            if e["cnt"] > 0:
                deps[e["key"]] = (e["sem"], e["cnt"])
        for q, Dq in self.dq.items():
            for s in Dq["slots"]:
                if s["tot"] > 0:
                    deps[s["key"]] = (s["sem"], s["tot"])
        if hasattr(self, "cc") and self.cc["tot"] > 0:
            deps["cc"] = (self.cc["sem"], self.cc["tot"])
        for name in engines:
            E = self.engs[name]
            for key, (sem, val) in deps.items():
                if key == E["key"]:
                    continue
                if E["known"].get(key, 0) >= val:
                    continue
                E["h"].wait_ge(sem, val)
                E["known"][key] = val


class _Stop(Exception):
    pass


def build(debug=False, stop=None):
    holder = {}
    try:
        return _build(debug, stop, holder)
    except _Stop:
        return holder['nc']


def _build(debug, stop, holder):
    nc = bass.Bass("TRN2", target_bir_lowering=False)
    holder['nc'] = nc

    def din(name, shape, dt=F32):
        return nc.dram_tensor(name, list(shape), dt, kind="ExternalInput").ap()

    def dscr(name, shape, dt):
        dbg = debug and name in ("Ud", "Od", "Md", "XB")
        return nc.dram_tensor(name, list(shape), dt, kind="ExternalOutput" if dbg else "Internal").ap()

    xin = din("xin", [KC, 128, NT])
    cvec = din("cvec", [128, KC, 2])
    w_mod = din("w_mod", [L, D, 6 * D])
    bmod = din("bmod", [L, 128, 48])
    gvec = din("gvec", [L, 128, 4, KC])
    w_in = din("w_in", [L, D, INW])
    convw = din("convw", [L, 128, 5, KC])
    flags = din("flags", [128, 2])
    convb = din("convb", [L, 128, KC])
    lru_wa = din("lru_wa", [L, 2, KC, 128, 128])
    lru_wx = din("lru_wx", [L, 2, KC, 128, 128])
    lruv = din("lruv", [L, 128, 3, 2, KC])
    sinkbc = din("sinkbc", [L, 128, KC])
    w_o_rnn = din("w_o_rnn", [L, D, D])
    w_o_attn = din("w_o_attn", [L, D, D])
    w_out = din("w_out", [L, D, D])
    w_ffn_in = din("w_ffn_in", [L, D, 2 * DFF])
    w_ffn_out = din("w_ffn_out", [L, DFF, D])
    ropeC = din("ropeC", [128, NT])
    ropeS = din("ropeS", [128, NT])
    consts = din("consts", [128, 3, 128])
    maskin = din("maskin", [128, 3, 512])
    out = nc.dram_tensor("out", [KC, 128, NLAT], F32, kind="ExternalOutput").ap()

    XA = dscr("XA", [KC, 128, NT], F32)
    XB = dscr("XB", [KC, 128, NT], F32)
    XRd = dscr("XRd", [KC, 128, NT], F32)
    Gd = dscr("Gd", [KC, 128, NT], BF16)
    Qd = dscr("Qd", [KC, 128, NT], BF16)
    GABd = dscr("GABd", [16, 128, NT], BF16)
    Ud = dscr("Ud", [KC, 128, NT], BF16)
    Od = dscr("Od", [KC, 128, NT], BF16)
    Md = dscr("Md", [KC, 128, NT], F32)
    ACTd = dscr("ACTd", [FC, 128, NT], BF16)
    Yd = dscr("Yd", [KC, 128, NT], F32)
    XLd = dscr("XLd", [KC, 128, NT], F32)
    HFd = dscr("HFd", [KC, 128, NT], F32)

    def dint(name, shape, dt):
        return nc.dram_tensor(name, list(shape), dt, kind="Internal").ap()
    EXx = [dint(f"EXx{l}", [128, 16], F32) for l in range(L)]
    GXx = [dint(f"GXx{l}", [256, 16], F32) for l in range(L)]
    EXk = [dint(f"EXk{l}", [128, 256], BF16) for l in range(L)]
    GXk = [dint(f"GXk{l}", [256, 256], BF16) for l in range(L)]
    EXv = [dint(f"EXv{l}", [128, 256], BF16) for l in range(L)]
    GXv = [dint(f"GXv{l}", [256, 256], BF16) for l in range(L)]
    EXs = [dint(f"EXs{l}", [128, 8], F32) for l in range(L)]
    GXs = [dint(f"GXs{l}", [256, 8], F32) for l in range(L)]
    PAIRS = [[0, 4], [1, 5], [2, 6], [3, 7]]

    top = ExitStack()
    with top:
        S = Sched(nc)
        nc._S = S

        nc._marks = []

        def checkpoint(name):
            nc._marks.append((name, S.engs["pe"]["cnt"]))
            if stop == name:
                S.barrier()
                raise _Stop()

        ncnt = dict(i=0)

        def sbt(es, name, shape, dt):
            ncnt["i"] += 1
            return es.enter_context(nc.sbuf_tensor(f"{name}_{ncnt['i']}", list(shape), dt))

        psb = [top.enter_context(nc.psum_tensor(f"ps{i}", [128, 512], F32)) for i in range(8)]
        psr = [Region(excl=True) for _ in range(8)]
        pstate = dict(i=0)

        def getps():
            i = pstate["i"]
            pstate["i"] = (i + 1) % 8
            return psb[i], psr[i]

        ident_f = sbt(top, "ident_f", [128, 2, 128], F32)
        cbf = sbt(top, "cbf", [128, 2, 128], BF16)
        ones_bf = sbt(top, "ones_bf", [128, 128], BF16)
        mask_f = sbt(top, "mask_f", [128, 3, 512], F32)
        mask_bf = sbt(top, "mask_bf", [128, 3, 512], BF16)
        FL = sbt(top, "FL", [128, 2], F32)
        KH = sbt(top, "KH", [128, 2, 128], BF16)
        VH = sbt(top, "VH", [128, 2, 128], BF16)
        HAL = sbt(top, "HAL", [128, KC, 2], F32)
        HIN = sbt(top, "HIN", [128, KC], F32)
        FIN = sbt(top, "FIN", [128, KC], F32)
        R_halo = Region()
        R_hin = Region()
        R_fin = Region()
        e_xh = sbt(top, "e1_xh", [128, KC, 2], F32)
        e_gx = sbt(top, "e1_gx", [128, 2, 16], F32)
        e_gk = sbt(top, "e1_gk", [128, 2, 256], BF16)
        e_gv = sbt(top, "e1_gv", [128, 2, 256], BF16)
        e_tk = sbt(top, "e1_tk", [128, 256], F32)
        e_gs = sbt(top, "e2_gs", [128, 2, 8], F32)
        e_ts = sbt(top, "e2_ts", [128, 8], F32)
        R_xrd = Region()
        MOD = sbt(top, "MOD", [128, L, 48, 2], F32)
        GV = sbt(top, "GV", [128, L, 4, KC], F32)
        SC = sbt(top, "SC", [128, L, 6, KC, 2], F32)
        CW = sbt(top, "CW", [128, L, 5, KC], F32)
        CB = sbt(top, "CB", [128, L, KC], F32)
        LV = sbt(top, "LV", [128, L, 3, 2, KC], F32)
        LD = sbt(top, "LD", [128, L, 4, 2, KC], F32)
        ESK = sbt(top, "ESK", [128, L, KC], F32)
        KT = sbt(top, "KT", [128, 2, NT], BF16)
        VT = sbt(top, "VT", [128, NBLK, 2, 128], BF16)
        R_const = Region()
        R_sc = Region()
        R_kv = Region()

        S.dma("sp", ident_f[:], consts[:, 0:2, :], writes=[R_const])
        S.dma("sp", mask_f[:], maskin, writes=[R_const])
        S.dma("sp", FL[:], flags, writes=[R_const])
        S.op("dve", lambda e: e.tensor_copy(out=cbf[:], in_=ident_f[:]), reads=[R_const], writes=[R_const])
        S.op("dve", lambda e: e.tensor_copy(out=mask_bf[:], in_=mask_f[:]), reads=[R_const], writes=[R_const])
        S.op("dve", lambda e: e.memset(ones_bf[:], 1.0), writes=[R_const])
        S.dma("sp", GV[:], gvec.rearrange("l p a k -> p l a k"), writes=[R_sc])
        S.dma("sp", CW[:], convw.rearrange("l p a k -> p l a k"), writes=[R_sc])
        S.dma("sp", CB[:], convb.rearrange("l p k -> p l k"), writes=[R_sc])
        S.dma("sp", LV[:], lruv.rearrange("l p a d k -> p l a d k"), writes=[R_sc])
        S.dma("sp", ESK[:], sinkbc.rearrange("l p k -> p l k"), writes=[R_sc])
        BM = sbt(top, "BM", [128, L, 48], F32)
        S.dma("sp", BM[:], bmod.rearrange("l p n -> p l n"), writes=[R_sc])

        ident_bf = cbf[:, 0, :]
        perm_bf = cbf[:, 1, :]

        with ExitStack() as es:
            cv = sbt(es, "cv", [128, KC, 2], F32)
            scv = sbt(es, "scv", [128, KC, 2], F32)
            wst = [sbt(es, f"wmst{i}", [128, KC, 512], F32) for i in range(2)]
            wsr = [Region(), Region()]
            R_cv = Region()
            S.dma("sp", cv[:], cvec, writes=[R_cv])
            S.op("act", lambda e: e.activation(out=scv[:], in_=cv[:], func=AF.Silu), reads=[R_cv], writes=[R_cv])
            it = 0
            for l in range(L):
                ps, pr = getps()
                for grp in range(12):
                    b = it % 2
                    it += 1
                    S.dma("sp", wst[b][:],
                          w_mod[l, :, grp * 512:(grp + 1) * 512].rearrange("(k p) n -> p k n", p=128),
                          writes=[wsr[b]])
                    for j in range(4):
                        n = grp * 4 + j
                        for kc in range(KC):
                            S.op("pe", lambda e, n=n, kc=kc, b=b, j=j, ps=ps: e.matmul(
                                ps[:, 2 * n:2 * n + 2], lhsT=wst[b][:, kc, j * 128:(j + 1) * 128],
                                rhs=scv[:, kc, :], start=(kc == 0), stop=(kc == KC - 1)),
                                reads=[wsr[b], R_cv], writes=[pr])
                for v in range(2):
                    S.op("dve", lambda e, l=l, ps=ps, v=v: e.tensor_tensor(
                        out=MOD[:, l, :, v], in0=ps[:, 0:96].rearrange("p (n v) -> p n v", v=2)[:, :, v],
                        in1=BM[:, l, :], op=ALU.add), reads=[pr, R_sc], writes=[R_sc])
            for l in range(L):
                for half, (gi_pre, gi_post) in enumerate([(0, 1), (2, 3)]):
                    base = half * 24
                    for v in range(2):
                        S.op("dve", lambda e, l=l, half=half, v=v, base=base, gi_pre=gi_pre: e.scalar_tensor_tensor(
                            out=SC[:, l, 3 * half + 0, :, v], in0=MOD[:, l, base + 8:base + 16, v], scalar=1.0,
                            in1=GV[:, l, gi_pre, :], op0=ALU.add, op1=ALU.mult), reads=[R_sc], writes=[R_sc])
                        S.op("dve", lambda e, l=l, half=half, v=v, base=base: e.tensor_copy(
                            out=SC[:, l, 3 * half + 1, :, v], in_=MOD[:, l, base:base + 8, v]), reads=[R_sc], writes=[R_sc])
                        S.op("dve", lambda e, l=l, half=half, v=v, base=base, gi_post=gi_post: e.tensor_tensor(
                            out=SC[:, l, 3 * half + 2, :, v], in0=MOD[:, l, base + 16:base + 24, v],
                            in1=GV[:, l, gi_post, :], op=ALU.mult), reads=[R_sc], writes=[R_sc])
                S.op("dve", lambda e, l=l: e.tensor_scalar(out=LD[:, l, 0:2], in0=LV[:, l, 0:2], scalar1=0.5, scalar2=None,
                                                          op0=ALU.mult), reads=[R_sc], writes=[R_sc])
                S.op("act", lambda e, l=l: e.activation(out=LD[:, l, 2], in_=LV[:, l, 2], func=AF.Exp, scale=-1.0),
                     reads=[R_sc], writes=[R_sc])
                S.op("act", lambda e, l=l: e.activation(out=LD[:, l, 2], in_=LD[:, l, 2], func=AF.Ln, bias=1.0),
                     reads=[R_sc], writes=[R_sc])
                S.op("dve", lambda e, l=l: e.tensor_scalar(out=LD[:, l, 3], in0=LD[:, l, 2], scalar1=-4.0, scalar2=None,
                                                          op0=ALU.mult), reads=[R_sc], writes=[R_sc])
                S.op("dve", lambda e, l=l: e.tensor_scalar(out=LD[:, l, 2], in0=LD[:, l, 2], scalar1=-8.0, scalar2=None,
                                                          op0=ALU.mult), reads=[R_sc], writes=[R_sc])
                S.op("act", lambda e, l=l: e.activation(out=ESK[:, l], in_=ESK[:, l], func=AF.Exp),
                     reads=[R_sc], writes=[R_sc])
            S.barrier()

        def norm_tile(es_name, l, which, xt, xr_reg, t0, w, v, Hdst, Hreg, tmp, rsd, sq):
            S.op("act", lambda e: e.activation(out=sq[:, :, 0:w], in_=xt[:, :, 0:w], func=AF.Square),
                 reads=[xr_reg], writes=[tmp["sq"]])
            ps, pr = getps()
            for k in range(KC):
                S.op("pe", lambda e, k=k: e.matmul(ps[:, 0:w], lhsT=ones_bf[:], rhs=sq[:, k, 0:w],
                                                   start=(k == 0), stop=(k == KC - 1)),
                     reads=[tmp["sq"], R_const], writes=[pr])
            S.op("act", lambda e: e.activation(out=rsd[:, 0:w], in_=ps[:, 0:w], func=AF.Sqrt, bias=EPS, scale=1.0 / D),
                 reads=[pr], writes=[tmp["rs"]])
            S.op("dve", lambda e: e.reciprocal(out=rsd[:, 0:w], in_=rsd[:, 0:w]), reads=[tmp["rs"]], writes=[tmp["rs"]])
            return

        def weight_block(es, name, kc, cols, nbuf=2):
            st = [sbt(es, f"{name}_st{i}", [128, kc, cols], F32) for i in range(nbuf)]
            wb = [sbt(es, f"{name}_wb{i}", [128, kc, cols], BF16) for i in range(nbuf)]
            return dict(st=st, wb=wb, sr=[Region() for _ in range(nbuf)], wr=[Region() for _ in range(nbuf)], i=0,
                        nbuf=nbuf)

        class WStream:
            def __init__(self, WB, srcs):
                self.WB, self.srcs, self.q, self.n = WB, srcs, {}, 0

            def get(self, i):
                dist = self.WB["nbuf"] - 1
                while self.n < len(self.srcs) and self.n <= i + dist:
                    self.q[self.n] = load_weight(self.WB, self.srcs[self.n])
                    self.n += 1
                return self.q.pop(i)

        def load_weight(WB, src_ap):
            b = WB["i"] % WB["nbuf"]
            WB["i"] += 1
            srcv = src_ap.rearrange("(k p) n -> p k n", p=128)
            kct = WB["st"][b].shape[1]
            for k0 in range(0, kct, 8):
                k1 = min(kct, k0 + 8)
                S.dma("sp", WB["st"][b][:, k0:k1, :], srcv[:, k0:k1, :], writes=[WB["sr"][b]])
            S.op("pool", lambda e: e.tensor_copy(out=WB["wb"][b][:], in_=WB["st"][b][:]),
                 reads=[WB["sr"][b]], writes=[WB["wr"][b]])
            return WB["wb"][b], WB["wr"][b]

        def pre_norm_pass(l, which, Xsrc, Hbuf, R_H, tiles):
            with ExitStack() as es:
                xt = [sbt(es, f"pn_x{i}", [128, KC, 512], F32) for i in range(3)]
                xr = [Region(), Region(), Region()]
                sq = sbt(es, "pn_sq", [128, KC, 512], BF16)
                rsd = sbt(es, "pn_rs", [128, 512], F32)
                tm = sbt(es, "pn_tm", [128, 2, 512], F32)
                tmp = dict(sq=Region(), rs=Region(), tm=[Region(), Region()])
                def pn_load(ti):
                    t0, w = tiles[ti]
                    b = ti % 3
                    S.dma("sp", xt[b][:, :, 0:w], Xsrc[:, :, t0:t0 + w].rearrange("k p t -> p k t"), writes=[xr[b]])
                pn_load(0)
                if len(tiles) > 1:
                    pn_load(1)
                for ti, (t0, w) in enumerate(tiles):
                    b = ti % 3
                    v = 1 if t0 < NCTX else 0
                    if ti + 2 < len(tiles):
                        pn_load(ti + 2)
                    norm_tile("pn", l, which, xt[b], xr[b], t0, w, v, Hbuf, R_H, tmp, rsd, sq)
                    for k in range(KC):
                        j = k % 2
                        S.op("dve", lambda e, k=k, j=j, b=b, w=w: e.tensor_tensor(
                            out=tm[:, j, 0:w], in0=xt[b][:, k, 0:w], in1=rsd[:, 0:w], op=ALU.mult),
                            reads=[xr[b], tmp["rs"]], writes=[tmp["tm"][j]])
                        S.op("act", lambda e, k=k, j=j, w=w, t0=t0, v=v: e.activation(
                            out=Hbuf[:, k, t0:t0 + w], in_=tm[:, j, 0:w], func=AF.Identity,
                            scale=SC[:, l, 3 * which + 0, k, v:v + 1], bias=SC[:, l, 3 * which + 1, k, v:v + 1]),
                            reads=[tmp["tm"][j], R_sc], writes=[R_H])
                S.barrier()

        def post_norm_pass(l, which, Ysrc, Xsrc, Xdst, tiles, Hbuf=None, R_H=None, lnext=None, final=False):
            with ExitStack() as es:
                yt = [sbt(es, f"po_y{i}", [128, KC, 512], F32) for i in range(3)]
                xt = [sbt(es, f"po_x{i}", [128, KC, 512], F32) for i in range(3)]
                yr = [Region(), Region(), Region()]
                xr = [Region(), Region(), Region()]
                sq = sbt(es, "po_sq", [128, KC, 512], BF16)
                rsd = sbt(es, "po_rs", [128, 512], F32)
                rsd2 = sbt(es, "po_rs2", [128, 512], F32)
                tm = sbt(es, "po_tm", [128, 2, 512], F32)
                tm2 = sbt(es, "po_tm2", [128, 2, 512], F32)
                tm2r = [Region(), Region()]
                tmp = dict(sq=Region(), rs=Region(), tm=[Region(), Region()])
                tmp2 = dict(sq=tmp["sq"], rs=Region(), tm=tmp["tm"])
                def po_load(ti):
                    t0, w = tiles[ti]
                    b = ti % 3
                    S.dma("sp", yt[b][:, :, 0:w], Ysrc[:, :, t0:t0 + w].rearrange("k p t -> p k t"), writes=[yr[b]])
                    S.dma("sp", xt[b][:, :, 0:w], Xsrc[:, :, t0:t0 + w].rearrange("k p t -> p k t"), writes=[xr[b]])
                po_load(0)
                if len(tiles) > 1:
                    po_load(1)
                for ti, (t0, w) in enumerate(tiles):
                    b = ti % 3
                    v = 1 if t0 < NCTX else 0
                    if ti + 2 < len(tiles):
                        po_load(ti + 2)
                    norm_tile("po", l, which, yt[b], yr[b], t0, w, v, None, None, tmp, rsd, sq)
                    for k in range(KC):
                        j = k % 2
                        S.op("dve", lambda e, k=k, j=j, b=b, w=w: e.tensor_tensor(
                            out=tm[:, j, 0:w], in0=yt[b][:, k, 0:w], in1=rsd[:, 0:w], op=ALU.mult),
                            reads=[yr[b], tmp["rs"]], writes=[tmp["tm"][j]])
                        S.op("dve", lambda e, k=k, j=j, b=b, w=w, v=v: e.scalar_tensor_tensor(
                            out=xt[b][:, k, 0:w], in0=tm[:, j, 0:w], scalar=SC[:, l, 3 * which + 2, k, v:v + 1],
                            in1=xt[b][:, k, 0:w], op0=ALU.mult, op1=ALU.add),
                            reads=[tmp["tm"][j], R_sc, xr[b]], writes=[xr[b]])
                    if final:
                        S.dma("sp", Xdst[:, :, t0 - NCTX:t0 - NCTX + w].rearrange("k p t -> p k t"), xt[b][:, :, 0:w],
                              reads=[xr[b]])
                    else:
                        S.dma("sp", Xdst[:, :, t0:t0 + w].rearrange("k p t -> p k t"), xt[b][:, :, 0:w], reads=[xr[b]])
                    if Hbuf is not None:
                        wn = 1 - which
                        norm_tile("po2", lnext, wn, xt[b], xr[b], t0, w, v, Hbuf, R_H, tmp2, rsd2, sq)
                        for k in range(KC):
                            j = k % 2
                            S.op("dve", lambda e, k=k, j=j, b=b, w=w: e.tensor_tensor(
                                out=tm2[:, j, 0:w], in0=xt[b][:, k, 0:w], in1=rsd2[:, 0:w], op=ALU.mult),
                                reads=[xr[b], tmp2["rs"]], writes=[tm2r[j]])
                            S.op("act", lambda e, k=k, j=j, w=w, t0=t0, v=v, wn=wn: e.activation(
                                out=Hbuf[:, k, t0:t0 + w], in_=tm2[:, j, 0:w], func=AF.Identity,
                                scale=SC[:, lnext, 3 * wn + 0, k, v:v + 1], bias=SC[:, lnext, 3 * wn + 1, k, v:v + 1]),
                                reads=[tm2r[j], R_sc], writes=[R_H])
                S.barrier()

        Hes = ExitStack()
        H = sbt(Hes, "H", [128, KC, NT], BF16)
        R_H = Region()
        checkpoint("mod")
        pre_norm_pass(0, 0, xin, H, R_H, TILES)
        checkpoint("pn0")
        Xcur = xin
        for l in range(L):
            need_ctx = l < L - 1
            tiles_out = TILES if need_ctx else TILES[1:]
            XA_l, XB_l = XA, XB

            R_e, R_g, R_d = Region(), Region(), Region()

            def e1a():
                S.dma("sp", e_xh[:], XRd[:, :, NT - 2:NT].rearrange("c p t -> p c t"), reads=[R_xrd], writes=[R_e])
                S.dma("sp", EXx[l], e_xh[:].rearrange("p c t -> p (c t)"), reads=[R_e], writes=[R_d])
                S.dma("sp", EXk[l].rearrange("p (h t) -> p h t", h=2), KT[:, :, NT - 128:NT], reads=[R_kv], writes=[R_d])
                S.dma("sp", EXv[l].rearrange("p (h t) -> p h t", h=2), VT[:, NBLK - 1, :, :], reads=[R_kv], writes=[R_d])
                S.collective(EXx[l], GXx[l], PAIRS, reads=[R_d], writes=[R_g])
                S.collective(EXk[l], GXk[l], PAIRS, reads=[R_d], writes=[R_g])
                S.collective(EXv[l], GXv[l], PAIRS, reads=[R_d], writes=[R_g])

            def e1b():
                gx, gk, gv, tk = e_gx, e_gk, e_gv, e_tk
                S.dma("sp", gx[:], GXx[l].rearrange("(r p) n -> p r n", r=2), reads=[R_g], writes=[R_e])
                S.dma("sp", gk[:], GXk[l].rearrange("(r p) n -> p r n", r=2), reads=[R_g], writes=[R_e])
                S.dma("sp", gv[:], GXv[l].rearrange("(r p) n -> p r n", r=2), reads=[R_g], writes=[R_e])
                S.op("dve", lambda e: e.tensor_scalar(out=tk[:, 0:16], in0=gx[:, 0, :], scalar1=FL[:, 0:1], scalar2=None,
                                                      op0=ALU.mult), reads=[R_e, R_const], writes=[R_e])
                S.op("dve", lambda e: e.scalar_tensor_tensor(out=HAL[:].rearrange("p c t -> p (c t)"), in0=gx[:, 1, :],
                                                             scalar=FL[:, 1:2], in1=tk[:, 0:16], op0=ALU.mult, op1=ALU.add),
                     reads=[R_e, R_const], writes=[R_halo])
                S.op("dve", lambda e: e.tensor_scalar(out=tk[:], in0=gk[:, 0, :], scalar1=FL[:, 0:1], scalar2=None,
                                                      op0=ALU.mult), reads=[R_e, R_const, R_halo], writes=[R_e])
                S.op("dve", lambda e: e.scalar_tensor_tensor(out=KH[:].rearrange("p h t -> p (h t)"), in0=gk[:, 1, :],
                                                             scalar=FL[:, 1:2], in1=tk[:], op0=ALU.mult, op1=ALU.add),
                     reads=[R_e, R_const], writes=[R_halo])
                S.op("dve", lambda e: e.tensor_scalar(out=tk[:], in0=gv[:, 0, :], scalar1=FL[:, 0:1], scalar2=None,
                                                      op0=ALU.mult), reads=[R_e, R_const, R_halo], writes=[R_e])
                S.op("dve", lambda e: e.scalar_tensor_tensor(out=VH[:].rearrange("p h t -> p (h t)"), in0=gv[:, 1, :],
                                                             scalar=FL[:, 1:2], in1=tk[:], op0=ALU.mult, op1=ALU.add),
                     reads=[R_e, R_const], writes=[R_halo])

            with ExitStack() as es:
                WB = weight_block(es, "pin", KC, 256, nbuf=3)
                rows = [sbt(es, f"pin_row{i}", [128, NT], F32) for i in range(2)]
                rrs = [[Region() for _ in TILES] for _ in range(2)]
                rtb = sbt(es, "pin_rt", [128, 2, NT], F32)
                R_rt = Region()
                S.dma("sp", rtb[:, 0, :], ropeC, writes=[R_rt])
                S.dma("sp", rtb[:, 1, :], ropeS, writes=[R_rt])
                qbf = sbt(es, "pin_qbf", [128, 2, 512], BF16)
                qbr = [Region(), Region()]
                t12 = sbt(es, "pin_t12", [128, 2, 2, 512], F32)
                t12r = [Region(), Region()]
                rowi = 0
                qi = 0
                order = list(range(0, 8)) + list(range(24, 28)) + list(range(16, 24)) + list(range(8, 16)) + list(range(28, 44))
                import os as _os
                if _os.environ.get("PIN_ORDER"):
                    order = [int(v) for v in _os.environ["PIN_ORDER"].split(",")]
                blocks = [(order[i], order[i + 1]) for i in range(0, len(order), 2)]
                wsP = WStream(WB, [w_in[l, :, bb[0] * 128:(bb[0] + 2) * 128] for bb in blocks])
                for bi_, (n0, n1) in enumerate(blocks):
                    assert n1 == n0 + 1 and n0 % 2 == 0
                    wt, wr = wsP.get(bi_)
                    for jn, n in enumerate((n0, n1)):
                        rb = rowi % 2
                        rowi += 1
                        row = rows[rb]
                        row_bf = row[:].bitcast(BF16)
                        for ti, (t0, w) in enumerate(TILES):
                            if (not need_ctx) and t0 < NCTX and (n < 24 and n >= 8 or n >= 28):
                                continue
                            ps, pr = getps()
                            for k in range(KC):
                                S.op("pe", lambda e, k=k, ps=ps, jn=jn, wt=wt, t0=t0, w=w: e.matmul(
                                    ps[:, 0:w], lhsT=wt[:, k, jn * 128:(jn + 1) * 128], rhs=H[:, k, t0:t0 + w],
                                    start=(k == 0), stop=(k == KC - 1)), reads=[wr, R_H], writes=[pr])
                            rr = rrs[rb][ti]
                            if n < 8:
                                S.op("act", lambda e, ps=ps, row=row, t0=t0, w=w: e.activation(
                                    out=row[:, t0:t0 + w], in_=ps[:, 0:w], func=AF.Copy), reads=[pr], writes=[rr])
                            elif n < 16:
                                S.op("act", lambda e, ps=ps, row_bf=row_bf, t0=t0, w=w: e.activation(
                                    out=row_bf[:, t0:t0 + w], in_=ps[:, 0:w], func=AF.Gelu_apprx_tanh),
                                    reads=[pr], writes=[rr])
                            elif n < 26:
                                qb = qi % 2
                                qi += 1
                                S.op("act", lambda e, ps=ps, qb=qb, w=w: e.activation(
                                    out=qbf[:, qb, 0:w], in_=ps[:, 0:w], func=AF.Copy), reads=[pr], writes=[qbr[qb]])
                                ps2, pr2 = getps()
                                S.op("pe", lambda e, ps2=ps2, qb=qb, w=w: e.matmul(
                                    ps2[:, 0:w], lhsT=perm_bf, rhs=qbf[:, qb, 0:w], start=True, stop=True),
                                    reads=[qbr[qb], R_const], writes=[pr2])
                                S.op("dve", lambda e, ps=ps, qb=qb, t0=t0, w=w: e.tensor_tensor(
                                    out=t12[:, qb, 0, 0:w], in0=ps[:, 0:w], in1=rtb[:, 0, t0:t0 + w], op=ALU.mult),
                                    reads=[pr, R_rt], writes=[t12r[qb]])
                                S.op("dve", lambda e, ps2=ps2, qb=qb, t0=t0, w=w: e.tensor_tensor(
                                    out=t12[:, qb, 1, 0:w], in0=ps2[:, 0:w], in1=rtb[:, 1, t0:t0 + w], op=ALU.mult),
                                    reads=[pr2, R_rt, t12r[qb]], writes=[t12r[qb]])
                                if n < 24:
                                    S.op("dve", lambda e, qb=qb, row_bf=row_bf, t0=t0, w=w: e.tensor_tensor(
                                        out=row_bf[:, t0:t0 + w], in0=t12[:, qb, 0, 0:w], in1=t12[:, qb, 1, 0:w],
                                        op=ALU.add), reads=[t12r[qb]], writes=[rr])
                                else:
                                    S.op("dve", lambda e, qb=qb, n=n, t0=t0, w=w: e.tensor_tensor(
                                        out=KT[:, n - 24, t0:t0 + w], in0=t12[:, qb, 0, 0:w], in1=t12[:, qb, 1, 0:w],
                                        op=ALU.add), reads=[t12r[qb]], writes=[R_kv])
                            elif n < 28:
                                S.op("act", lambda e, ps=ps, row_bf=row_bf, t0=t0, w=w: e.activation(
                                    out=row_bf[:, t0:t0 + w], in_=ps[:, 0:w], func=AF.Copy), reads=[pr], writes=[rr])
                                for bi in range(w // 128):
                                    blk = (t0 + bi * 128) // 128
                                    pst, prt = getps()
                                    pst_bf = pst[:].bitcast(BF16)
                                    S.op("pe", lambda e, pst_bf=pst_bf, row_bf=row_bf, t0=t0, bi=bi: e.transpose(
                                        pst_bf[:, 0:128], row_bf[:, t0 + bi * 128:t0 + (bi + 1) * 128], ident_bf),
                                        reads=[rr, R_const], writes=[prt])
                                    S.op("dve", lambda e, pst_bf=pst_bf, blk=blk, n=n: e.tensor_copy(
                                        out=VT[:, blk, n - 26, :], in_=pst_bf[:, 0:128]), reads=[prt], writes=[R_kv])
                            else:
                                S.op("act", lambda e, ps=ps, row_bf=row_bf, t0=t0, w=w: e.activation(
                                    out=row_bf[:, t0:t0 + w], in_=ps[:, 0:w], func=AF.Sigmoid), reads=[pr], writes=[rr])
                        if n < 8:
                            S.dma("sp", XRd[n], row[:], reads=rrs[rb], writes=[R_xrd])
                        elif n < 16:
                            S.dma("sp", Gd[n - 8], row_bf[:, 0:NT], reads=rrs[rb])
                        elif n < 24:
                            S.dma("sp", Qd[n - 16], row_bf[:, 0:NT], reads=rrs[rb])
                        elif n >= 28:
                            S.dma("sp", GABd[n - 28], row_bf[:, 0:NT], reads=rrs[rb])
                    if bi_ == 5 and len(blocks) > 6:
                        e1a()
                if len(blocks) <= 6:
                    e1a()
                e1b()
                S.barrier()
            Hes.close()
            checkpoint(f"pin{l}")


            def rnn_sweep(d):
                with ExitStack() as es:
                    sets = []
                    for si in range(2):
                        st = dict(
                            P=sbt(es, "rn_P", [128, PW], F32), XL=sbt(es, "rn_XL", [128, NT], F32),
                            XLb=sbt(es, "rn_XLb", [128, NT], BF16), A=sbt(es, "rn_A", [128, NT], F32),
                            A2=sbt(es, "rn_A2", [128, NT], F32), TI=sbt(es, "rn_TI", [128, NT], F32),
                            HF=sbt(es, "rn_HF", [128, NT], F32), Gs=sbt(es, "rn_G", [128, NT], BF16),
                            Ub=sbt(es, "rn_U", [128, NT], BF16))
                        for nm in ("P", "XL", "XLb", "A", "A2", "TI", "HF", "Gs", "Ub"):
                            st["R_" + nm] = Region()
                        sets.append(st)
                    trt = sbt(es, "rn_tr", [128, 2, 512], F32)
                    trr = [Region(), Region()]
                    wst = sbt(es, "rn_wst", [128, 2, KC, 128], F32)
                    wbf = sbt(es, "rn_wbf", [128, 2, KC, 128], BF16)
                    R_w = Region()
                    S.dma("sp", wst[:, 0], lru_wa[l, d].rearrange("b i j -> i b j"), writes=[R_w])
                    S.dma("sp", wst[:, 1], lru_wx[l, d].rearrange("b i j -> i b j"), writes=[R_w])
                    S.op("pool", lambda e: e.tensor_copy(out=wbf[:], in_=wst[:]), reads=[R_w], writes=[R_w])
                    if d == 0:
                        for st in sets:
                            S.op("pool", lambda e, st=st: e.memset(st["P"][:], 0.0), writes=[st["R_P"]])
                    tstate = dict(i=0)

                    def stage1(c):
                        st = sets[c % 2]
                        P, XL, XLb, HF, Gs = st["P"], st["XL"], st["XLb"], st["HF"], st["Gs"]
                        if d == 0:
                            S.dma("sp", P[:, 2:2 + NCTX], XRd[c, :, 0:NCTX], writes=[st["R_P"]])
                            S.dma("sp", P[:, LP0 + 2:LP0 + 2 + NLAT], XRd[c, :, NCTX:NT], writes=[st["R_P"]])
                            S.op("dve", lambda e: e.tensor_copy(out=P[:, LP0 + 2 + NLAT:LP0 + 3 + NLAT],
                                                                in_=HAL[:, c, 1:2]), reads=[R_halo, st["R_P"]], writes=[st["R_P"]])
                            S.op("dve", lambda e: e.tensor_copy(out=P[:, LP0 + 3 + NLAT:LP0 + 4 + NLAT],
                                                                in_=HAL[:, c, 0:1]), reads=[R_halo, st["R_P"]], writes=[st["R_P"]])
                            for (o0, n_, p0) in ((0, NCTX, 0), (NCTX, NLAT, LP0)):
                                S.op("dve", lambda e, o0=o0, n_=n_, p0=p0: e.tensor_scalar(
                                    out=XL[:, o0:o0 + n_], in0=P[:, p0:p0 + n_], scalar1=CW[:, l, 0, c:c + 1],
                                    scalar2=CB[:, l, c:c + 1], op0=ALU.mult, op1=ALU.add),
                                    reads=[st["R_P"], R_sc], writes=[st["R_XL"]])
                                for kk in range(1, 5):
                                    S.op("dve", lambda e, o0=o0, n_=n_, p0=p0, kk=kk: e.scalar_tensor_tensor(
                                        out=XL[:, o0:o0 + n_], in0=P[:, p0 + kk:p0 + kk + n_], scalar=CW[:, l, kk, c:c + 1],
                                        in1=XL[:, o0:o0 + n_], op0=ALU.mult, op1=ALU.add),
                                        reads=[st["R_P"], R_sc, st["R_XL"]], writes=[st["R_XL"]])
                            S.dma("sp", XLd[c], XL[:], reads=[st["R_XL"]])
                        else:
                            S.dma("sp", XL[:], XLd[c], writes=[st["R_XL"]])
                            S.dma("sp", HF[:], HFd[c], writes=[st["R_HF"]])
                            S.dma("sp", Gs[:], Gd[c], writes=[st["R_Gs"]])
                        S.op("pool", lambda e: e.tensor_copy(out=XLb[:], in_=XL[:]), reads=[st["R_XL"]], writes=[st["R_XLb"]])

                    def stage2(c):
                        st = sets[c % 2]
                        XLb, A, A2, TI = st["XLb"], st["A"], st["A2"], st["TI"]
                        for (t0, w) in TILES:
                            psr_, prr_ = getps()
                            psi_, pri_ = getps()
                            S.op("pe", lambda e, ps=psr_, t0=t0, w=w: e.matmul(
                                ps[:, 0:w], lhsT=wbf[:, 0, c, :], rhs=XLb[:, t0:t0 + w], start=True, stop=True),
                                reads=[R_w, st["R_XLb"]], writes=[prr_])
                            S.op("pe", lambda e, ps=psi_, t0=t0, w=w: e.matmul(
                                ps[:, 0:w], lhsT=wbf[:, 1, c, :], rhs=XLb[:, t0:t0 + w], start=True, stop=True),
                                reads=[R_w, st["R_XLb"]], writes=[pri_])
                            tb = tstate["i"] % 2
                            tstate["i"] += 1
                            S.op("act", lambda e, ps=psr_, tb=tb, w=w: e.activation(
                                out=trt[:, tb, 0:w], in_=ps[:, 0:w], func=AF.Tanh, scale=0.5,
                                bias=LD[:, l, 0, d, c:c + 1]), reads=[prr_, R_sc], writes=[trr[tb]])
                            S.op("act", lambda e, ps=psi_, t0=t0, w=w: e.activation(
                                out=TI[:, t0:t0 + w], in_=ps[:, 0:w], func=AF.Tanh, scale=0.5,
                                bias=LD[:, l, 1, d, c:c + 1]), reads=[pri_, R_sc], writes=[st["R_TI"]])
                            S.op("act", lambda e, tb=tb, t0=t0, w=w: e.activation(
                                out=A[:, t0:t0 + w], in_=trt[:, tb, 0:w], func=AF.Exp,
                                scale=LD[:, l, 3, d, c:c + 1], bias=LD[:, l, 3, d, c:c + 1]),
                                reads=[trr[tb], R_sc], writes=[st["R_A"]])
                            S.op("act", lambda e, tb=tb, t0=t0, w=w: e.activation(
                                out=A2[:, t0:t0 + w], in_=trt[:, tb, 0:w], func=AF.Exp,
                                scale=LD[:, l, 2, d, c:c + 1], bias=LD[:, l, 2, d, c:c + 1]),
                                reads=[trr[tb], R_sc], writes=[st["R_A2"]])
                        S.op("act", lambda e: e.activation(out=A2[:], in_=A2[:], func=AF.Sqrt, scale=-1.0, bias=1.0),
                             reads=[st["R_A2"]], writes=[st["R_A2"]])

                    def stage3(c):
                        st = sets[c % 2]
                        P, XL, A, A2, TI, HF, Gs, Ub = (st[k] for k in ("P", "XL", "A", "A2", "TI", "HF", "Gs", "Ub"))
                        S.op("dve", lambda e: e.scalar_tensor_tensor(out=TI[:], in0=TI[:], scalar=1.0, in1=XL[:],
                                                                     op0=ALU.add, op1=ALU.mult),
                             reads=[st["R_TI"], st["R_XL"]], writes=[st["R_TI"]])
                        S.op("dve", lambda e: e.scalar_tensor_tensor(out=TI[:], in0=TI[:], scalar=0.5, in1=A2[:],
                                                                     op0=ALU.mult, op1=ALU.mult),
                             reads=[st["R_TI"], st["R_A2"]], writes=[st["R_TI"]])
                        if d == 0:
                            S.op("dve", lambda e: e.tensor_tensor_scan(out=HF[:], data0=A[:], data1=TI[:], initial=0.0,
                                                                       op0=ALU.mult, op1=ALU.add),
                                 reads=[st["R_A"], st["R_TI"]], writes=[st["R_HF"]])
                            S.op("dve", lambda e: e.tensor_copy(out=FIN[:, c:c + 1], in_=HF[:, NT - 1:NT]),
                                 reads=[st["R_HF"]], writes=[R_fin])
                            S.dma("sp", HFd[c], HF[:], reads=[st["R_HF"]])
                        else:
                            S.op("dve", lambda e: e.tensor_tensor_scan(
                                out=P[:, NCTX - 1::-1], data0=A[:, NCTX - 1::-1], data1=TI[:, NCTX - 1::-1],
                                initial=0.0, op0=ALU.mult, op1=ALU.add),
                                reads=[st["R_A"], st["R_TI"], st["R_P"]], writes=[st["R_P"]])
                            S.op("dve", lambda e: e.tensor_tensor_scan(
                                out=P[:, NT - 1:NCTX - 1:-1], data0=A[:, NT - 1:NCTX - 1:-1], data1=TI[:, NT - 1:NCTX - 1:-1],
                                initial=HIN[:, c:c + 1], op0=ALU.mult, op1=ALU.add),
                                reads=[st["R_A"], st["R_TI"], st["R_P"], R_hin], writes=[st["R_P"]])
                            S.op("dve", lambda e: e.tensor_tensor(out=HF[:], in0=HF[:], in1=P[:, 0:NT], op=ALU.add),
                                 reads=[st["R_HF"], st["R_P"]], writes=[st["R_HF"]])
                            S.op("dve", lambda e: e.tensor_tensor(out=Ub[:], in0=HF[:], in1=Gs[:], op=ALU.mult),
                                 reads=[st["R_HF"], st["R_Gs"]], writes=[st["R_Ub"]])
                            S.dma("sp", Ud[c], Ub[:], reads=[st["R_Ub"]])

                    stage1(0)
                    for c in range(KC):
                        if c + 1 < KC:
                            stage1(c + 1)
                        stage2(c)
                        stage3(c)
                    S.barrier()

            rnn_sweep(0)
            R_e2, R_g2, R_d2 = Region(), Region(), Region()
            S.dma("sp", EXs[l], FIN[:], reads=[R_fin], writes=[R_d2])
            S.collective(EXs[l], GXs[l], PAIRS, reads=[R_d2], writes=[R_g2])
            checkpoint(f"rnnA{l}")
            with ExitStack() as es:
                Qt = [sbt(es, f"at_q{i}", [128, 4, 512], BF16) for i in range(2)]
                Qr = [Region(), Region()]
                Pj = [sbt(es, f"at_p{i}", [128, 512], BF16) for i in range(6)]
                Pr = [Region() for _ in range(6)]
                Ot = [sbt(es, f"at_o{i}", [128, 4, 512], BF16) for i in range(2)]
                Or = [Region(), Region()]
                dn = sbt(es, "at_dn", [128, 2, 512], F32)
                dnr = [Region(), Region()]
                pji = 0
                qti = 0
                bi_ = 0
                qtiles = (TILES if need_ctx else TILES[1:])
                abank = dict(s=0, o=0)
                qlist = [(kv, t0, w) for kv in range(2) for (t0, w) in qtiles]

                def q_load(i):
                    kv_, t0_, w_ = qlist[i]
                    S.dma("sp", Qt[i % 2][:, :, 0:w_],
                          Qd[kv_ * 4:(kv_ + 1) * 4, :, t0_:t0_ + w_].rearrange("g p t -> p g t"), writes=[Qr[i % 2]])
                q_load(0)
                for qi_, (kv, t0, w) in enumerate(qlist):
                    if True:
                        qb = qi_ % 2
                        nh = w // 128
                        if qi_ + 1 < len(qlist):
                            q_load(qi_ + 1)
                        for qq in range(nh):
                            q0 = qq * 128
                            tq = t0 + q0
                            if tq < NCTX:
                                keys = [(0, None), (128, None)]
                            else:
                                lb = (tq - NCTX) // 128
                                keys = []
                                if lb > 0:
                                    keys.append((tq - 128, 0))
                                keys.append((tq, None))
                                if lb < NLAT // 128 - 1:
                                    keys.append((tq + 128, 1))
                                else:
                                    keys.append((-1, 2))
                                keys += [(0, None), (128, None)]
                            pjs = []
                            for (k0, mk) in keys:
                                sbank = abank["s"] % 4
                                abank["s"] += 1
                                ps, pr = psb[sbank], psr[sbank]
                                kap = KH[:, kv, :] if k0 < 0 else KT[:, kv, k0:k0 + 128]
                                S.op("pe", lambda e, ps=ps, kap=kap, qb=qb, q0=q0, mk=mk: e.matmul(
                                    ps[:, 0:512], lhsT=kap, rhs=Qt[qb][:, :, q0:q0 + 128],
                                    start=True, stop=(mk is None)), reads=[R_kv, R_halo, Qr[qb]], writes=[pr])
                                if mk is not None:
                                    S.op("pe", lambda e, ps=ps, mk=mk: e.matmul(
                                        ps[:, 0:512], lhsT=ident_bf, rhs=mask_bf[:, mk, :], start=False, stop=True),
                                        reads=[R_const], writes=[pr])
                                pb = pji % 6
                                pji += 1
                                S.op("act", lambda e, ps=ps, pb=pb: e.activation(
                                    out=Pj[pb][:], in_=ps[:, 0:512], func=AF.Exp, scale=float(128 ** -0.5)),
                                    reads=[pr], writes=[Pr[pb]])
                                pjs.append((pb, k0))
                            ob = abank["o"] % 2
                            abank["o"] += 1
                            pso, pro = psb[4 + ob], psr[4 + ob]
                            psd, prd = psb[6 + ob], psr[6 + ob]
                            for ji, (pb, k0) in enumerate(pjs):
                                vap = VH[:, kv, :] if k0 < 0 else VT[:, k0 // 128, kv, :]
                                S.op("pe", lambda e, pso=pso, pb=pb, vap=vap, ji=ji: e.matmul(
                                    pso[:, 0:512], lhsT=vap, rhs=Pj[pb][:],
                                    start=(ji == 0), stop=(ji == len(pjs) - 1)), reads=[R_kv, R_halo, Pr[pb]], writes=[pro])
                            for ji, (pb, k0) in enumerate(pjs):
                                S.op("pe", lambda e, psd=psd, pb=pb, ji=ji: e.matmul(
                                    psd[:, 0:512], lhsT=ones_bf[:], rhs=Pj[pb][:],
                                    start=(ji == 0), stop=(ji == len(pjs) - 1)), reads=[R_const, Pr[pb]], writes=[prd])
                            db = bi_ % 2
                            bi_ += 1
                            for g in range(4):
                                S.op("dve", lambda e, psd=psd, db=db, g=g: e.tensor_scalar(
                                    out=dn[:, db, g * 128:(g + 1) * 128], in0=psd[:, g * 128:(g + 1) * 128],
                                    scalar1=ESK[:, l, kv * 4 + g:kv * 4 + g + 1], scalar2=None, op0=ALU.add),
                                    reads=[prd, R_sc], writes=[dnr[db]])
                            S.op("dve", lambda e, db=db: e.reciprocal(out=dn[:, db, :], in_=dn[:, db, :]),
                                 reads=[dnr[db]], writes=[dnr[db]])
                            S.op("dve", lambda e, pso=pso, db=db, qb=qb, q0=q0: e.tensor_tensor(
                                out=Ot[qb][:, :, q0:q0 + 128], in0=pso[:, 0:512].rearrange("p (g m) -> p g m", g=4),
                                in1=dn[:, db, :].rearrange("p (g m) -> p g m", g=4), op=ALU.mult),
                                reads=[pro, dnr[db]], writes=[Or[qb]])
                        S.dma("sp", Od[kv * 4:(kv + 1) * 4, :, t0:t0 + w].rearrange("g p t -> p g t"), Ot[qb][:, :, 0:w],
                              reads=[Or[qb]])
                S.barrier()

            checkpoint(f"attn{l}")
            S.dma("sp", e_gs[:], GXs[l].rearrange("(r p) n -> p r n", r=2), reads=[R_g2], writes=[R_e2])
            S.op("dve", lambda e: e.tensor_scalar(out=e_ts[:], in0=e_gs[:, 0, :], scalar1=FL[:, 0:1], scalar2=None,
                                                  op0=ALU.mult), reads=[R_e2, R_const], writes=[R_e2])
            S.op("dve", lambda e: e.scalar_tensor_tensor(out=HIN[:], in0=e_gs[:, 1, :], scalar=FL[:, 1:2], in1=e_ts[:],
                                                         op0=ALU.mult, op1=ALU.add),
                 reads=[R_e2, R_const], writes=[R_hin])
            rnn_sweep(1)
            checkpoint(f"rnn{l}")

            halves = [tiles_out]
            for hv in halves:
                h0 = hv[0][0]
                hw = hv[-1][0] + hv[-1][1] - h0
                with ExitStack() as es:
                    Uh = sbt(es, "mg_U", [128, KC, 2304], BF16)
                    Oh = sbt(es, "mg_O", [128, KC, 2304], BF16)
                    Zh = sbt(es, "mg_Z", [128, KC, 2304], BF16)
                    R_U2, R_O2, R_Z = Region(), Region(), Region()
                    WBa = weight_block(es, "mga", KC, 128)
                    WBb = weight_block(es, "mgb", KC, 128)
                    gab = sbt(es, "mg_gab", [128, 2, 2, 512], BF16)
                    gabr = [Region(), Region()]
                    zt = sbt(es, "mg_zt", [128, 2, 512], F32)
                    ztr = [Region(), Region()]
                    mrow = [sbt(es, f"mg_row{i}", [128, 2304], F32) for i in range(2)]
                    mrr = [[Region() for _ in hv] for _ in range(2)]
                    R_Ut = {t0: Region() for (t0, w) in hv}
                    R_Ot = {t0: Region() for (t0, w) in hv}

                    def uo_load(ti):
                        t0, w = hv[ti]
                        o = t0 - h0
                        S.dma("sp", Uh[:, :, o:o + w], Ud[:, :, t0:t0 + w].rearrange("k p t -> p k t"), writes=[R_Ut[t0]])
                        S.dma("sp", Oh[:, :, o:o + w], Od[:, :, t0:t0 + w].rearrange("k p t -> p k t"), writes=[R_Ot[t0]])
                    gi = 0
                    wqa = load_weight(WBa, w_o_rnn[l, :, 0:128])
                    wqb = load_weight(WBb, w_o_attn[l, :, 0:128])
                    uo_load(0)
                    for n0 in range(0, KC):
                        wa_t, wa_r = wqa
                        wb_t, wb_r = wqb
                        if n0 + 1 < KC:
                            wqa = load_weight(WBa, w_o_rnn[l, :, (n0 + 1) * 128:(n0 + 2) * 128])
                            wqb = load_weight(WBb, w_o_attn[l, :, (n0 + 1) * 128:(n0 + 2) * 128])
                        else:
                            wqa = load_weight(WBa, w_out[l, :, 0:128])
                        for jn in range(1):
                            n = n0 + jn
                            for ti_, (t0, w) in enumerate(hv):
                                o = t0 - h0
                                if n0 == 0 and ti_ + 1 < len(hv):
                                    uo_load(ti_ + 1)
                                gb_ = gi % 2
                                gi += 1
                                S.dma("sp", gab[:, gb_, 0, 0:w], GABd[n, :, t0:t0 + w], writes=[gabr[gb_]])
                                S.dma("sp", gab[:, gb_, 1, 0:w], GABd[8 + n, :, t0:t0 + w], writes=[gabr[gb_]])
                                psa, pra = getps()
                                psb_, prb = getps()
                                for k in range(KC):
                                    S.op("pe", lambda e, k=k, psa=psa, jn=jn, wa_t=wa_t, o=o, w=w: e.matmul(
                                        psa[:, 0:w], lhsT=wa_t[:, k, jn * 128:(jn + 1) * 128], rhs=Uh[:, k, o:o + w],
                                        start=(k == 0), stop=(k == KC - 1)), reads=[wa_r, R_Ut[t0]], writes=[pra])
                                for k in range(KC):
                                    S.op("pe", lambda e, k=k, psb_=psb_, jn=jn, wb_t=wb_t, o=o, w=w: e.matmul(
                                        psb_[:, 0:w], lhsT=wb_t[:, k, jn * 128:(jn + 1) * 128], rhs=Oh[:, k, o:o + w],
                                        start=(k == 0), stop=(k == KC - 1)), reads=[wb_r, R_Ot[t0]], writes=[prb])
                                S.op("dve", lambda e, psa=psa, gb_=gb_, w=w: e.tensor_tensor(
                                    out=zt[:, gb_, 0:w], in0=psa[:, 0:w], in1=gab[:, gb_, 0, 0:w], op=ALU.mult),
                                    reads=[pra, gabr[gb_]], writes=[ztr[gb_]])
                                S.op("dve", lambda e, psb_=psb_, gb_=gb_, w=w: e.tensor_tensor(
                                    out=gab[:, gb_, 1, 0:w], in0=psb_[:, 0:w], in1=gab[:, gb_, 1, 0:w], op=ALU.mult),
                                    reads=[prb, gabr[gb_]], writes=[gabr[gb_]])
                                S.op("pool", lambda e, gb_=gb_, n=n, o=o, w=w: e.tensor_tensor(
                                    out=Zh[:, n, o:o + w], in0=zt[:, gb_, 0:w], in1=gab[:, gb_, 1, 0:w], op=ALU.add),
                                    reads=[ztr[gb_], gabr[gb_]], writes=[R_Z])
                    ri = 0
                    for n0 in range(0, KC):
                        wo_t, wo_r = wqa
                        if n0 + 1 < KC:
                            wqa = load_weight(WBa, w_out[l, :, (n0 + 1) * 128:(n0 + 2) * 128])
                        for jn in range(1):
                            n = n0 + jn
                            rb = ri % 2
                            ri += 1
                            for ti, (t0, w) in enumerate(hv):
                                o = t0 - h0
                                ps, pr = getps()
                                for k in range(KC):
                                    S.op("pe", lambda e, k=k, ps=ps, jn=jn, o=o, w=w: e.matmul(
                                        ps[:, 0:w], lhsT=wo_t[:, k, jn * 128:(jn + 1) * 128], rhs=Zh[:, k, o:o + w],
                                        start=(k == 0), stop=(k == KC - 1)), reads=[wo_r, R_Z], writes=[pr])
                                S.op("act", lambda e, ps=ps, rb=rb, o=o, w=w: e.activation(
                                    out=mrow[rb][:, o:o + w], in_=ps[:, 0:w], func=AF.Copy), reads=[pr], writes=[mrr[rb][ti]])
                            S.dma("sp", Md[n, :, h0:h0 + hw], mrow[rb][:, 0:hw], reads=mrr[rb])
                    S.barrier()

            checkpoint(f"merge{l}")
            with ExitStack() as es2:
                H2 = sbt(es2, "H2", [128, KC, NT], BF16)
                R_H2 = Region()
                post_norm_pass(l, 0, Md, Xcur, XA_l, tiles_out, Hbuf=H2, R_H=R_H2, lnext=l)
                checkpoint(f"post1{l}")
                with ExitStack() as es:
                    WBg = weight_block(es, "f1g", KC, 128, nbuf=3)
                    WBu = weight_block(es, "f1u", KC, 128, nbuf=3)
                    sg = sbt(es, "f1_sg", [128, 2, 512], F32)
                    sgr = [Region(), Region()]
                    arow = [sbt(es, f"f1_row{i}", [128, NT], BF16) for i in range(2)]
                    arr = [[Region() for _ in TILES] for _ in range(2)]
                    si = 0
                    wsG = WStream(WBg, [w_ffn_in[l, :, j * 128:(j + 1) * 128] for j in range(FC)])
                    wsU = WStream(WBu, [w_ffn_in[l, :, DFF + j * 128:DFF + (j + 1) * 128] for j in range(FC)])
                    for j in range(FC):
                        wg_t, wg_r = wsG.get(j)
                        wu_t, wu_r = wsU.get(j)
                        rb = j % 2
                        for ti, (t0, w) in enumerate(TILES):
                            if (t0, w) not in tiles_out:
                                continue
                            psg, prg = getps()
                            psu, pru = getps()
                            for k in range(KC):
                                S.op("pe", lambda e, k=k, psg=psg, t0=t0, w=w: e.matmul(
                                    psg[:, 0:w], lhsT=wg_t[:, k, :], rhs=H2[:, k, t0:t0 + w],
                                    start=(k == 0), stop=(k == KC - 1)), reads=[wg_r, R_H2], writes=[prg])
                            for k in range(KC):
                                S.op("pe", lambda e, k=k, psu=psu, t0=t0, w=w: e.matmul(
                                    psu[:, 0:w], lhsT=wu_t[:, k, :], rhs=H2[:, k, t0:t0 + w],
                                    start=(k == 0), stop=(k == KC - 1)), reads=[wu_r, R_H2], writes=[pru])
                            sb_ = si % 2
                            si += 1
                            S.op("act", lambda e, psg=psg, sb_=sb_, w=w: e.activation(
                                out=sg[:, sb_, 0:w], in_=psg[:, 0:w], func=AF.Silu), reads=[prg], writes=[sgr[sb_]])
                            S.op("dve", lambda e, psu=psu, sb_=sb_, rb=rb, t0=t0, w=w: e.tensor_tensor(
                                out=arow[rb][:, t0:t0 + w], in0=psu[:, 0:w], in1=sg[:, sb_, 0:w], op=ALU.mult),
                                reads=[pru, sgr[sb_]], writes=[arr[rb][ti]])
                        tl0 = tiles_out[0][0]
                        S.dma("sp", ACTd[j, :, tl0:NT], arow[rb][:, tl0:NT], reads=arr[rb])
                    S.barrier()
            checkpoint(f"f1{l}")
            segs = [tiles_out]
            for sg_ in segs:
                s0 = sg_[0][0]
                sw = sg_[-1][0] + sg_[-1][1] - s0
                with ExitStack() as es:
                    As = sbt(es, "f2_A", [128, FC, NT], BF16)
                    R_As = Region()
                    WB2 = weight_block(es, "f2w", FC, 128)
                    yrow = [sbt(es, f"f2_row{i}", [128, NT], F32) for i in range(2)]
                    yrr = [[Region() for _ in sg_] for _ in range(2)]
                    R_Ast = [Region() for _ in sg_]

                    def as_load(ti):
                        t0, w = sg_[ti]
                        o = t0 - s0
                        for k0 in range(0, FC, 8):
                            k1 = min(FC, k0 + 8)
                            S.dma("sp", As[:, k0:k1, o:o + w], ACTd[k0:k1, :, t0:t0 + w].rearrange("k p t -> p k t"),
                                  writes=[R_Ast[ti]])
                    wq2 = load_weight(WB2, w_ffn_out[l, :, 0:128])
                    as_load(0)
                    for n in range(KC):
                        w2_t, w2_r = wq2
                        if n + 1 < KC:
                            wq2 = load_weight(WB2, w_ffn_out[l, :, (n + 1) * 128:(n + 2) * 128])
                        rb = n % 2
                        for ti, (t0, w) in enumerate(sg_):
                            o = t0 - s0
                            if n == 0 and ti + 1 < len(sg_):
                                as_load(ti + 1)
                            ps, pr = getps()
                            for k in range(FC):
                                S.op("pe", lambda e, k=k, ps=ps, o=o, w=w: e.matmul(
                                    ps[:, 0:w], lhsT=w2_t[:, k, :], rhs=As[:, k, o:o + w],
                                    start=(k == 0), stop=(k == FC - 1)), reads=[w2_r, R_Ast[ti]], writes=[pr])
                            S.op("act", lambda e, ps=ps, rb=rb, o=o, w=w: e.activation(
                                out=yrow[rb][:, o:o + w], in_=ps[:, 0:w], func=AF.Copy), reads=[pr], writes=[yrr[rb][ti]])
                        S.dma("sp", Yd[n, :, s0:s0 + sw], yrow[rb][:, 0:sw], reads=yrr[rb])
                    S.barrier()
            checkpoint(f"f2{l}")
            if l < L - 1:
                Hes = ExitStack()
                H = sbt(Hes, "H", [128, KC, NT], BF16)
                R_H = Region()
                post_norm_pass(l, 1, Yd, XA_l, XB_l, tiles_out, Hbuf=H, R_H=R_H, lnext=l + 1)
                Xcur = XB_l
            else:
                post_norm_pass(l, 1, Yd, XA_l, out, tiles_out, final=True)
        S.barrier()
    return nc


def _host_tables(half):
    inv = (10000.0 ** (-np.arange(32, dtype=np.float32) / 32.0)).astype(np.float32)
    t = np.arange(NLAT)
    pos = t if half == 0 else (NLAT_FULL - 1 - t)
    row = (pos // 64).astype(np.float32)
    col = (pos % 64).astype(np.float32)
    C = np.ones((128, NT), np.float32)
    Sg = np.zeros((128, NT), np.float32)
    perm = np.zeros((128, 128), np.float32)
    for p in range(128):
        hf = p // 64
        pp = p % 64
        fi = pp % 32
        ps_ = row if hf == 0 else col
        ang = (ps_ * inv[fi]).astype(np.float32)
        C[p, NCTX:] = np.cos(ang)
        if pp < 32:
            partner, sign = p + 32, -1.0
        else:
            partner, sign = p - 32, 1.0
        Sg[p, NCTX:] = sign * np.sin(ang)
        perm[partner, p] = 1.0
    ident = np.eye(128, dtype=np.float32)
    consts = np.stack([ident, perm, np.zeros_like(ident)], axis=1)
    i = np.arange(128)[:, None]
    m = np.arange(128)[None, :]
    lo = np.where(m <= i, 0.0, NEG).astype(np.float32)
    hi = np.where(i <= m, 0.0, NEG).astype(np.float32)
    halo = np.where((127 - i) <= m, 0.0, NEG).astype(np.float32)
    mask = np.stack([np.tile(lo, (1, 4)), np.tile(hi, (1, 4)), np.tile(halo, (1, 4))], axis=1)
    return C, Sg, np.ascontiguousarray(consts), np.ascontiguousarray(mask)


def _pc(v, nchunk):
    v = np.asarray(v, np.float32)
    lead = v.shape[:-1]
    v = v.reshape(lead + (nchunk, 128))
    return np.ascontiguousarray(np.moveaxis(v, -1, 0))


def make_in_maps(x, c, ctx, c_ctx, w_mod, b_mod, g_mix_pre, g_mix_post, g_ffn_pre, g_ffn_post, w_in, conv_w,
                 conv_b, lru_wa, lru_ba, lru_wx, lru_bx, lru_lam, attn_sink, w_o_rnn, w_o_attn, w_out,
                 w_ffn_in, w_ffn_out, n_cores=8):
    f = lambda a: np.ascontiguousarray(np.asarray(a, np.float32))
    bmod = np.stack([_pc(b_mod[l], 48) for l in range(L)])
    gvec = np.stack([np.stack([_pc(g[l], KC) for g in (g_mix_pre, g_mix_post, g_ffn_pre, g_ffn_post)], axis=1)
                     for l in range(L)])
    convb = np.stack([_pc(conv_b[l], KC) for l in range(L)])
    sinkbc = np.ascontiguousarray(np.broadcast_to(np.asarray(attn_sink, np.float32)[:, None, :], (L, 128, KC)))
    cw = np.asarray(conv_w, np.float32)
    zero = np.zeros_like(cw[:, :1])
    shared = dict(w_mod=f(w_mod), bmod=f(bmod), gvec=f(gvec), w_in=f(w_in), convb=f(convb),
                  sinkbc=f(sinkbc), w_o_rnn=f(w_o_rnn), w_o_attn=f(w_o_attn), w_out=f(w_out),
                  w_ffn_in=f(w_ffn_in), w_ffn_out=f(w_ffn_out))
    per_half = []
    for half in range(2):
        C, Sg, consts, mask = _host_tables(half)
        if half == 0:
            taps = np.concatenate([cw, zero], axis=1)
            dsel = [0, 1]
        else:
            taps = np.concatenate([zero, cw[:, ::-1]], axis=1)
            dsel = [1, 0]
        convw = np.stack([_pc(taps[l], KC) for l in range(L)])
        lruv = np.stack([np.stack([_pc(np.asarray(a, np.float32)[l][dsel], KC) for a in (lru_ba, lru_bx, lru_lam)],
                                  axis=1) for l in range(L)])
        wa = f(np.asarray(lru_wa, np.float32)[:, dsel])
        wx = f(np.asarray(lru_wx, np.float32)[:, dsel])
        flags = np.zeros((128, 2), np.float32)
        flags[:, 1 - half] = 1.0
        per_half.append(dict(ropeC=C, ropeS=Sg, consts=consts, maskin=mask, convw=f(convw), lruv=f(lruv),
                             lru_wa=wa, lru_wx=wx, flags=flags))
    maps = []
    for core in range(n_cores):
        b = core % 4
        half = core // 4
        xl = np.asarray(x[b], np.float32)[half * NLAT:(half + 1) * NLAT]
        cx = np.asarray(ctx[b], np.float32)
        if half == 1:
            xl = xl[::-1]
            cx = cx[::-1]
        xcat = np.concatenate([cx, xl], axis=0)
        xin = np.ascontiguousarray(xcat.T).reshape(KC, 128, NT)
        cv = np.stack([np.asarray(c[b], np.float32), np.asarray(c_ctx, np.float32)], axis=-1)
        cvec = np.ascontiguousarray(cv.reshape(KC, 128, 2).transpose(1, 0, 2))
        m = dict(shared)
        m.update(per_half[half])
        m["xin"] = xin
        m["cvec"] = cvec
        maps.append(m)
    return maps


_NC_CACHE = {}


def kernel(**inputs):
    if "nc" not in _NC_CACHE:
        _NC_CACHE["nc"] = build(debug=False)
    nc = _NC_CACHE["nc"]
    maps = make_in_maps(**inputs)
    res = run_bass_kernel_spmd(nc, maps, core_ids=list(range(8)))
    out = np.empty((4, NLAT_FULL, D), np.float32)
    for core in range(8):
        b = core % 4
        half = core // 4
        o = np.asarray(res.results[core]["out"], np.float32).reshape(D, NLAT).T
        if half == 1:
            o = o[::-1]
        out[b, half * NLAT:(half + 1) * NLAT] = o
    return out
```
